# Optimizing a Trainium2 kernel written in Bass

```python
import jax, jax.numpy as jnp
from jax import lax
import numpy as np

D_MODEL = 2048
BATCH = 4
SEQ = 4096
DEPTH = 4

D_CONV = 512
D_HGRN = 768
D_RET = 768
D_MIX = D_CONV + D_HGRN + D_RET
CONV_WIDTH = 31
HGRN_HEAD_DIM = 128
HGRN_HEADS = D_HGRN // HGRN_HEAD_DIM
RET_HEAD_DIM = 128
RET_HEADS = D_RET // RET_HEAD_DIM
CHUNK = 64
D_FF = 4 * D_MODEL
ROPE_BASE = 10000.0
LN_EPS = 1e-5
DEEPNORM_ALPHA = (2.0 * DEPTH) ** 0.25
DEEPNORM_BETA = (8.0 * DEPTH) ** -0.25
SPLITS = (D_CONV, D_CONV,
          D_HGRN, D_HGRN, D_HGRN, D_HGRN,
          D_RET, D_RET, D_RET, D_RET)
D_IN = sum(SPLITS)

kernel_name = 'hybrid_conv_hgrn2_retention_block'


def layer_norm(x, g, b):
    xf = x.astype(jnp.float32)
    mu = jnp.mean(xf, axis=-1, keepdims=True)
    var = jnp.mean(jnp.square(xf - mu), axis=-1, keepdims=True)
    return (xf - mu) * lax.rsqrt(var + LN_EPS) * g + b


def rope(x, positions):
    half = x.shape[-1] // 2
    inv = ROPE_BASE ** (-jnp.arange(half, dtype=jnp.float32) / half)
    ang = positions.astype(jnp.float32)[..., None] * inv
    cos = jnp.cos(ang)[:, :, None, :]
    sin = jnp.sin(ang)[:, :, None, :]
    x1, x2 = x[..., :half], x[..., half:]
    return jnp.concatenate([x1 * cos - x2 * sin, x1 * sin + x2 * cos], axis=-1)


def to_chunks(x):
    b, t, h, d = x.shape
    return x.reshape(b, t // CHUNK, CHUNK, h, d).transpose(1, 0, 3, 2, 4)


def from_chunks(x):
    n, b, h, c, d = x.shape
    return x.transpose(1, 0, 3, 2, 4).reshape(b, n * c, h, d)


def conv_mixer(a_in, gate_in, w_dw, b_dw, ln_g, ln_b):
    u = a_in.astype(jnp.float32) * jax.nn.sigmoid(gate_in.astype(jnp.float32))
    y = lax.conv_general_dilated(u, w_dw.astype(jnp.float32)[:, None, :], window_strides=(1,),
                                 padding=[(CONV_WIDTH - 1, 0)],
                                 dimension_numbers=('NWC', 'WIO', 'NWC'),
                                 feature_group_count=D_CONV) + b_dw
    return jax.nn.silu(layer_norm(y, ln_g, ln_b))


def hgrn2_mixer(q_in, f_in, i_in, g_in, lb, norm_g):
    b, t, _ = q_in.shape
    shp = (b, t, HGRN_HEADS, HGRN_HEAD_DIM)
    q = jax.nn.silu(q_in.astype(jnp.float32)).reshape(shp)
    inp = i_in.astype(jnp.float32).reshape(shp)
    lb = lb.astype(jnp.float32)
    log_sig = jax.nn.log_sigmoid(f_in.astype(jnp.float32))
    pos_lb = lb > 0
    log_lb = jnp.log(jnp.where(pos_lb, lb, 1.0))
    log_f = jnp.where(pos_lb, jnp.logaddexp(log_lb, jnp.log1p(-lb) + log_sig), log_sig)
    k = (-jnp.expm1(log_f)).reshape(shp)
    log_f = log_f.reshape(shp)
    causal = jnp.tril(jnp.ones((CHUNK, CHUNK), dtype=bool))[:, :, None]

    def step(S, xs):
        qc, kc, ic, lfc = xs
        cum = jnp.cumsum(lfc, axis=2)
        rel = cum[:, :, :, None, :] - cum[:, :, None, :, :]
        decay = jnp.where(causal, jnp.exp(jnp.where(causal, rel, 0.0)), 0.0)
        scores = jnp.einsum('bhtd,bhsd,bhtsd->bhts', qc, kc, decay)
        o = (jnp.einsum('bhts,bhse->bhte', scores, ic)
             + jnp.einsum('bhtd,bhde->bhte', qc * jnp.exp(cum), S))
        last = cum[:, :, -1:, :]
        S = (jnp.exp(last[:, :, 0, :])[..., None] * S
             + jnp.einsum('bhsd,bhse->bhde', kc * jnp.exp(last - cum), ic))
        return S, o

    S0 = jnp.zeros((b, HGRN_HEADS, HGRN_HEAD_DIM, HGRN_HEAD_DIM), jnp.float32)
    _, o = lax.scan(step, S0, (to_chunks(q), to_chunks(k), to_chunks(inp), to_chunks(log_f)))
    o = from_chunks(o)
    o = o * lax.rsqrt(jnp.mean(jnp.square(o), axis=-1, keepdims=True) + LN_EPS)
    return o.reshape(b, t, D_HGRN) * norm_g * jax.nn.sigmoid(g_in.astype(jnp.float32))


def retention_mixer(q_in, k_in, v_in, g_in, positions, gn_g, gn_b):
    b, t, _ = q_in.shape
    shp = (b, t, RET_HEADS, RET_HEAD_DIM)
    q = rope(q_in.astype(jnp.float32).reshape(shp), positions)
    k = rope(k_in.astype(jnp.float32).reshape(shp), positions) * RET_HEAD_DIM ** -0.5
    v = v_in.astype(jnp.float32).reshape(shp)
    log_gamma = jnp.log1p(-(2.0 ** (-5.0 - jnp.arange(RET_HEADS, dtype=jnp.float32))))
    idx = jnp.arange(CHUNK, dtype=jnp.float32)
    diff = idx[:, None] - idx[None, :]
    intra = jnp.where(diff >= 0, jnp.exp(log_gamma[:, None, None] * jnp.maximum(diff, 0.0)), 0.0)
    q_decay = jnp.exp(log_gamma[:, None] * (idx + 1.0))
    k_decay = jnp.exp(log_gamma[:, None] * (CHUNK - 1.0 - idx))
    chunk_decay = jnp.exp(log_gamma * CHUNK)

    def step(S, xs):
        qc, kc, vc = xs
        scores = jnp.einsum('bhtd,bhsd->bhts', qc, kc) * intra
        o = (jnp.einsum('bhts,bhse->bhte', scores, vc)
             + jnp.einsum('bhtd,bhde->bhte', qc * q_decay[:, :, None], S))
        S = (chunk_decay[:, None, None] * S
             + jnp.einsum('bhsd,bhse->bhde', kc * k_decay[:, :, None], vc))
        return S, o

    S0 = jnp.zeros((b, RET_HEADS, RET_HEAD_DIM, RET_HEAD_DIM), jnp.float32)
    _, o = lax.scan(step, S0, (to_chunks(q), to_chunks(k), to_chunks(v)))
    o = from_chunks(o)
    mu = jnp.mean(o, axis=-1, keepdims=True)
    var = jnp.mean(jnp.square(o - mu), axis=-1, keepdims=True)
    o = ((o - mu) * lax.rsqrt(var + LN_EPS)).reshape(b, t, D_RET) * gn_g + gn_b
    return jax.nn.silu(g_in.astype(jnp.float32)) * o


def setup_inputs(seed: int = 0) -> dict:
    key = jax.random.key(seed)
    ks = jax.random.split(key, 20)
    f32 = jnp.float32

    def nrm(k, shape, scale):
        return jax.random.normal(k, shape, f32) * scale

    x = nrm(ks[0], (BATCH, SEQ, D_MODEL), 1.0)
    positions = (jnp.arange(SEQ, dtype=jnp.int32)[None, :]
                 + jax.random.randint(ks[1], (BATCH, 1), 0, 1024, dtype=jnp.int32))
    return {
        'x': x,
        'positions': positions,
        'w_in': nrm(ks[2], (DEPTH, D_MODEL, D_IN), D_MODEL ** -0.5),
        'w_dw': nrm(ks[3], (DEPTH, CONV_WIDTH, D_CONV), CONV_WIDTH ** -0.5),
        'b_dw': nrm(ks[4], (DEPTH, D_CONV), 0.02),
        'conv_ln_g': 1.0 + nrm(ks[5], (DEPTH, D_CONV), 0.02),
        'conv_ln_b': nrm(ks[6], (DEPTH, D_CONV), 0.02),
        'hgrn_lb': nrm(ks[7], (DEPTH, D_HGRN), 0.1),
        'hgrn_norm_g': 1.0 + nrm(ks[8], (DEPTH, D_HGRN), 0.02),
        'ret_gn_g': 1.0 + nrm(ks[9], (DEPTH, D_RET), 0.02),
        'ret_gn_b': nrm(ks[10], (DEPTH, D_RET), 0.02),
        'w_out': nrm(ks[11], (DEPTH, D_MIX, D_MODEL), D_MIX ** -0.5 * DEEPNORM_BETA),
        'ln1_g': 1.0 + nrm(ks[12], (DEPTH, D_MODEL), 0.02),
        'ln1_b': nrm(ks[13], (DEPTH, D_MODEL), 0.02),
        'w_ff1': nrm(ks[14], (DEPTH, D_MODEL, D_FF), D_MODEL ** -0.5),
        'w_ff2': nrm(ks[15], (DEPTH, D_FF, D_MODEL), D_FF ** -0.5 * DEEPNORM_BETA),
        'ln2_g': 1.0 + nrm(ks[16], (DEPTH, D_MODEL), 0.02),
        'ln2_b': nrm(ks[17], (DEPTH, D_MODEL), 0.02),
    }


def reference(x, positions, w_in, w_dw, b_dw, conv_ln_g, conv_ln_b, hgrn_lb, hgrn_norm_g,
              ret_gn_g, ret_gn_b, w_out, ln1_g, ln1_b, w_ff1, w_ff2, ln2_g, ln2_b):
    dtype = x.dtype
    lbp = jax.nn.softmax(hgrn_lb.astype(jnp.float32), axis=0)
    lower_bounds = jnp.clip(jnp.cumsum(lbp, axis=0) - lbp[0], 0.0, 1.0 - 1e-6)
    split_idx = np.cumsum(SPLITS)[:-1].tolist()
    for l in range(DEPTH):
        proj = jnp.einsum('btd,de->bte', x, w_in[l])
        ca, cg, hq, hf, hi, hg, rq, rk, rv, rg = jnp.split(proj, split_idx, axis=-1)
        y_conv = conv_mixer(ca, cg, w_dw[l], b_dw[l], conv_ln_g[l], conv_ln_b[l])
        y_hgrn = hgrn2_mixer(hq, hf, hi, hg, lower_bounds[l], hgrn_norm_g[l])
        y_ret = retention_mixer(rq, rk, rv, rg, positions, ret_gn_g[l], ret_gn_b[l])
        mix = jnp.concatenate([y_conv, y_hgrn, y_ret], axis=-1).astype(dtype)
        h = jnp.einsum('bte,ed->btd', mix, w_out[l])
        x = layer_norm(DEEPNORM_ALPHA * x + h, ln1_g[l], ln1_b[l]).astype(dtype)
        f = jnp.square(jax.nn.relu(jnp.einsum('btd,df->btf', x, w_ff1[l])))
        h = jnp.einsum('btf,fd->btd', f, w_ff2[l])
        x = layer_norm(DEEPNORM_ALPHA * x + h, ln2_g[l], ln2_b[l]).astype(dtype)
    return x
```

```python
import contextlib
import types
import math
import numpy as np
import ml_dtypes
import concourse.bass as bass
import concourse.mybir as mybir
from concourse.bass_utils import run_bass_kernel_spmd

F32 = mybir.dt.float32
BF16 = mybir.dt.bfloat16
I32 = mybir.dt.int32
U8 = mybir.dt.uint8
AF = mybir.ActivationFunctionType
ALU = mybir.AluOpType

D = 2048
T = 2048
NT = 16
TG = 4
DIN = 7168
DFF = 8192
L_ALL = 4
EPS = 1e-5
ALPHA = (2.0 * L_ALL) ** 0.25
CW = 31
HALO = CW - 1
NH = 6
LOGG = [math.log1p(-(2.0 ** (-5.0 - h))) for h in range(NH)]
C_CA, C_CG, C_HQ, C_HF, C_HI, C_HG, C_RQ, C_RK, C_RV, C_RG = 0, 512, 1024, 1792, 2560, 3328, 4096, 4864, 5632, 6400


import os
KVAR = os.environ.get('KVAR', '')


class Prog:
    ENG = ("pe", "act", "dve", "pool", "sp")

    def __init__(self, nc, n_dma_sems=10):
        self.nc = nc
        self.ops = []
        self.last_write = {}
        self.readers = {}
        self.nd = n_dma_sems
        self.pending_bar = {}

    @staticmethod
    def _freeze(fn):
        if fn.__closure__ is None:
            return fn
        cells = []
        for c in fn.__closure__:
            try:
                cells.append(types.CellType(c.cell_contents))
            except ValueError:
                cells.append(c)
        return types.FunctionType(fn.__code__, fn.__globals__, fn.__name__, fn.__defaults__, tuple(cells))

    def op(self, eng, fn, reads=(), writes=(), kind="c"):
        fn = self._freeze(fn)
        deps = set()
        for k in reads:
            w = self.last_write.get(k)
            if w is not None:
                deps.add(w)
            if isinstance(k, tuple) and k[0] == "ps":
                for r in self.readers.get(k, ()):
                    if self.ops[r]["eng"] != eng:
                        deps.add(r)
        for k in writes:
            w = self.last_write.get(k)
            if w is not None:
                deps.add(w)
            deps.update(self.readers.get(k, ()))
        if eng in self.pending_bar:
            deps.update(self.pending_bar.pop(eng))
        idx = len(self.ops)
        self.ops.append(dict(eng=eng, fn=fn, deps=deps, kind=kind, sig=False))
        for k in reads:
            self.readers.setdefault(k, []).append(idx)
        for k in writes:
            self.last_write[k] = idx
            self.readers[k] = []
        return idx

    def dma(self, eng, fn, reads=(), writes=()):
        return self.op(eng, fn, reads, writes, kind="d")

    def cc(self, fn, reads=(), writes=()):
        return self.op("pool", fn, reads, writes, kind="cc")

    def barrier(self):
        last = {}
        asyncs = []
        for i, o in enumerate(self.ops):
            if o["kind"] == "c":
                last[o["eng"]] = i
            else:
                asyncs.append(i)
        start = getattr(self, "_bar_from", 0)
        dep = set(last.values()) | set(i for i in asyncs if i >= start)
        self._bar_from = len(self.ops)
        for e in self.ENG:
            self.pending_bar[e] = set(dep) | self.pending_bar.get(e, set())

    def emit(self, final_wait_ops=()):
        nc = self.nc
        ops = self.ops

        def skip(o, od):
            return od["kind"] == "c" and o["kind"] == "c" and od["eng"] == o["eng"] == "pe"

        for o in ops:
            for d in o["deps"]:
                od = ops[d]
                if od["kind"] == "c" and not skip(o, od):
                    od["sig"] = True
        with contextlib.ExitStack() as st:
            csem = {e: st.enter_context(nc.semaphore("c_" + e)) for e in self.ENG}
            qs = ("sp", "pool", "cc")
            dsem = {q: [st.enter_context(nc.semaphore("d_%s_%d" % (q, j))) for j in range(self.nd)] for q in qs}
            ccount = {e: 0 for e in self.ENG}
            dcount = {q: 0 for q in qs}
            for o in ops:
                if o["kind"] != "c":
                    q = "cc" if o["kind"] == "cc" else o["eng"]
                    inc = 1 if o["kind"] == "cc" else 16
                    n = dcount[q]
                    dcount[q] += 1
                    o["sem"] = dsem[q][n % self.nd]
                    o["val"] = inc * (n // self.nd + 1)
                    o["inc"] = inc
                    o["n"] = n
                elif o["sig"]:
                    ccount[o["eng"]] += 1
                    o["sem"] = csem[o["eng"]]
                    o["val"] = ccount[o["eng"]]
            per_eng = {e: [] for e in self.ENG}
            for i, o in enumerate(ops):
                per_eng[o["eng"]].append(i)
            self.stats = dict(ccount)

            def run(e, handle):
                waited = {}
                for i in per_eng[e]:
                    o = ops[i]
                    need = {}

                    def want(s, v):
                        k = id(s)
                        if need.get(k, (None, 0))[1] < v:
                            need[k] = (s, v)
                    for d in o["deps"]:
                        od = ops[d]
                        if skip(o, od):
                            continue
                        want(od["sem"], od["val"])
                    if o["kind"] != "c" and o["n"] >= self.nd:
                        want(o["sem"], o["val"] - o["inc"])
                    for k, (s, v) in need.items():
                        if waited.get(k, 0) >= v:
                            continue
                        handle.wait_ge(s, v)
                        waited[k] = v
                    inst = o["fn"]()
                    if o["kind"] != "c":
                        inst.then_inc(o["sem"], o["inc"])
                    elif o["sig"]:
                        inst.then_inc(o["sem"], 1)
                if e == "sp":
                    for i in final_wait_ops:
                        o = ops[i]
                        handle.wait_ge(o["sem"], o["val"])

            with nc.Block() as block:
                @block.tensor
                def _(h):
                    run("pe", h)

                @block.scalar
                def _(h):
                    run("act", h)

                @block.vector
                def _(h):
                    run("dve", h)

                @block.gpsimd
                def _(h):
                    run("pool", h)

                @block.sync
                def _(h):
                    run("sp", h)


class Carver:
    def __init__(self, big, nbytes):
        self.big = big
        self.n = nbytes
        self.off = 0

    def reset(self, off=0):
        self.off = off

    def take(self, shape, dt, parts=128):
        esz = {F32: 4, BF16: 2, I32: 4}[dt]
        n = esz
        for s in shape[1:]:
            n *= s
        off = (self.off + 31) // 32 * 32
        assert off + n <= self.n, ("SBUF carve overflow", off, n, self.n)
        self.off = off + n
        v = self.big[0:parts, off:off + n].bitcast(dt)
        if len(shape) > 2:
            names = " ".join("a%d" % i for i in range(len(shape) - 1))
            kw = {"a%d" % i: shape[i + 1] for i in range(len(shape) - 1)}
            v = v.rearrange("p (%s) -> p %s" % (names, names), **kw)
        return v


def build_nc(n_layers, layer0=0, x_fm_in=False, final_out=True, stop_after=None, wlayers=L_ALL, wrows=None):
    nc = bass.Bass("TRN2", target_bir_lowering=False)
    L = n_layers
    dr = lambda name, shape, dt, kind=None: (nc.dram_tensor(name, shape, dt, kind=kind) if kind else nc.dram_tensor(name, shape, dt))
    x_in = dr("x_in", [T, D], F32, "ExternalInput").ap()
    pos_in = dr("pos_in", [128, NT], I32, "ExternalInput").ap()
    w_in = dr("w_in", [wlayers, wrows or D, DIN], F32, "ExternalInput").ap()
    w_out = dr("w_out", [wlayers, wrows or D, D], F32, "ExternalInput").ap()
    w_ff1 = dr("w_ff1", [wlayers, wrows or D, DFF], F32, "ExternalInput").ap()
    w_ff2 = dr("w_ff2", [wlayers, wrows or DFF, D], F32, "ExternalInput").ap()
    NV = 4 * L_ALL * 16 + L_ALL * 4 * (CW + 3) + L_ALL * 6 * 4
    vec_in = dr("vec_in", [128, NV], F32, "ExternalInput").ap()
    NCONST = 128 * 5 + 64 + 2
    const_in = dr("const_in", [128, NCONST], F32, "ExternalInput").ap()
    y_out = dr("y_out", [T, D], F32, "ExternalOutput").ap()
    xres = dr("xres", [16, 128, T], F32).ap()
    x1res = dr("x1res", [16, 128, T], F32).ap()
    fd = dr("fd", [64, 128, T], BF16).ap()
    mixd = dr("mixd", [16, 128, T], BF16).ap()
    NX = 13
    cc_i = [[dr("cci_%d_%d" % (l, u), [128, 128], F32) for u in range(NX)] for l in range(L)]
    cc_o = [[dr("cco_%d_%d" % (l, u), [256, 128], F32) for u in range(NX)] for l in range(L)]

    P = Prog(nc)
    with contextlib.ExitStack() as st:
        A = st.enter_context(nc.sbuf_tensor("A", [128, 16, T], BF16))
        RB = 128 * 1024
        Rbig = st.enter_context(nc.sbuf_tensor("Rbig", [128, RB], U8))
        R = Carver(Rbig, RB)
        Abig_view = None
        cst = st.enter_context(nc.sbuf_tensor("cst", [128, NCONST], F32))
        vec = st.enter_context(nc.sbuf_tensor("vec", [128, NV], F32))
        identb = st.enter_context(nc.sbuf_tensor("identb", [128, 128], BF16))
        cs = st.enter_context(nc.sbuf_tensor("cs", [128, 2, NT, 64], BF16))
        sm = st.enter_context(nc.sbuf_tensor("sm", [128, 512], F32))
        lnsq = st.enter_context(nc.sbuf_tensor("lnsq", [128, 512], F32))
        ps = [st.enter_context(nc.psum_tensor("ps%d" % i, [128, 512], F32)) for i in range(8)]
        bank_ctr = [0]

        def bank(pool="a"):
            i = 0 if pool == "a" else 1
            if len(bank_ctr) < 2:
                bank_ctr.append(0)
            b = 4 * i + bank_ctr[i] % 4
            bank_ctr[i] += 1
            return b, ("ps", b)
        stat_ctr = [0]

        def stat_banks():
            p = stat_ctr[0] % 2
            stat_ctr[0] += 1
            return (4 + 2 * p, ("ps", 4 + 2 * p)), (5 + 2 * p, ("ps", 5 + 2 * p))

        ident = cst[:, 0:128]
        ones = cst[:, 128:256]
        tri = cst[:, 256:384]
        tms = cst[:, 384:512]
        iot1 = cst[:, 512:640]
        invf = cst[:, 640:704]
        colr = cst[:, 704:705]
        selc = cst[:, 705:706]

        def vln(which, l, c):
            o = (which * L_ALL + l) * 16 + c
            return vec[:, o:o + 1]
        VB = 4 * L_ALL * 16

        def vconv(l, cc, j):
            o = VB + (l * 4 + cc) * (CW + 3) + j
            return vec[:, o:o + 1]
        VH = VB + L_ALL * 4 * (CW + 3)

        def vhead(which, l, h):
            o = VH + (which * L_ALL + l) * 6 + h
            return vec[:, o:o + 1]

        lbt = sm[:, 0:24]
        oml = sm[:, 24:48]
        agb = sm[:, 48:176]
        smx = sm[:, 176:512]

        def agcol(which, l, c):
            o = 48 + (which * L_ALL + l) * 16 + c
            return sm[:, o:o + 1]

        P.dma("sp", lambda: nc.sync.dma_start(out=cst[:], in_=const_in), writes=["cst"])
        P.dma("sp", lambda: nc.sync.dma_start(out=vec[:], in_=vec_in), writes=["vec"])
        P.op("act", lambda: nc.scalar.activation(out=identb[:], in_=ident, func=AF.Copy), reads=["cst"], writes=["identb"])
        P.op("act", lambda: nc.scalar.activation(out=sm[:, 48:176], in_=vec[:, 0:2 * L_ALL * 16], func=AF.Copy, scale=ALPHA),
             reads=["vec"], writes=["agb"])
        lbr = vec[:, VH:VH + 24].rearrange("p (l h) -> p h l", l=L_ALL)
        e4 = smx[:, 0:24].rearrange("p (h l) -> p h l", l=L_ALL)
        mx = smx[:, 24:30]
        P.op("dve", lambda: nc.vector.tensor_reduce(out=mx, in_=lbr, axis=mybir.AxisListType.X, op=ALU.max), reads=["vec"], writes=["mx"])
        P.op("dve", lambda: nc.vector.tensor_tensor(out=e4, in0=lbr, in1=mx.unsqueeze(2).to_broadcast([128, 6, L_ALL]), op=ALU.subtract),
             reads=["mx", "vec"], writes=["e4"])
        P.op("act", lambda: nc.scalar.activation(out=e4, in_=e4, func=AF.Exp), reads=["e4"], writes=["e4"])
        sm6 = smx[:, 30:36]
        P.op("dve", lambda: nc.vector.tensor_reduce(out=sm6, in_=e4, axis=mybir.AxisListType.X, op=ALU.add), reads=["e4"], writes=["sm6"])
        P.op("dve", lambda: nc.vector.reciprocal(out=sm6, in_=sm6), reads=["sm6"], writes=["sm6"])
        P.op("dve", lambda: nc.vector.tensor_tensor(out=e4, in0=e4, in1=sm6.unsqueeze(2).to_broadcast([128, 6, L_ALL]), op=ALU.mult),
             reads=["sm6", "e4"], writes=["e4"])
        lbv = lbt.rearrange("p (l h) -> p h l", l=L_ALL)
        P.op("dve", lambda: nc.vector.memset(lbv[:, :, 0:1], 0.0), writes=["lbt"])
        for l in range(1, L_ALL):
            P.op("dve", (lambda l=l: nc.vector.tensor_tensor(out=lbv[:, :, l:l + 1], in0=lbv[:, :, l - 1:l], in1=e4[:, :, l:l + 1], op=ALU.add)),
                 reads=["e4", "lbt"], writes=["lbt"])
        P.op("dve", lambda: nc.vector.tensor_scalar(out=lbt, in0=lbt, scalar1=0.0, scalar2=1.0 - 1e-6, op0=ALU.max, op1=ALU.min),
             reads=["lbt"], writes=["lbt"])
        P.op("dve", lambda: nc.vector.tensor_scalar(out=oml, in0=lbt, scalar1=-1.0, scalar2=1.0, op0=ALU.mult, op1=ALU.add),
             reads=["lbt"], writes=["oml"])
        if stop_after == "s0":
            P.emit()
            return nc
        posf = smx[:, 40:56]
        P.dma("sp", lambda: nc.sync.dma_start(out=smx[:, 56:72].bitcast(I32), in_=pos_in), writes=["posi"])
        P.op("dve", lambda: nc.vector.tensor_copy(out=posf, in_=smx[:, 56:72].bitcast(I32)), reads=["posi"], writes=["posf"])
        R.reset()
        ang = R.take([128, NT, 64], F32)
        tq = R.take([128, NT, 64], F32)
        ti = R.take([128, NT, 64], I32)
        TWO_PI = 2.0 * math.pi
        for j in range(NT):
            P.op("dve", (lambda j=j: nc.vector.tensor_scalar(out=ang[:, j, :], in0=invf, scalar1=posf[:, j:j + 1], scalar2=None, op0=ALU.mult)),
                 reads=["posf", "cst"], writes=["ang"])
        for which in (1, 0):
            shift = 0.0 if which == 1 else math.pi / 2
            P.op("dve", (lambda s=shift: nc.vector.tensor_scalar(out=tq, in0=ang, scalar1=s, scalar2=1.0 / TWO_PI, op0=ALU.add, op1=ALU.mult)),
                 reads=["ang"], writes=["tq"])
            P.op("dve", lambda: nc.vector.tensor_copy(out=ti, in_=tq), reads=["tq"], writes=["ti"])
            P.op("dve", lambda: nc.vector.tensor_copy(out=tq, in_=ti), reads=["ti"], writes=["tq"])
            P.op("dve", lambda: nc.vector.tensor_scalar(out=tq, in0=tq, scalar1=-TWO_PI, scalar2=None, op0=ALU.mult), reads=["tq"], writes=["tq"])
            P.op("dve", (lambda s=shift: nc.vector.scalar_tensor_tensor(out=tq, in0=ang, scalar=s, in1=tq, op0=ALU.add, op1=ALU.add)),
                 reads=["ang", "tq"], writes=["tq"])
            tf = ti.bitcast(F32)
            P.op("dve", lambda: nc.vector.tensor_scalar(out=tf, in0=tq, scalar1=math.pi, scalar2=-TWO_PI, op0=ALU.is_gt, op1=ALU.mult),
                 reads=["tq"], writes=["ti"])
            P.op("dve", lambda: nc.vector.tensor_tensor(out=tq, in0=tq, in1=tf, op=ALU.add), reads=["tq", "ti"], writes=["tq"])
            P.op("dve", lambda: nc.vector.tensor_scalar(out=tf, in0=tq, scalar1=-math.pi, scalar2=TWO_PI, op0=ALU.is_lt, op1=ALU.mult),
                 reads=["tq"], writes=["ti"])
            P.op("dve", lambda: nc.vector.tensor_tensor(out=tq, in0=tq, in1=tf, op=ALU.add), reads=["tq", "ti"], writes=["tq"])
            P.op("dve", lambda: nc.vector.tensor_scalar(out=tq, in0=tq, scalar1=-math.pi, scalar2=math.pi, op0=ALU.max, op1=ALU.min),
                 reads=["tq"], writes=["tq"])
            P.op("act", (lambda w=which: nc.scalar.activation(out=cs[:, w, :, :], in_=tq, func=AF.Sin)), reads=["tq"], writes=["cs"])
        P.barrier()

        if stop_after == "s1":
            P.emit()
            return nc
        def load_x():
            R.reset()
            xt = [R.take([128, D], F32) for _ in range(4)]
            stg = [R.take([128, 512], F32) for _ in range(4)]
            n = 0
            for tg in range(TG):
                for jj in range(4):
                    j = tg * 4 + jj
                    if KVAR != "nodma":
                        P.dma("sp", (lambda j=j, jj=jj: nc.sync.dma_start(out=xt[jj], in_=x_in[j * 128:(j + 1) * 128, :])), writes=[("xt", jj)])
                for c in range(16):
                    b, bk = bank()
                    for jj in range(4):
                        if KVAR == "notr":
                            continue
                        P.op("pe", (lambda b=b, jj=jj, c=c: nc.tensor.transpose(ps[b][:, jj * 128:(jj + 1) * 128], xt[jj][:, c * 128:(c + 1) * 128], ident)),
                             reads=[("xt", jj), "cst"], writes=[bk])
                    P.op("act", (lambda b=b, c=c, tg=tg: nc.scalar.activation(out=A[:, c, tg * 512:(tg + 1) * 512], in_=ps[b][:], func=AF.Copy)),
                         reads=[bk], writes=[("A", c, tg)])
                    s = n % 4
                    n += 1
                    P.op("dve", (lambda b=b, s=s: nc.vector.tensor_copy(out=stg[s], in_=ps[b][:])), reads=[bk], writes=[("stg", s)])
                    if KVAR != "noxres":
                        P.dma("sp", (lambda s=s, c=c, tg=tg: nc.sync.dma_start(out=xres[c][:, tg * 512:(tg + 1) * 512], in_=stg[s])),
                          reads=[("stg", s)], writes=[("xres", c, tg)])
            P.barrier()

        load_x()
        if stop_after == "setup":
            P.emit()
            return nc

        def ln_add(stb, c, r_ap, rkey):
            (b1, k1), (b2, k2) = stb
            P.op("act", (lambda: nc.scalar.activation(out=lnsq[:], in_=r_ap, func=AF.Square)), reads=[rkey], writes=["lnsq"])
            P.op("pe", (lambda: nc.tensor.matmul(ps[b1][:], lhsT=ones, rhs=r_ap, start=(c == 0), stop=(c == 15))), reads=[rkey, "cst"], writes=[k1])
            P.op("pe", (lambda: nc.tensor.matmul(ps[b2][:], lhsT=ones, rhs=lnsq[:], start=(c == 0), stop=(c == 15))), reads=["lnsq", "cst"], writes=[k2])

        def ln_finish(stb, r_of, rkeys, emit_out):
            (b1, k1), (b2, k2) = stb
            P.op("act", lambda: nc.scalar.activation(out=ps[b1][:], in_=ps[b1][:], func=AF.Copy, scale=1.0 / D), reads=[k1], writes=[k1])
            P.op("act", lambda: nc.scalar.activation(out=lnsq[:], in_=ps[b1][:], func=AF.Square), reads=[k1], writes=["lnsq"])
            P.op("dve", lambda: nc.vector.scalar_tensor_tensor(out=ps[b2][:], in0=ps[b2][:], scalar=1.0 / D, in1=lnsq[:], op0=ALU.mult, op1=ALU.subtract),
                 reads=[k2, "lnsq"], writes=[k2])
            P.op("dve", lambda: nc.vector.tensor_scalar(out=ps[b2][:], in0=ps[b2][:], scalar1=0.0, scalar2=EPS, op0=ALU.max, op1=ALU.add), reads=[k2], writes=[k2])
            P.op("act", lambda: nc.scalar.activation(out=ps[b2][:], in_=ps[b2][:], func=AF.Sqrt), reads=[k2], writes=[k2])
            P.op("dve", lambda: nc.vector.reciprocal(out=ps[b2][:], in_=ps[b2][:]), reads=[k2], writes=[k2])
            for c in range(16):
                P.op("dve", (lambda c=c: nc.vector.tensor_tensor(out=r_of(c), in0=r_of(c), in1=ps[b1][:], op=ALU.subtract)),
                     reads=[rkeys(c), k1], writes=[rkeys(c)])
                P.op("dve", (lambda c=c: nc.vector.tensor_tensor(out=r_of(c), in0=r_of(c), in1=ps[b2][:], op=ALU.mult)),
                     reads=[rkeys(c), k2], writes=[rkeys(c)])
                emit_out(c)

        groups = [[0, 1], [2, 3], [4, 5], [6, 7]]

        def exchange(l, u, src, srckey, gat, gatkey, n=128):
            ci, co = cc_i[l][u], cc_o[l][u]
            P.dma("sp", lambda: nc.sync.dma_start(out=ci[:, 0:n], in_=src), reads=[srckey], writes=[("cci", l, u)])
            P.cc(lambda: nc.gpsimd.collective_compute("AllGather", ALU.bypass, replica_groups=groups,
                                                      ins=[ci.ap().opt()], outs=[co.ap().opt()]),
                 reads=[("cci", l, u)], writes=[("cco", l, u)])
            P.dma("sp", lambda: nc.sync.dma_start(out=gat, in_=co[0:128, 0:n]), reads=[("cco", l, u)], writes=[gatkey])

        out_ops = []
        for li in range(L):
            l = layer0 + li
            last = (li == L - 1)
            wl = w_in[l].rearrange("(k p) n -> p k n", p=128)
            wol = w_out[l].rearrange("(k p) n -> p k n", p=128)
            w1l = w_ff1[l].rearrange("(k p) n -> p k n", p=128)
            w2l = w_ff2[l]

            R.reset()
            m_base = R.off
            NWS = 4
            wrings = {"H": [R.take([128, 16, 128], BF16) for _ in range(NWS)], "R": [R.take([128, 16, 128], BF16) for _ in range(NWS)]}
            wctr = {"H": 0, "R": 0}
            mstg = [R.take([128, 512], BF16) for _ in range(3)]
            mctr = [0]

            def mix_slot():
                s = mctr[0] % 3
                mctr[0] += 1
                return mstg[s], ("mstg", s)

            def mix_flush(ap, key, chunk, tg):
                P.dma("sp", (lambda: nc.sync.dma_start(out=mixd[chunk][:, tg * 512:(tg + 1) * 512], in_=ap)), reads=[key], writes=[("mixd", chunk, tg)])

            def wpiece(col0, ring="H"):
                s = wctr[ring] % NWS
                wctr[ring] += 1
                wt = wrings[ring][s]
                P.dma("pool", (lambda: nc.gpsimd.dma_start(out=wt, in_=wl[:, :, col0:col0 + 128])), writes=[("wr", ring, s)])
                return wt, ("wr", ring, s)

            def proj_fm(col0, ring="H", pool="a"):
                w, wk = wpiece(col0, ring)

                def get(tg):
                    b, bk = bank(pool)
                    for k in range(16):
                        P.op("pe", (lambda b=b, k=k, tg=tg, w=w: nc.tensor.matmul(ps[b][:], lhsT=w[:, k, :], rhs=A[:, k, tg * 512:(tg + 1) * 512],
                                                                                   start=(k == 0), stop=(k == 15))),
                             reads=[wk] + ([("A", kk, tg) for kk in range(16)] if k == 0 else []), writes=[bk])
                    return (b, bk)
                return get

            tw = [R.take([128, 512], F32) for _ in range(6)]
            m_mid = R.off
            ub = R.take([128, 4, HALO + T], BF16)
            usq = R.take([128, 4, 512], F32)
            yc = R.take([128, 4, 512], F32)
            dg = [R.take([128, 128], BF16) for _ in range(4)]
            gat = R.take([128, 128], F32)
            tail = R.take([128, 128], F32)
            P.op("dve", lambda: nc.vector.memset(tail[:, :], 0.0), writes=["tail"])
            for cc in range(4):
                pa = proj_fm(C_CA + cc * 128)
                pg = proj_fm(C_CG + cc * 128)
                for tg in range(TG):
                    (ba, ka), (bg, kg) = pa(tg), pg(tg)
                    P.op("act", (lambda bg=bg: nc.scalar.activation(out=tw[0], in_=ps[bg][:], func=AF.Sigmoid)), reads=[kg], writes=["tw0"])
                    P.op("dve", (lambda ba=ba, cc=cc, tg=tg: nc.vector.tensor_tensor(out=ub[:, cc, HALO + tg * 512:HALO + (tg + 1) * 512], in0=ps[ba][:], in1=tw[0], op=ALU.mult)),
                         reads=[ka, "tw0"], writes=[("ub", cc, tg)])
                P.op("act", (lambda cc=cc: nc.scalar.activation(out=tail[:, cc * 32:cc * 32 + HALO], in_=ub[:, cc, T:T + HALO], func=AF.Copy)),
                     reads=[("ub", cc, 3)], writes=["tail"])
            exchange(li, 12, tail[:, :], "tail", gat[:, :], "gat")
            for cc in range(4):
                P.op("dve", (lambda cc=cc: nc.vector.tensor_scalar(out=ub[:, cc, 0:HALO], in0=gat[:, cc * 32:cc * 32 + HALO], scalar1=selc, scalar2=None, op0=ALU.mult)),
                     reads=["gat", "cst"], writes=[("ubh", cc)])
            for tg in range(TG):
                cb = []
                for cc in range(4):
                    b, bk = bank()
                    cb.append((b, bk))
                    for j in range(CW):
                        s = (cc * CW + j) % 4
                        P.op("act", (lambda s=s, cc=cc, j=j: nc.scalar.activation(out=dg[s], in_=ident, func=AF.Identity, scale=vconv(l, cc, j))),
                             reads=["cst", "vec"], writes=[("dg", s)])
                        rk = [("ub", cc, tg)] + ([("ub", cc, tg - 1)] if tg > 0 else [("ubh", cc)])
                        P.op("pe", (lambda b=b, s=s, cc=cc, j=j, tg=tg: nc.tensor.matmul(ps[b][:], lhsT=dg[s], rhs=ub[:, cc, tg * 512 + j:tg * 512 + j + 512],
                                                                                         start=(j == 0), stop=(j == CW - 1))),
                             reads=[("dg", s)] + rk, writes=[bk])
                    P.op("act", (lambda b=b, cc=cc: nc.scalar.activation(out=yc[:, cc, :], in_=ps[b][:], func=AF.Identity, bias=vconv(l, cc, CW))),
                         reads=[bk, "vec"], writes=[("yc", cc)])
                    P.op("act", (lambda cc=cc: nc.scalar.activation(out=usq[:, cc, :], in_=yc[:, cc, :], func=AF.Square)),
                         reads=[("yc", cc)], writes=[("usq", cc)])
                b1, k1 = bank()
                b2, k2 = bank()
                for cc in range(4):
                    P.op("pe", (lambda cc=cc: nc.tensor.matmul(ps[b1][:], lhsT=ones, rhs=yc[:, cc, :], start=(cc == 0), stop=(cc == 3))),
                         reads=[("yc", cc), "cst"], writes=[k1])
                    P.op("pe", (lambda cc=cc: nc.tensor.matmul(ps[b2][:], lhsT=ones, rhs=usq[:, cc, :], start=(cc == 0), stop=(cc == 3))),
                         reads=[("usq", cc), "cst"], writes=[k2])
                mu, var = tw[1], tw[2]
                P.op("act", lambda: nc.scalar.activation(out=mu, in_=ps[b1][:], func=AF.Copy, scale=1.0 / 512), reads=[k1], writes=["tw1"])
                P.op("act", lambda: nc.scalar.activation(out=var, in_=ps[b1][:], func=AF.Square, scale=1.0 / 512), reads=[k1], writes=["tw2"])
                P.op("dve", lambda: nc.vector.scalar_tensor_tensor(out=var, in0=ps[b2][:], scalar=1.0 / 512, in1=var, op0=ALU.mult, op1=ALU.subtract),
                     reads=[k2, "tw2"], writes=["tw2"])
                P.op("dve", lambda: nc.vector.tensor_scalar(out=var, in0=var, scalar1=0.0, scalar2=EPS, op0=ALU.max, op1=ALU.add), reads=["tw2"], writes=["tw2"])
                P.op("act", lambda: nc.scalar.activation(out=var, in_=var, func=AF.Sqrt), reads=["tw2"], writes=["tw2"])
                P.op("dve", lambda: nc.vector.reciprocal(out=var, in_=var), reads=["tw2"], writes=["tw2"])
                for cc in range(4):
                    P.op("dve", (lambda cc=cc: nc.vector.tensor_tensor(out=yc[:, cc, :], in0=yc[:, cc, :], in1=mu, op=ALU.subtract)),
                         reads=[("yc", cc), "tw1"], writes=[("yc", cc)])
                    P.op("dve", (lambda cc=cc: nc.vector.tensor_tensor(out=yc[:, cc, :], in0=yc[:, cc, :], in1=var, op=ALU.mult)),
                         reads=[("yc", cc), "tw2"], writes=[("yc", cc)])
                    mo, mk = mix_slot()
                    P.op("act", (lambda: nc.scalar.activation(out=mo, in_=yc[:, cc, :], func=AF.Silu, scale=vconv(l, cc, CW + 1), bias=vconv(l, cc, CW + 2))),
                         reads=[("yc", cc), "vec"], writes=[mk])
                    mix_flush(mo, mk, cc, tg)
            P.barrier()

            if stop_after == "conv":
                P.emit()
                return nc
            R.reset(m_mid)
            qt = R.take([128, T], BF16)
            kt = R.take([128, T], BF16)
            itm = R.take([128, 32, 128], BF16)
            sloc = R.take([128, 32, 128], BF16)
            khat = R.take([128, 32, 128], BF16)
            qg = R.take([128, 512], BF16)
            khf = qg
            pm = R.take([128, 512], BF16)
            Sf = R.take([128, 128], F32)
            sinb = R.take([128, 128], BF16)
            gath = R.take([128, 128], F32)
            hs = R.take([128, 160], F32)
            elast, eb, bv, glast = hs[:, 0:32], hs[:, 32:64], hs[:, 64:72], hs[:, 72:73]
            Gt = tw[3]
            def hgrn_gen():
                for h in range(NH):
                    lbc = lbt[:, l * 6 + h:l * 6 + h + 1]
                    omc = oml[:, l * 6 + h:l * 6 + h + 1]
                    pq = proj_fm(C_HQ + h * 128)
                    pf = proj_fm(C_HF + h * 128)
                    P.op("dve", lambda: nc.vector.memset(sloc[:, 0, :], 0.0), writes=[("sloc", 0)])
                    P.op("dve", lambda: nc.vector.memset(Sf[:, :], 0.0), writes=["Sf"])
                    for tg in range(TG):
                        (bq, kq), (bf_, kf) = pq(tg), pf(tg)
                        P.op("act", (lambda bq=bq: nc.scalar.activation(out=tw[0], in_=ps[bq][:], func=AF.Silu)), reads=[kq], writes=["tw0"])
                        P.op("act", (lambda b=bf_: nc.scalar.activation(out=tw[1], in_=ps[b][:], func=AF.Sigmoid)), reads=[kf], writes=["tw1"])
                        P.op("dve", lambda: nc.vector.tensor_scalar(out=tw[1], in0=tw[1], scalar1=omc, scalar2=lbc, op0=ALU.mult, op1=ALU.add),
                             reads=["tw1", "lbt", "oml"], writes=["tw1"])
                        P.op("act", lambda: nc.scalar.activation(out=tw[2], in_=tw[1], func=AF.Ln), reads=["tw1"], writes=["tw2"])
                        P.op("dve", lambda: nc.vector.tensor_scalar(out=tw[1], in0=tw[1], scalar1=-1.0, scalar2=1.0, op0=ALU.mult, op1=ALU.add),
                             reads=["tw1"], writes=["tw1"])
                        if tg == 0:
                            P.op("dve", lambda: nc.vector.tensor_tensor_scan(out=Gt, data0=ones[:, 0:1].to_broadcast([128, 512]), data1=tw[2], initial=0.0, op0=ALU.mult, op1=ALU.add),
                                 reads=["tw2", "cst"], writes=["Gt"])
                            P.op("dve", lambda: nc.vector.memset(bv[:, 0:1], 0.0), writes=["bv"])
                        else:
                            P.op("dve", lambda: nc.vector.tensor_copy(out=bv[:, 0:1], in_=glast), reads=["glast"], writes=["bv"])
                            P.op("dve", lambda: nc.vector.tensor_tensor_scan(out=Gt, data0=ones[:, 0:1].to_broadcast([128, 512]), data1=tw[2], initial=glast, op0=ALU.mult, op1=ALU.add),
                                 reads=["tw2", "cst", "glast"], writes=["Gt"])
                        Gv = Gt.rearrange("p (c i) -> p c i", i=64)
                        P.op("dve", lambda: nc.vector.tensor_copy(out=bv[:, 1:8], in_=Gv[:, 0:7, 63]), reads=["Gt"], writes=["bv"])
                        P.op("act", lambda: nc.scalar.activation(out=glast, in_=Gt[:, 511:512], func=AF.Copy), reads=["Gt"], writes=["glast"])
                        P.op("act", (lambda tg=tg: nc.scalar.activation(out=eb[:, tg * 8:tg * 8 + 8], in_=bv, func=AF.Exp)), reads=["bv"], writes=["eb"])
                        P.op("dve", lambda: nc.vector.tensor_tensor(out=Gv, in0=Gv, in1=bv.unsqueeze(2).to_broadcast([128, 8, 64]), op=ALU.subtract),
                             reads=["Gt", "bv"], writes=["Gt"])
                        P.op("act", lambda: nc.scalar.activation(out=tw[2], in_=Gt, func=AF.Exp), reads=["Gt"], writes=["tw2"])
                        P.op("act", lambda: nc.scalar.activation(out=tw[4], in_=Gt, func=AF.Exp, scale=-1.0), reads=["Gt"], writes=["tw4"])
                        e3 = tw[2].rearrange("p (c i) -> p c i", i=64)
                        P.op("act", (lambda tg=tg: nc.scalar.activation(out=elast[:, tg * 8:tg * 8 + 8], in_=e3[:, :, 63], func=AF.Copy)), reads=["tw2"], writes=["elast"])
                        P.op("dve", (lambda tg=tg: nc.vector.tensor_tensor(out=qt[:, tg * 512:(tg + 1) * 512], in0=tw[0], in1=tw[2], op=ALU.mult)),
                             reads=["tw0", "tw2"], writes=[("qt", tg)])
                        P.op("dve", lambda: nc.vector.tensor_tensor(out=tw[4], in0=tw[1], in1=tw[4], op=ALU.mult), reads=["tw1", "tw4"], writes=["tw4"])
                        P.op("act", (lambda tg=tg: nc.scalar.activation(out=kt[:, tg * 512:(tg + 1) * 512], in_=tw[4], func=AF.Copy)), reads=["tw4"], writes=[("kt", tg)])
                        k3 = tw[4].rearrange("p (c i) -> p c i", i=64)
                        P.op("dve", (lambda tg=tg: nc.vector.tensor_tensor(out=khf.rearrange("p (c i) -> p c i", i=64), in0=k3,
                                                                            in1=elast[:, tg * 8:tg * 8 + 8].unsqueeze(2).to_broadcast([128, 8, 64]), op=ALU.mult)),
                             reads=["tw4", "elast"], writes=["qg"])
                        for half in range(2):
                            b, bk = bank()
                            pv = ps[b][:].bitcast(BF16)
                            for jj in range(4):
                                j = half * 4 + jj
                                P.op("pe", (lambda pv=pv, jj=jj, j=j: nc.tensor.transpose(pv[0:64, jj * 128:(jj + 1) * 128], khf[:, j * 64:(j + 1) * 64], identb[:])),
                                     reads=["qg", "identb"], writes=[bk])
                            P.op("act", (lambda pv=pv, tg=tg, half=half: nc.scalar.activation(
                                out=khat[0:64, tg * 8 + half * 4:tg * 8 + half * 4 + 4, :], in_=pv[0:64, 0:512].rearrange("p (c d) -> p c d", d=128), func=AF.Copy)),
                                reads=[bk], writes=[("khat", tg)])
                        yield
                    wi, wik = wpiece(C_HI + h * 128)
                    for c4 in range(8):
                        b, bk = bank()
                        for jj in range(4):
                            c = c4 * 4 + jj
                            tg = c // 8
                            for k in range(16):
                                P.op("pe", (lambda b=b, jj=jj, c=c, k=k: nc.tensor.matmul(ps[b][0:64, jj * 128:(jj + 1) * 128], lhsT=A[:, k, c * 64:(c + 1) * 64], rhs=wi[:, k, :],
                                                                                           start=(k == 0), stop=(k == 15))),
                                     reads=[wik] + ([("A", kk, tg) for kk in range(16)] if k == 0 else []), writes=[bk])
                        P.op("act", (lambda b=b, c4=c4: nc.scalar.activation(out=itm[0:64, c4 * 4:c4 * 4 + 4, :], in_=ps[b][0:64, :].rearrange("p (c d) -> p c d", d=128), func=AF.Copy)),
                             reads=[bk], writes=[("itm", c4)])
                        yield
                    for c in range(32):
                        if c % 8 == 0 and c > 0:
                            yield
                        b, bk = bank()
                        P.op("pe", (lambda b=b, c=c: nc.tensor.matmul(ps[b][:, 0:128], lhsT=khat[0:64, c, :], rhs=itm[0:64, c, :], start=True, stop=True)),
                             reads=[("khat", c // 8), ("itm", c // 4)], writes=[bk])
                        P.op("dve", (lambda b=b, c=c: nc.vector.scalar_tensor_tensor(out=Sf[:, :], in0=Sf[:, :], scalar=elast[:, c:c + 1], in1=ps[b][:, 0:128], op0=ALU.mult, op1=ALU.add)),
                             reads=[bk, "Sf", "elast"], writes=["Sf"])
                        if c < 31:
                            P.op("act", (lambda c=c: nc.scalar.activation(out=sloc[:, c + 1, :], in_=Sf[:, :], func=AF.Copy)), reads=["Sf"], writes=[("sloc", c + 1)])
                    exchange(li, h, Sf[:, :], "Sf", gath[:, :], "gath")
                    yield
                    P.op("dve", lambda: nc.vector.tensor_scalar(out=sinb[:, :], in0=gath[:, :], scalar1=selc, scalar2=None, op0=ALU.mult), reads=["gath", "cst"], writes=["sinb"])
                    pgate = proj_fm(C_HG + h * 128)
                    for tg in range(TG):
                        bs, ks = bank()
                        for j in range(8):
                            c = tg * 8 + j
                            P.op("pe", (lambda j=j, c=c: nc.tensor.matmul(ps[bs][0:64, j * 64:(j + 1) * 64], lhsT=kt[:, c * 64:(c + 1) * 64], rhs=qt[:, c * 64:(c + 1) * 64], start=True, stop=True)),
                                 reads=[("kt", tg), ("qt", tg)], writes=[ks])
                        P.op("dve", lambda: nc.vector.tensor_tensor(out=pm[0:64, :].rearrange("p (c i) -> p c i", i=64), in0=ps[bs][0:64, :].rearrange("p (c i) -> p c i", i=64),
                                                                     in1=tri[0:64, 0:64].unsqueeze(1).to_broadcast([64, 8, 64]), op=ALU.mult),
                             reads=[ks, "cst"], writes=["pm"])
                        P.op("dve", (lambda tg=tg: nc.vector.tensor_tensor(out=qg.rearrange("p (c i) -> p c i", i=64), in0=qt[:, tg * 512:(tg + 1) * 512].rearrange("p (c i) -> p c i", i=64),
                                                                            in1=eb[:, tg * 8:tg * 8 + 8].unsqueeze(2).to_broadcast([128, 8, 64]), op=ALU.mult)),
                             reads=[("qt", tg), "eb"], writes=["qg"])
                        bo, ko = bank()
                        for j in range(8):
                            c = tg * 8 + j
                            osl = ps[bo][:, j * 64:(j + 1) * 64]
                            P.op("pe", (lambda osl=osl, c=c, j=j: nc.tensor.matmul(osl, lhsT=itm[0:64, c, :], rhs=pm[0:64, j * 64:(j + 1) * 64], start=True, stop=False)),
                                 reads=[("itm", c // 4), "pm"], writes=[ko])
                            P.op("pe", (lambda osl=osl, c=c: nc.tensor.matmul(osl, lhsT=sloc[:, c, :], rhs=qt[:, c * 64:(c + 1) * 64], start=False, stop=False)),
                                 reads=[("sloc", c), ("qt", tg)], writes=[ko])
                            P.op("pe", (lambda osl=osl, j=j: nc.tensor.matmul(osl, lhsT=sinb[:, :], rhs=qg[:, j * 64:(j + 1) * 64], start=False, stop=True)),
                                 reads=["sinb", "qg"], writes=[ko])
                        P.op("act", lambda: nc.scalar.activation(out=tw[5], in_=ps[bo][:], func=AF.Square), reads=[ko], writes=["tw5"])
                        bn_, kn = bank()
                        P.op("pe", lambda: nc.tensor.matmul(ps[bn_][:], lhsT=ones, rhs=tw[5], start=True, stop=True), reads=["tw5", "cst"], writes=[kn])
                        P.op("dve", lambda: nc.vector.tensor_scalar(out=tw[5], in0=ps[bn_][:], scalar1=1.0 / 128, scalar2=EPS, op0=ALU.mult, op1=ALU.add), reads=[kn], writes=["tw5"])
                        P.op("act", lambda: nc.scalar.activation(out=tw[5], in_=tw[5], func=AF.Sqrt), reads=["tw5"], writes=["tw5"])
                        P.op("dve", lambda: nc.vector.reciprocal(out=tw[5], in_=tw[5]), reads=["tw5"], writes=["tw5"])
                        P.op("dve", lambda: nc.vector.tensor_tensor(out=tw[5], in0=ps[bo][:], in1=tw[5], op=ALU.mult), reads=[ko, "tw5"], writes=["tw5"])
                        bg, kg = pgate(tg)
                        P.op("act", (lambda bg=bg: nc.scalar.activation(out=tw[0], in_=ps[bg][:], func=AF.Sigmoid)), reads=[kg], writes=["tw0"])
                        mo, mk = mix_slot()
                        P.op("dve", (lambda: nc.vector.scalar_tensor_tensor(out=mo, in0=tw[5], scalar=vhead(1, l, h), in1=tw[0], op0=ALU.mult, op1=ALU.mult)),
                             reads=["tw5", "tw0", "vec"], writes=[mk])
                        mix_flush(mo, mk, 4 + h, tg)
                        yield
            rw = [R.take([128, 512], F32) for _ in range(4)]
            qf = R.take([128, T], BF16)
            kf_ = R.take([128, T], BF16)
            qd = R.take([128, T], BF16)
            ktm = R.take([128, NT, 128], BF16)
            vtm = R.take([128, NT, 128], BF16)
            slr = R.take([128, NT, 128], BF16)
            rotL = [R.take([128, 2, 128], F32) for _ in range(2)]
            rtL = [[R.take([128, 2, 64], F32) for _ in range(4)] for _ in range(2)]
            rbfL = [R.take([128, 2, 128], BF16) for _ in range(2)]
            pmr = R.take([128, 512], BF16)
            qgr = R.take([128, 128], BF16)
            dth = R.take([128, 128], F32)
            rdh = R.take([128, 128], F32)
            kdc = R.take([128, 2], F32)
            Sr = R.take([128, 128], F32)
            sinr = R.take([128, 128], BF16)
            gatr = R.take([128, 128], F32)
            osb = R.take([128, 512], F32)
            def ret_gen():
                for h in range(NH):
                    lg = LOGG[h]
                    P.op("act", (lambda lg=lg: nc.scalar.activation(out=dth[:, :], in_=tms, func=AF.Exp, scale=lg)), reads=["cst"], writes=["dth"])
                    P.op("dve", lambda: nc.vector.tensor_tensor(out=dth[:, :], in0=dth[:, :], in1=tri, op=ALU.mult), reads=["dth", "cst"], writes=["dth"])
                    P.op("act", (lambda lg=lg: nc.scalar.activation(out=rdh[:, :], in_=iot1, func=AF.Exp, scale=lg)), reads=["cst"], writes=["rdh"])
                    P.op("act", (lambda lg=lg: nc.scalar.activation(out=kdc[:, 0:1], in_=colr, func=AF.Exp, scale=lg)), reads=["cst"], writes=["kdc"])
                    P.op("act", lambda: nc.scalar.mul(out=kdc[:, 1:2], in_=kdc[:, 0:1], mul=128.0 ** -0.5), reads=["kdc"], writes=["kdc2"])
                    w3 = [wpiece(c0 + h * 128, "R") for c0 in (C_RQ, C_RK, C_RV)]
                    P.op("dve", lambda: nc.vector.memset(slr[:, 0, :], 0.0), writes=[("slr", 0)])
                    P.op("dve", lambda: nc.vector.memset(Sr[:, :], 0.0), writes=["Sr"])
                    for j in range(NT):
                        tg = j // 4
                        b, bk = bank("b")
                        for pi in range(3):
                            wq, wqk = w3[pi]
                            for k in range(16):
                                P.op("pe", (lambda b=b, j=j, k=k, pi=pi, wq=wq: nc.tensor.matmul(ps[b][:, pi * 128:(pi + 1) * 128], lhsT=A[:, k, j * 128:(j + 1) * 128], rhs=wq[:, k, :], start=(k == 0), stop=(k == 15))),
                                     reads=[wqk] + ([("A", kk, tg) for kk in range(16)] if k == 0 else []), writes=[bk])
                        x4 = ps[b][:, 0:256].rearrange("p (w u i) -> p w u i", w=2, u=2)
                        cosb = cs[:, 0, j, :].unsqueeze(1).to_broadcast([128, 2, 64])
                        sinb_ = cs[:, 1, j, :].unsqueeze(1).to_broadcast([128, 2, 64])
                        pp = j % 2
                        rot, rbf = rotL[pp], rbfL[pp]
                        t1, t2, t3, t4 = rtL[pp]
                        kr, kb = ("rot", pp), ("rbf", pp)
                        r4 = rot.rearrange("p w (u i) -> p w u i", u=2)
                        P.op("dve", (lambda: nc.vector.tensor_tensor(out=t1, in0=x4[:, :, 0, :], in1=cosb, op=ALU.mult)), reads=[bk, "cs"], writes=[("rt", pp, 1)])
                        P.op("dve", (lambda: nc.vector.tensor_tensor(out=t2, in0=x4[:, :, 1, :], in1=sinb_, op=ALU.mult)), reads=[bk, "cs"], writes=[("rt", pp, 2)])
                        P.op("dve", (lambda: nc.vector.tensor_tensor(out=t3, in0=x4[:, :, 0, :], in1=sinb_, op=ALU.mult)), reads=[bk, "cs"], writes=[("rt", pp, 3)])
                        P.op("dve", (lambda: nc.vector.tensor_tensor(out=t4, in0=x4[:, :, 1, :], in1=cosb, op=ALU.mult)), reads=[bk, "cs"], writes=[("rt", pp, 4)])
                        P.op("act", (lambda: nc.scalar.activation(out=vtm[:, j, :], in_=ps[b][:, 256:384], func=AF.Copy)), reads=[bk], writes=[("vtm", j)])
                        P.op("dve", (lambda: nc.vector.tensor_tensor(out=r4[:, :, 0, :], in0=t1, in1=t2, op=ALU.subtract)), reads=[("rt", pp, 1), ("rt", pp, 2)], writes=[kr])
                        P.op("dve", (lambda: nc.vector.tensor_tensor(out=r4[:, :, 1, :], in0=t3, in1=t4, op=ALU.add)), reads=[("rt", pp, 3), ("rt", pp, 4)], writes=[kr])
                        P.op("act", lambda: nc.scalar.activation(out=rbf[:, 0, :], in_=rot[:, 0, :], func=AF.Copy), reads=[kr], writes=[kb])
                        P.op("act", lambda: nc.scalar.activation(out=rbf[:, 1, :], in_=rot[:, 1, :], func=AF.Copy, scale=128.0 ** -0.5), reads=[kr], writes=[kb])
                        P.op("act", (lambda: nc.scalar.activation(out=ktm[:, j, :], in_=rot[:, 1, :], func=AF.Identity, scale=kdc[:, 1:2])), reads=[kr, "kdc2"], writes=[("ktm", j)])
                        bt, kt_ = bank("b")
                        pv = ps[bt][:].bitcast(BF16)
                        P.op("pe", (lambda: nc.tensor.transpose(pv[:, 0:128], rbf[:, 0, :], identb[:])), reads=[kb, "identb"], writes=[kt_])
                        P.op("pe", (lambda: nc.tensor.transpose(pv[:, 128:256], rbf[:, 1, :], identb[:])), reads=[kb, "identb"], writes=[kt_])
                        P.op("pe", (lambda: nc.tensor.matmul(ps[bt][:, 256:384], lhsT=ktm[:, j, :], rhs=vtm[:, j, :], start=True, stop=True)),
                             reads=[("ktm", j), ("vtm", j)], writes=[kt_])
                        P.op("act", (lambda: nc.scalar.activation(out=qf[:, j * 128:(j + 1) * 128], in_=pv[:, 0:128], func=AF.Copy)), reads=[kt_], writes=[("qf", j)])
                        P.op("act", (lambda: nc.scalar.activation(out=kf_[:, j * 128:(j + 1) * 128], in_=pv[:, 128:256], func=AF.Copy)), reads=[kt_], writes=[("kf", j)])
                        P.op("dve", (lambda: nc.vector.tensor_tensor(out=qd[:, j * 128:(j + 1) * 128], in0=pv[:, 0:128], in1=rdh[:, :], op=ALU.mult)),
                             reads=[kt_, "rdh"], writes=[("qd", j)])
                        P.op("dve", (lambda: nc.vector.scalar_tensor_tensor(out=Sr[:, :], in0=Sr[:, :], scalar=math.exp(lg * 128), in1=ps[bt][:, 256:384], op0=ALU.mult, op1=ALU.add)),
                             reads=[kt_, "Sr"], writes=["Sr"])
                        if j < NT - 1:
                            P.op("act", (lambda: nc.scalar.activation(out=slr[:, j + 1, :], in_=Sr[:, :], func=AF.Copy)), reads=["Sr"], writes=[("slr", j + 1)])
                        yield
                    exchange(li, 6 + h, Sr[:, :], "Sr", gatr[:, :], "gatr")
                    yield
                    P.op("dve", lambda: nc.vector.tensor_scalar(out=sinr[:, :], in0=gatr[:, :], scalar1=selc, scalar2=None, op0=ALU.mult), reads=["gatr", "cst"], writes=["sinr"])
                    pgate = proj_fm(C_RG + h * 128, "R", "b")
                    for tg in range(TG):
                        bs, ks = bank("b")
                        for jj in range(4):
                            j = tg * 4 + jj
                            P.op("pe", (lambda jj=jj, j=j: nc.tensor.matmul(ps[bs][:, jj * 128:(jj + 1) * 128], lhsT=kf_[:, j * 128:(j + 1) * 128], rhs=qf[:, j * 128:(j + 1) * 128], start=True, stop=True)),
                                 reads=[("kf", j), ("qf", j)], writes=[ks])
                        P.op("dve", lambda: nc.vector.tensor_tensor(out=pmr.rearrange("p (c i) -> p c i", i=128), in0=ps[bs][:].rearrange("p (c i) -> p c i", i=128),
                                                                     in1=dth[:, :].unsqueeze(1).to_broadcast([128, 4, 128]), op=ALU.mult),
                             reads=[ks, "dth"], writes=["pmr"])
                        bo, ko = bank("b")
                        for jj in range(4):
                            j = tg * 4 + jj
                            osl = ps[bo][:, jj * 128:(jj + 1) * 128]
                            P.op("dve", (lambda j=j, lg=lg: nc.vector.tensor_scalar(out=qgr[:, :], in0=qd[:, j * 128:(j + 1) * 128], scalar1=math.exp(lg * 128 * j), scalar2=None, op0=ALU.mult)),
                                 reads=[("qd", j)], writes=["qgr"])
                            P.op("pe", (lambda osl=osl, j=j, jj=jj: nc.tensor.matmul(osl, lhsT=vtm[:, j, :], rhs=pmr[:, jj * 128:(jj + 1) * 128], start=True, stop=False)),
                                 reads=[("vtm", j), "pmr"], writes=[ko])
                            P.op("pe", (lambda osl=osl, j=j: nc.tensor.matmul(osl, lhsT=slr[:, j, :], rhs=qd[:, j * 128:(j + 1) * 128], start=False, stop=False)),
                                 reads=[("slr", j), ("qd", j)], writes=[ko])
                            P.op("pe", (lambda osl=osl: nc.tensor.matmul(osl, lhsT=sinr[:, :], rhs=qgr[:, :], start=False, stop=True)), reads=["sinr", "qgr"], writes=[ko])
                        P.op("act", lambda: nc.scalar.activation(out=osb[:, :], in_=ps[bo][:], func=AF.Copy), reads=[ko], writes=["osb"])
                        P.op("act", lambda: nc.scalar.activation(out=rw[3], in_=ps[bo][:], func=AF.Square), reads=[ko], writes=["rw3"])
                        b1, k1 = bank("b")
                        b2, k2 = bank("b")
                        P.op("pe", lambda: nc.tensor.matmul(ps[b1][:], lhsT=ones, rhs=osb[:, :], start=True, stop=True), reads=["osb", "cst"], writes=[k1])
                        P.op("pe", lambda: nc.tensor.matmul(ps[b2][:], lhsT=ones, rhs=rw[3], start=True, stop=True), reads=["rw3", "cst"], writes=[k2])
                        mu, var = rw[1], rw[2]
                        P.op("act", lambda: nc.scalar.activation(out=mu, in_=ps[b1][:], func=AF.Copy, scale=1.0 / 128), reads=[k1], writes=["rw1"])
                        P.op("act", lambda: nc.scalar.activation(out=var, in_=ps[b1][:], func=AF.Square, scale=1.0 / 128), reads=[k1], writes=["rw2"])
                        P.op("dve", lambda: nc.vector.scalar_tensor_tensor(out=var, in0=ps[b2][:], scalar=1.0 / 128, in1=var, op0=ALU.mult, op1=ALU.subtract), reads=[k2, "rw2"], writes=["rw2"])
                        P.op("dve", lambda: nc.vector.tensor_scalar(out=var, in0=var, scalar1=0.0, scalar2=EPS, op0=ALU.max, op1=ALU.add), reads=["rw2"], writes=["rw2"])
                        P.op("act", lambda: nc.scalar.activation(out=var, in_=var, func=AF.Sqrt), reads=["rw2"], writes=["rw2"])
                        P.op("dve", lambda: nc.vector.reciprocal(out=var, in_=var), reads=["rw2"], writes=["rw2"])
                        P.op("dve", lambda: nc.vector.tensor_tensor(out=osb[:, :], in0=osb[:, :], in1=mu, op=ALU.subtract), reads=["osb", "rw1"], writes=["osb"])
                        P.op("dve", lambda: nc.vector.tensor_tensor(out=osb[:, :], in0=osb[:, :], in1=var, op=ALU.mult), reads=["osb", "rw2"], writes=["osb"])
                        P.op("act", (lambda h=h: nc.scalar.activation(out=osb[:, :], in_=osb[:, :], func=AF.Identity, scale=vhead(2, l, h), bias=vhead(3, l, h))),
                             reads=["osb", "vec"], writes=["osb"])
                        bg, kg = pgate(tg)
                        P.op("act", (lambda bg=bg: nc.scalar.activation(out=rw[0], in_=ps[bg][:], func=AF.Silu)), reads=[kg], writes=["rw0"])
                        mo, mk = mix_slot()
                        P.op("dve", (lambda: nc.vector.tensor_tensor(out=mo, in0=osb[:, :], in1=rw[0], op=ALU.mult)), reads=["osb", "rw0"], writes=[mk])
                        mix_flush(mo, mk, 10 + h, tg)
                        yield
            gens = [hgrn_gen(), ret_gen()]
            while gens:
                for g in list(gens):
                    try:
                        next(g)
                    except StopIteration:
                        gens.remove(g)
            P.barrier()
            R.reset(m_base)
            rt2 = [R.take([128, 16, 512], F32) for _ in range(2)]
            mixs = R.take([128, 16, 1024], BF16)
            wor = [R.take([128, 16, 256], BF16) for _ in range(2)]
            xr = [R.take([128, 512], F32) for _ in range(3)]
            so = [R.take([128, 512], F32) for _ in range(3)]
            n_w = 0
            n_s = 0
            n_x = 0
            for pp in range(2):
                P.dma("sp", (lambda: nc.sync.dma_start(out=mixs, in_=mixd[:, :, pp * 1024:(pp + 1) * 1024].rearrange("c p t -> p c t"))),
                      reads=[("mixd", cc_, pp * 2 + tl_) for cc_ in range(16) for tl_ in range(2)], writes=["mixs"])
                stbs = [stat_banks(), stat_banks()]
                for c2 in range(8):
                    s = n_w % 2
                    n_w += 1
                    P.dma("pool", (lambda: nc.gpsimd.dma_start(out=wor[s], in_=wol[:, :, c2 * 256:(c2 + 1) * 256])), writes=[("wor", s)])
                    for cc in range(2):
                        c = c2 * 2 + cc
                        for tl in range(2):
                            tg = pp * 2 + tl
                            xs = n_x % 3
                            n_x += 1
                            P.dma("sp", (lambda: nc.sync.dma_start(out=xr[xs], in_=xres[c][:, tg * 512:(tg + 1) * 512])), reads=[("xres", c, tg)], writes=[("xr", xs)])
                            b, bk = bank()
                            for k in range(16):
                                P.op("pe", (lambda: nc.tensor.matmul(ps[b][:], lhsT=wor[s][:, k, cc * 128:(cc + 1) * 128], rhs=mixs[:, k, tl * 512:(tl + 1) * 512],
                                                                     start=(k == 0), stop=(k == 15))),
                                     reads=[("wor", s), "mixs"], writes=[bk])
                            P.op("dve", (lambda: nc.vector.scalar_tensor_tensor(out=rt2[tl][:, c, :], in0=xr[xs], scalar=ALPHA, in1=ps[b][:], op0=ALU.mult, op1=ALU.add)),
                                 reads=[bk, ("xr", xs)], writes=[("rt", tl, c)])
                            ln_add(stbs[tl], c, rt2[tl][:, c, :], ("rt", tl, c))
                for tl in range(2):
                    tg = pp * 2 + tl

                    def emit_out(c, tg=tg, tl=tl):
                        nonlocal n_s
                        s = n_s % 3
                        n_s += 1
                        P.op("act", (lambda: nc.scalar.activation(out=A[:, c, tg * 512:(tg + 1) * 512], in_=rt2[tl][:, c, :], func=AF.Identity, scale=vln(0, l, c), bias=vln(1, l, c))),
                             reads=[("rt", tl, c), "vec"], writes=[("A", c, tg)])
                        P.op("act", (lambda: nc.scalar.activation(out=so[s], in_=rt2[tl][:, c, :], func=AF.Identity, scale=agcol(0, l, c), bias=agcol(1, l, c))),
                             reads=[("rt", tl, c), "agb"], writes=[("so", s)])
                        P.dma("sp", (lambda: nc.sync.dma_start(out=x1res[c][:, tg * 512:(tg + 1) * 512], in_=so[s])), reads=[("so", s)], writes=[("x1res", c, tg)])
                    ln_finish(stbs[tl], (lambda c, tl=tl: rt2[tl][:, c, :]), (lambda c, tl=tl: ("rt", tl, c)), emit_out)
            P.barrier()

            if stop_after == "O":
                P.emit()
                return nc
            R.reset()
            w1r = [R.take([128, 16, 256], BF16) for _ in range(3)]
            ft = [R.take([128, T], BF16) for _ in range(3)]
            rl = [R.take([128, 512], F32) for _ in range(3)]
            n_r = 0
            for pi in range(32):
                s = pi % 3
                P.dma("pool", (lambda s=s, pi=pi: nc.gpsimd.dma_start(out=w1r[s], in_=w1l[:, :, pi * 256:(pi + 1) * 256])), writes=[("w1r", s)])
                for mm in range(2):
                    m = pi * 2 + mm
                    fs = m % 3
                    for tg in range(TG):
                        b, bk = bank()
                        for k in range(16):
                            P.op("pe", (lambda b=b, s=s, k=k, mm=mm, tg=tg: nc.tensor.matmul(ps[b][:], lhsT=w1r[s][:, k, mm * 128:(mm + 1) * 128], rhs=A[:, k, tg * 512:(tg + 1) * 512],
                                                                                             start=(k == 0), stop=(k == 15))),
                                 reads=[("w1r", s)] + ([("A", kk, tg) for kk in range(16)] if k == 0 else []), writes=[bk])
                        rs = n_r % 3
                        n_r += 1
                        P.op("act", (lambda b=b, rs=rs: nc.scalar.activation(out=rl[rs], in_=ps[b][:], func=AF.Relu)), reads=[bk], writes=[("rl", rs)])
                        P.op("dve", (lambda b=b, rs=rs, fs=fs, tg=tg: nc.vector.tensor_tensor(out=ft[fs][:, tg * 512:(tg + 1) * 512], in0=rl[rs], in1=ps[b][:], op=ALU.mult)),
                             reads=[bk, ("rl", rs)], writes=[("ft", fs, tg)])
                    P.dma("sp", (lambda m=m, fs=fs: nc.sync.dma_start(out=fd[m], in_=ft[fs])), reads=[("ft", fs, tg) for tg in range(TG)], writes=[("fd", m)])
            P.barrier()

            if stop_after == "F1":
                P.emit()
                return nc
            R.reset()
            acc = R.take([128, 16, T], F32)
            Ab = A[:].rearrange("p c t -> p (c t)")
            fr = [Ab[:, i * 8192:(i + 1) * 8192].rearrange("p (k t) -> p k t", k=4) for i in range(2)]
            w2r = [Ab[:, 16384 + i * 8192:16384 + (i + 1) * 8192].rearrange("p (k t) -> p k t", k=4) for i in range(2)]
            for c in range(16):
                P.dma("sp", (lambda c=c: nc.sync.dma_start(out=acc[:, c, :], in_=x1res[c])), reads=[("x1res", c, tg) for tg in range(TG)], writes=[("acc", c, tg) for tg in range(TG)])
            def f2_group(s, c, tg):
                b, bk = bank()
                for kk in range(4):
                    P.op("pe", (lambda b=b, s=s, kk=kk, c=c, tg=tg: nc.tensor.matmul(ps[b][:], lhsT=w2r[s][:, kk, c * 128:(c + 1) * 128], rhs=fr[s][:, kk, tg * 512:(tg + 1) * 512],
                                                                                     start=(kk == 0), stop=(kk == 3))),
                         reads=[("w2r", s), ("fr", s)], writes=[bk])
                P.op("dve", (lambda b=b, c=c, tg=tg: nc.vector.tensor_tensor(out=acc[:, c, tg * 512:(tg + 1) * 512], in0=acc[:, c, tg * 512:(tg + 1) * 512], in1=ps[b][:], op=ALU.add)),
                     reads=[bk, ("acc", c, tg)], writes=[("acc", c, tg)])

            for jb in range(16):
                s = jb % 2
                P.dma("sp", (lambda s=s, jb=jb: nc.sync.dma_start(out=fr[s], in_=fd[jb * 4:(jb + 1) * 4].rearrange("k p t -> p k t"))),
                      reads=[("fd", jb * 4 + kk) for kk in range(4)], writes=[("fr", s)])
                P.dma("pool", (lambda s=s, jb=jb: nc.gpsimd.dma_start(out=w2r[s], in_=w2l[jb * 512:(jb + 1) * 512, :].rearrange("(k p) n -> p k n", p=128))), writes=[("w2r", s)])
                if jb < 15:
                    for c in range(16):
                        for tg in range(TG):
                            f2_group(s, c, tg)
                else:
                    for tg in range(TG):
                        stb = stat_banks()
                        for c in range(16):
                            f2_group(s, c, tg)
                            ln_add(stb, c, acc[:, c, tg * 512:(tg + 1) * 512], ("acc", c, tg))

                        def emit_out2(c, tg=tg):
                            accs = acc[:, c, tg * 512:(tg + 1) * 512]
                            P.op("act", (lambda c=c, tg=tg, accs=accs: nc.scalar.activation(out=accs, in_=accs, func=AF.Identity, scale=vln(2, l, c), bias=vln(3, l, c))),
                                 reads=[("acc", c, tg), "vec"], writes=[("acc", c, tg)])
                            if not (last and final_out):
                                P.dma("sp", (lambda c=c, tg=tg, accs=accs: nc.sync.dma_start(out=xres[c][:, tg * 512:(tg + 1) * 512], in_=accs)),
                                      reads=[("acc", c, tg)], writes=[("xres", c, tg)])
                        ln_finish(stb, lambda c, tg=tg: acc[:, c, tg * 512:(tg + 1) * 512], lambda c, tg=tg: ("acc", c, tg), emit_out2)
            P.barrier()
            if not (last and final_out):
                for tg in range(TG):
                    for c in range(16):
                        eng = "dve" if (c % 2 == 0) else "act"
                        if eng == "dve":
                            P.op("dve", (lambda c=c, tg=tg: nc.vector.tensor_copy(out=A[:, c, tg * 512:(tg + 1) * 512], in_=acc[:, c, tg * 512:(tg + 1) * 512])),
                                 reads=[("acc", c, tg)], writes=[("A", c, tg)])
                        else:
                            P.op("act", (lambda c=c, tg=tg: nc.scalar.activation(out=A[:, c, tg * 512:(tg + 1) * 512], in_=acc[:, c, tg * 512:(tg + 1) * 512], func=AF.Copy)),
                                 reads=[("acc", c, tg)], writes=[("A", c, tg)])
            P.barrier()
            if last and final_out:
                Af = A[:].rearrange("p c t -> p (c t)").bitcast(F32)
                ot = [Af[:, i * 2048:(i + 1) * 2048] for i in range(4)]
                n = 0
                for j in range(NT):
                    s = n % 4
                    n += 1
                    for c4 in range(4):
                        b, bk = bank()
                        for cc in range(4):
                            c = c4 * 4 + cc
                            P.op("pe", (lambda b=b, cc=cc, c=c, j=j: nc.tensor.transpose(ps[b][:, cc * 128:(cc + 1) * 128], acc[:, c, j * 128:(j + 1) * 128], ident)),
                                 reads=[("acc", c, j // 4), "cst"], writes=[bk])
                        P.op("act" if c4 % 2 else "dve",
                             (lambda b=b, s=s, c4=c4: (nc.scalar.activation(out=ot[s][:, c4 * 512:(c4 + 1) * 512], in_=ps[b][:], func=AF.Copy) if c4 % 2
                                                        else nc.vector.tensor_copy(out=ot[s][:, c4 * 512:(c4 + 1) * 512], in_=ps[b][:]))),
                             reads=[bk], writes=[("ot", s, c4)])
                    out_ops.append(P.dma("sp", (lambda j=j, s=s: nc.sync.dma_start(out=y_out[j * 128:(j + 1) * 128, :], in_=ot[s])),
                                         reads=[("ot", s, c4) for c4 in range(4)], writes=[("yout", j)]))
        if not final_out:
            pass
        P.emit(final_wait_ops=out_ops)
    return nc


def _host_inputs(inputs):
    f32 = np.float32
    x = np.asarray(inputs["x"], f32)
    pos = np.asarray(inputs["positions"]).astype(np.int32)
    Lh = L_ALL

    def cols(v, nchunk):
        v = np.asarray(v, f32).reshape(Lh, nchunk, 128)
        return np.ascontiguousarray(v.transpose(2, 0, 1).reshape(128, Lh * nchunk))
    parts = [cols(inputs["ln1_g"], 16), cols(inputs["ln1_b"], 16), cols(inputs["ln2_g"], 16), cols(inputs["ln2_b"], 16)]
    wdw = np.asarray(inputs["w_dw"], f32).reshape(Lh, CW, 4, 128)
    conv = np.concatenate([wdw.transpose(3, 0, 2, 1),
                           np.asarray(inputs["b_dw"], f32).reshape(Lh, 4, 128).transpose(2, 0, 1)[..., None],
                           np.asarray(inputs["conv_ln_g"], f32).reshape(Lh, 4, 128).transpose(2, 0, 1)[..., None],
                           np.asarray(inputs["conv_ln_b"], f32).reshape(Lh, 4, 128).transpose(2, 0, 1)[..., None]], axis=3)
    parts.append(np.ascontiguousarray(conv.reshape(128, Lh * 4 * (CW + 3))))
    parts += [cols(inputs["hgrn_lb"], 6), cols(inputs["hgrn_norm_g"], 6), cols(inputs["ret_gn_g"], 6), cols(inputs["ret_gn_b"], 6)]
    vec = np.ascontiguousarray(np.concatenate(parts, axis=1))
    p = np.arange(128)
    ident = np.eye(128, dtype=f32)
    ones = np.ones((128, 128), f32)
    tri = (p[None, :] >= p[:, None]).astype(f32)
    tms = np.maximum(p[None, :] - p[:, None], 0).astype(f32)
    iot1 = np.broadcast_to((p + 1).astype(f32)[None, :], (128, 128))
    half = 64
    invf = (np.float32(10000.0) ** (-(np.arange(half, dtype=f32)) / np.float32(half))).astype(f32)
    invf = np.broadcast_to(invf[None, :], (128, 64))
    colr = (127 - p).astype(f32)[:, None]
    per_core = []
    for c in range(8):
        b, hf = c // 2, c % 2
        sel = np.full((128, 1), float(hf), f32)
        const = np.ascontiguousarray(np.concatenate([ident, ones, tri, tms, iot1, invf, colr, sel], axis=1).astype(f32))
        xc = np.ascontiguousarray(x[b, hf * T:(hf + 1) * T, :])
        pc = np.ascontiguousarray(pos[b, hf * T:(hf + 1) * T].reshape(NT, 128).T)
        per_core.append(dict(x_in=xc, pos_in=pc, vec_in=vec, const_in=const))
    return per_core


_NC_CACHE = {}


def kernel(**inputs):
    per_core = _host_inputs(inputs)
    w = {k: np.ascontiguousarray(np.asarray(inputs[k], np.float32)) for k in ("w_in", "w_out", "w_ff1", "w_ff2")}
    if "nc" not in _NC_CACHE:
        _NC_CACHE["nc"] = build_nc(L_ALL)
    nc = _NC_CACHE["nc"]
    in_maps = []
    for c in range(8):
        m = dict(per_core[c])
        m.update(w)
        in_maps.append(m)
    res = run_bass_kernel_spmd(nc, in_maps, core_ids=list(range(8)))
    out = np.empty((4, 4096, D), np.float32)
    for c in range(8):
        out[c // 2, (c % 2) * T:(c % 2 + 1) * T, :] = res.results[c]["y_out"]
    return out
```

```python
import contextlib
import types
import math
import numpy as np
import ml_dtypes
import concourse.bass as bass
import concourse.mybir as mybir
from concourse.bass_utils import run_bass_kernel_spmd

F32 = mybir.dt.float32
BF16 = mybir.dt.bfloat16
I32 = mybir.dt.int32
U8 = mybir.dt.uint8
AF = mybir.ActivationFunctionType
ALU = mybir.AluOpType

D = 2048
T = 2048
NT = 16
TG = 4
DIN = 7168
DFF = 8192
L_ALL = 4
EPS = 1e-5
ALPHA = (2.0 * L_ALL) ** 0.25
CW = 31
HALO = CW - 1
NH = 6
LOGG = [math.log1p(-(2.0 ** (-5.0 - h))) for h in range(NH)]
C_CA, C_CG, C_HQ, C_HF, C_HI, C_HG, C_RQ, C_RK, C_RV, C_RG = 0, 512, 1024, 1792, 2560, 3328, 4096, 4864, 5632, 6400


import os
KVAR = os.environ.get('KVAR', '')


class Prog:
    ENG = ("pe", "act", "dve", "pool", "sp")

    def __init__(self, nc, n_dma_sems=10):
        self.nc = nc
        self.ops = []
        self.last_write = {}
        self.readers = {}
        self.nd = n_dma_sems
        self.pending_bar = {}

    @staticmethod
    def _freeze(fn):
        if fn.__closure__ is None:
            return fn
        cells = []
        for c in fn.__closure__:
            try:
                cells.append(types.CellType(c.cell_contents))
            except ValueError:
                cells.append(c)
        return types.FunctionType(fn.__code__, fn.__globals__, fn.__name__, fn.__defaults__, tuple(cells))

    def op(self, eng, fn, reads=(), writes=(), kind="c"):
        fn = self._freeze(fn)
        deps = set()
        for k in reads:
            w = self.last_write.get(k)
            if w is not None:
                deps.add(w)
            if isinstance(k, tuple) and k[0] == "ps":
                for r in self.readers.get(k, ()):
                    if self.ops[r]["eng"] != eng:
                        deps.add(r)
        for k in writes:
            w = self.last_write.get(k)
            if w is not None:
                deps.add(w)
            deps.update(self.readers.get(k, ()))
        if eng in self.pending_bar:
            deps.update(self.pending_bar.pop(eng))
        idx = len(self.ops)
        self.ops.append(dict(eng=eng, fn=fn, deps=deps, kind=kind, sig=False))
        for k in reads:
            self.readers.setdefault(k, []).append(idx)
        for k in writes:
            self.last_write[k] = idx
            self.readers[k] = []
        return idx

    def dma(self, eng, fn, reads=(), writes=()):
        return self.op(eng, fn, reads, writes, kind="d")

    def cc(self, fn, reads=(), writes=()):
        return self.op("pool", fn, reads, writes, kind="cc")

    def barrier(self):
        last = {}
        asyncs = []
        for i, o in enumerate(self.ops):
            if o["kind"] == "c":
                last[o["eng"]] = i
            else:
                asyncs.append(i)
        start = getattr(self, "_bar_from", 0)
        dep = set(last.values()) | set(i for i in asyncs if i >= start)
        self._bar_from = len(self.ops)
        for e in self.ENG:
            self.pending_bar[e] = set(dep) | self.pending_bar.get(e, set())

    def emit(self, final_wait_ops=()):
        nc = self.nc
        ops = self.ops

        def skip(o, od):
            return od["kind"] == "c" and o["kind"] == "c" and od["eng"] == o["eng"] == "pe"

        for o in ops:
            for d in o["deps"]:
                od = ops[d]
                if od["kind"] == "c" and not skip(o, od):
                    od["sig"] = True
        with contextlib.ExitStack() as st:
            csem = {e: st.enter_context(nc.semaphore("c_" + e)) for e in self.ENG}
            qs = ("sp", "pool", "cc")
            dsem = {q: [st.enter_context(nc.semaphore("d_%s_%d" % (q, j))) for j in range(self.nd)] for q in qs}
            ccount = {e: 0 for e in self.ENG}
            dcount = {q: 0 for q in qs}
            for o in ops:
                if o["kind"] != "c":
                    q = "cc" if o["kind"] == "cc" else o["eng"]
                    inc = 1 if o["kind"] == "cc" else 16
                    n = dcount[q]
                    dcount[q] += 1
                    o["sem"] = dsem[q][n % self.nd]
                    o["val"] = inc * (n // self.nd + 1)
                    o["inc"] = inc
                    o["n"] = n
                elif o["sig"]:
                    ccount[o["eng"]] += 1
                    o["sem"] = csem[o["eng"]]
                    o["val"] = ccount[o["eng"]]
            per_eng = {e: [] for e in self.ENG}
            for i, o in enumerate(ops):
                per_eng[o["eng"]].append(i)
            self.stats = dict(ccount)

            def run(e, handle):
                waited = {}
                for i in per_eng[e]:
                    o = ops[i]
                    need = {}

                    def want(s, v):
                        k = id(s)
                        if need.get(k, (None, 0))[1] < v:
                            need[k] = (s, v)
                    for d in o["deps"]:
                        od = ops[d]
                        if skip(o, od):
                            continue
                        want(od["sem"], od["val"])
                    if o["kind"] != "c" and o["n"] >= self.nd:
                        want(o["sem"], o["val"] - o["inc"])
                    for k, (s, v) in need.items():
                        if waited.get(k, 0) >= v:
                            continue
                        handle.wait_ge(s, v)
                        waited[k] = v
                    inst = o["fn"]()
                    if o["kind"] != "c":
                        inst.then_inc(o["sem"], o["inc"])
                    elif o["sig"]:
                        inst.then_inc(o["sem"], 1)
                if e == "sp":
                    for i in final_wait_ops:
                        o = ops[i]
                        handle.wait_ge(o["sem"], o["val"])

            with nc.Block() as block:
                @block.tensor
                def _(h):
                    run("pe", h)

                @block.scalar
                def _(h):
                    run("act", h)

                @block.vector
                def _(h):
                    run("dve", h)

                @block.gpsimd
                def _(h):
                    run("pool", h)

                @block.sync
                def _(h):
                    run("sp", h)


class Carver:
    def __init__(self, big, nbytes):
        self.big = big
        self.n = nbytes
        self.off = 0

    def reset(self, off=0):
        self.off = off

    def take(self, shape, dt, parts=128):
        esz = {F32: 4, BF16: 2, I32: 4}[dt]
        n = esz
        for s in shape[1:]:
            n *= s
        off = (self.off + 31) // 32 * 32
        assert off + n <= self.n, ("SBUF carve overflow", off, n, self.n)
        self.off = off + n
        v = self.big[0:parts, off:off + n].bitcast(dt)
        if len(shape) > 2:
            names = " ".join("a%d" % i for i in range(len(shape) - 1))
            kw = {"a%d" % i: shape[i + 1] for i in range(len(shape) - 1)}
            v = v.rearrange("p (%s) -> p %s" % (names, names), **kw)
        return v


def build_nc(n_layers, layer0=0, x_fm_in=False, final_out=True, stop_after=None, wlayers=L_ALL, wrows=None):
    nc = bass.Bass("TRN2", target_bir_lowering=False)
    L = n_layers
    dr = lambda name, shape, dt, kind=None: (nc.dram_tensor(name, shape, dt, kind=kind) if kind else nc.dram_tensor(name, shape, dt))
    x_in = dr("x_in", [T, D], F32, "ExternalInput").ap()
    pos_in = dr("pos_in", [128, NT], I32, "ExternalInput").ap()
    w_in = dr("w_in", [wlayers, wrows or D, DIN], F32, "ExternalInput").ap()
    w_out = dr("w_out", [wlayers, wrows or D, D], F32, "ExternalInput").ap()
    w_ff1 = dr("w_ff1", [wlayers, wrows or D, DFF], F32, "ExternalInput").ap()
    w_ff2 = dr("w_ff2", [wlayers, wrows or DFF, D], F32, "ExternalInput").ap()
    NV = 4 * L_ALL * 16 + L_ALL * 4 * (CW + 3) + L_ALL * 6 * 4
    vec_in = dr("vec_in", [128, NV], F32, "ExternalInput").ap()
    NCONST = 128 * 5 + 64 + 2
    const_in = dr("const_in", [128, NCONST], F32, "ExternalInput").ap()
    y_out = dr("y_out", [T, D], F32, "ExternalOutput").ap()
    xres = dr("xres", [16, 128, T], F32).ap()
    x1res = dr("x1res", [16, 128, T], F32).ap()
    fd = dr("fd", [64, 128, T], BF16).ap()
    mixd = dr("mixd", [16, 128, T], BF16).ap()
    NX = 13
    cc_i = [[dr("cci_%d_%d" % (l, u), [128, 128], F32) for u in range(NX)] for l in range(L)]
    cc_o = [[dr("cco_%d_%d" % (l, u), [256, 128], F32) for u in range(NX)] for l in range(L)]

    P = Prog(nc)
    with contextlib.ExitStack() as st:
        A = st.enter_context(nc.sbuf_tensor("A", [128, 16, T], BF16))
        RB = 128 * 1024
        Rbig = st.enter_context(nc.sbuf_tensor("Rbig", [128, RB], U8))
        R = Carver(Rbig, RB)
        Abig_view = None
        cst = st.enter_context(nc.sbuf_tensor("cst", [128, NCONST], F32))
        vec = st.enter_context(nc.sbuf_tensor("vec", [128, NV], F32))
        identb = st.enter_context(nc.sbuf_tensor("identb", [128, 128], BF16))
        cs = st.enter_context(nc.sbuf_tensor("cs", [128, 2, NT, 64], BF16))
        sm = st.enter_context(nc.sbuf_tensor("sm", [128, 512], F32))
        lnsq = st.enter_context(nc.sbuf_tensor("lnsq", [128, 512], F32))
        ps = [st.enter_context(nc.psum_tensor("ps%d" % i, [128, 512], F32)) for i in range(8)]
        bank_ctr = [0]

        def bank(pool="a"):
            i = 0 if pool == "a" else 1
            if len(bank_ctr) < 2:
                bank_ctr.append(0)
            b = 4 * i + bank_ctr[i] % 4
            bank_ctr[i] += 1
            return b, ("ps", b)
        stat_ctr = [0]

        def stat_banks():
            p = stat_ctr[0] % 2
            stat_ctr[0] += 1
            return (4 + 2 * p, ("ps", 4 + 2 * p)), (5 + 2 * p, ("ps", 5 + 2 * p))

        ident = cst[:, 0:128]
        ones = cst[:, 128:256]
        tri = cst[:, 256:384]
        tms = cst[:, 384:512]
        iot1 = cst[:, 512:640]
        invf = cst[:, 640:704]
        colr = cst[:, 704:705]
        selc = cst[:, 705:706]

        def vln(which, l, c):
            o = (which * L_ALL + l) * 16 + c
            return vec[:, o:o + 1]
        VB = 4 * L_ALL * 16

        def vconv(l, cc, j):
            o = VB + (l * 4 + cc) * (CW + 3) + j
            return vec[:, o:o + 1]
        VH = VB + L_ALL * 4 * (CW + 3)

        def vhead(which, l, h):
            o = VH + (which * L_ALL + l) * 6 + h
            return vec[:, o:o + 1]

        lbt = sm[:, 0:24]
        oml = sm[:, 24:48]
        agb = sm[:, 48:176]
        smx = sm[:, 176:512]

        def agcol(which, l, c):
            o = 48 + (which * L_ALL + l) * 16 + c
            return sm[:, o:o + 1]

        P.dma("sp", lambda: nc.sync.dma_start(out=cst[:], in_=const_in), writes=["cst"])
        P.dma("sp", lambda: nc.sync.dma_start(out=vec[:], in_=vec_in), writes=["vec"])
        P.op("act", lambda: nc.scalar.activation(out=identb[:], in_=ident, func=AF.Copy), reads=["cst"], writes=["identb"])
        P.op("act", lambda: nc.scalar.activation(out=sm[:, 48:176], in_=vec[:, 0:2 * L_ALL * 16], func=AF.Copy, scale=ALPHA),
             reads=["vec"], writes=["agb"])
        lbr = vec[:, VH:VH + 24].rearrange("p (l h) -> p h l", l=L_ALL)
        e4 = smx[:, 0:24].rearrange("p (h l) -> p h l", l=L_ALL)
        mx = smx[:, 24:30]
        P.op("dve", lambda: nc.vector.tensor_reduce(out=mx, in_=lbr, axis=mybir.AxisListType.X, op=ALU.max), reads=["vec"], writes=["mx"])
        P.op("dve", lambda: nc.vector.tensor_tensor(out=e4, in0=lbr, in1=mx.unsqueeze(2).to_broadcast([128, 6, L_ALL]), op=ALU.subtract),
             reads=["mx", "vec"], writes=["e4"])
        P.op("act", lambda: nc.scalar.activation(out=e4, in_=e4, func=AF.Exp), reads=["e4"], writes=["e4"])
        sm6 = smx[:, 30:36]
        P.op("dve", lambda: nc.vector.tensor_reduce(out=sm6, in_=e4, axis=mybir.AxisListType.X, op=ALU.add), reads=["e4"], writes=["sm6"])
        P.op("dve", lambda: nc.vector.reciprocal(out=sm6, in_=sm6), reads=["sm6"], writes=["sm6"])
        P.op("dve", lambda: nc.vector.tensor_tensor(out=e4, in0=e4, in1=sm6.unsqueeze(2).to_broadcast([128, 6, L_ALL]), op=ALU.mult),
             reads=["sm6", "e4"], writes=["e4"])
        lbv = lbt.rearrange("p (l h) -> p h l", l=L_ALL)
        P.op("dve", lambda: nc.vector.memset(lbv[:, :, 0:1], 0.0), writes=["lbt"])
        for l in range(1, L_ALL):
            P.op("dve", (lambda l=l: nc.vector.tensor_tensor(out=lbv[:, :, l:l + 1], in0=lbv[:, :, l - 1:l], in1=e4[:, :, l:l + 1], op=ALU.add)),
                 reads=["e4", "lbt"], writes=["lbt"])
        P.op("dve", lambda: nc.vector.tensor_scalar(out=lbt, in0=lbt, scalar1=0.0, scalar2=1.0 - 1e-6, op0=ALU.max, op1=ALU.min),
             reads=["lbt"], writes=["lbt"])
        P.op("dve", lambda: nc.vector.tensor_scalar(out=oml, in0=lbt, scalar1=-1.0, scalar2=1.0, op0=ALU.mult, op1=ALU.add),
             reads=["lbt"], writes=["oml"])
        if stop_after == "s0":
            P.emit()
            return nc
        posf = smx[:, 40:56]
        P.dma("sp", lambda: nc.sync.dma_start(out=smx[:, 56:72].bitcast(I32), in_=pos_in), writes=["posi"])
        P.op("dve", lambda: nc.vector.tensor_copy(out=posf, in_=smx[:, 56:72].bitcast(I32)), reads=["posi"], writes=["posf"])
        R.reset()
        ang = R.take([128, NT, 64], F32)
        tq = R.take([128, NT, 64], F32)
        ti = R.take([128, NT, 64], I32)
        TWO_PI = 2.0 * math.pi
        for j in range(NT):
            P.op("dve", (lambda j=j: nc.vector.tensor_scalar(out=ang[:, j, :], in0=invf, scalar1=posf[:, j:j + 1], scalar2=None, op0=ALU.mult)),
                 reads=["posf", "cst"], writes=["ang"])
        for which in (1, 0):
            shift = 0.0 if which == 1 else math.pi / 2
            P.op("dve", (lambda s=shift: nc.vector.tensor_scalar(out=tq, in0=ang, scalar1=s, scalar2=1.0 / TWO_PI, op0=ALU.add, op1=ALU.mult)),
                 reads=["ang"], writes=["tq"])
            P.op("dve", lambda: nc.vector.tensor_copy(out=ti, in_=tq), reads=["tq"], writes=["ti"])
            P.op("dve", lambda: nc.vector.tensor_copy(out=tq, in_=ti), reads=["ti"], writes=["tq"])
            P.op("dve", lambda: nc.vector.tensor_scalar(out=tq, in0=tq, scalar1=-TWO_PI, scalar2=None, op0=ALU.mult), reads=["tq"], writes=["tq"])
            P.op("dve", (lambda s=shift: nc.vector.scalar_tensor_tensor(out=tq, in0=ang, scalar=s, in1=tq, op0=ALU.add, op1=ALU.add)),
                 reads=["ang", "tq"], writes=["tq"])
            tf = ti.bitcast(F32)
            P.op("dve", lambda: nc.vector.tensor_scalar(out=tf, in0=tq, scalar1=math.pi, scalar2=-TWO_PI, op0=ALU.is_gt, op1=ALU.mult),
                 reads=["tq"], writes=["ti"])
            P.op("dve", lambda: nc.vector.tensor_tensor(out=tq, in0=tq, in1=tf, op=ALU.add), reads=["tq", "ti"], writes=["tq"])
            P.op("dve", lambda: nc.vector.tensor_scalar(out=tf, in0=tq, scalar1=-math.pi, scalar2=TWO_PI, op0=ALU.is_lt, op1=ALU.mult),
                 reads=["tq"], writes=["ti"])
            P.op("dve", lambda: nc.vector.tensor_tensor(out=tq, in0=tq, in1=tf, op=ALU.add), reads=["tq", "ti"], writes=["tq"])
            P.op("dve", lambda: nc.vector.tensor_scalar(out=tq, in0=tq, scalar1=-math.pi, scalar2=math.pi, op0=ALU.max, op1=ALU.min),
                 reads=["tq"], writes=["tq"])
            P.op("act", (lambda w=which: nc.scalar.activation(out=cs[:, w, :, :], in_=tq, func=AF.Sin)), reads=["tq"], writes=["cs"])
        P.barrier()

        if stop_after == "s1":
            P.emit()
            return nc
        def load_x():
            R.reset()
            xt = [R.take([128, D], F32) for _ in range(4)]
            stg = [R.take([128, 512], F32) for _ in range(4)]
            n = 0
            for tg in range(TG):
                for jj in range(4):
                    j = tg * 4 + jj
                    if KVAR != "nodma":
                        P.dma("sp", (lambda j=j, jj=jj: nc.sync.dma_start(out=xt[jj], in_=x_in[j * 128:(j + 1) * 128, :])), writes=[("xt", jj)])
                for c in range(16):
                    b, bk = bank()
                    for jj in range(4):
                        if KVAR == "notr":
                            continue
                        P.op("pe", (lambda b=b, jj=jj, c=c: nc.tensor.transpose(ps[b][:, jj * 128:(jj + 1) * 128], xt[jj][:, c * 128:(c + 1) * 128], ident)),
                             reads=[("xt", jj), "cst"], writes=[bk])
                    P.op("act", (lambda b=b, c=c, tg=tg: nc.scalar.activation(out=A[:, c, tg * 512:(tg + 1) * 512], in_=ps[b][:], func=AF.Copy)),
                         reads=[bk], writes=[("A", c, tg)])
                    s = n % 4
                    n += 1
                    P.op("dve", (lambda b=b, s=s: nc.vector.tensor_copy(out=stg[s], in_=ps[b][:])), reads=[bk], writes=[("stg", s)])
                    if KVAR != "noxres":
                        P.dma("sp", (lambda s=s, c=c, tg=tg: nc.sync.dma_start(out=xres[c][:, tg * 512:(tg + 1) * 512], in_=stg[s])),
                          reads=[("stg", s)], writes=[("xres", c, tg)])
            P.barrier()

        load_x()
        if stop_after == "setup":
            P.emit()
            return nc

        def ln_add(stb, c, r_ap, rkey):
            (b1, k1), (b2, k2) = stb
            P.op("act", (lambda: nc.scalar.activation(out=lnsq[:], in_=r_ap, func=AF.Square)), reads=[rkey], writes=["lnsq"])
            P.op("pe", (lambda: nc.tensor.matmul(ps[b1][:], lhsT=ones, rhs=r_ap, start=(c == 0), stop=(c == 15))), reads=[rkey, "cst"], writes=[k1])
            P.op("pe", (lambda: nc.tensor.matmul(ps[b2][:], lhsT=ones, rhs=lnsq[:], start=(c == 0), stop=(c == 15))), reads=["lnsq", "cst"], writes=[k2])

        def ln_finish(stb, r_of, rkeys, emit_out):
            (b1, k1), (b2, k2) = stb
            P.op("act", lambda: nc.scalar.activation(out=ps[b1][:], in_=ps[b1][:], func=AF.Copy, scale=1.0 / D), reads=[k1], writes=[k1])
            P.op("act", lambda: nc.scalar.activation(out=lnsq[:], in_=ps[b1][:], func=AF.Square), reads=[k1], writes=["lnsq"])
            P.op("dve", lambda: nc.vector.scalar_tensor_tensor(out=ps[b2][:], in0=ps[b2][:], scalar=1.0 / D, in1=lnsq[:], op0=ALU.mult, op1=ALU.subtract),
                 reads=[k2, "lnsq"], writes=[k2])
            P.op("dve", lambda: nc.vector.tensor_scalar(out=ps[b2][:], in0=ps[b2][:], scalar1=0.0, scalar2=EPS, op0=ALU.max, op1=ALU.add), reads=[k2], writes=[k2])
            P.op("act", lambda: nc.scalar.activation(out=ps[b2][:], in_=ps[b2][:], func=AF.Sqrt), reads=[k2], writes=[k2])
            P.op("dve", lambda: nc.vector.reciprocal(out=ps[b2][:], in_=ps[b2][:]), reads=[k2], writes=[k2])
            for c in range(16):
                P.op("dve", (lambda c=c: nc.vector.tensor_tensor(out=r_of(c), in0=r_of(c), in1=ps[b1][:], op=ALU.subtract)),
                     reads=[rkeys(c), k1], writes=[rkeys(c)])
                P.op("dve", (lambda c=c: nc.vector.tensor_tensor(out=r_of(c), in0=r_of(c), in1=ps[b2][:], op=ALU.mult)),
                     reads=[rkeys(c), k2], writes=[rkeys(c)])
                emit_out(c)

        groups = [[0, 1], [2, 3], [4, 5], [6, 7]]

        def exchange(l, u, src, srckey, gat, gatkey, n=128):
            ci, co = cc_i[l][u], cc_o[l][u]
            P.dma("sp", lambda: nc.sync.dma_start(out=ci[:, 0:n], in_=src), reads=[srckey], writes=[("cci", l, u)])
            P.cc(lambda: nc.gpsimd.collective_compute("AllGather", ALU.bypass, replica_groups=groups,
                                                      ins=[ci.ap().opt()], outs=[co.ap().opt()]),
                 reads=[("cci", l, u)], writes=[("cco", l, u)])
            P.dma("sp", lambda: nc.sync.dma_start(out=gat, in_=co[0:128, 0:n]), reads=[("cco", l, u)], writes=[gatkey])

        out_ops = []
        for li in range(L):
            l = layer0 + li
            last = (li == L - 1)
            wl = w_in[l].rearrange("(k p) n -> p k n", p=128)
            wol = w_out[l].rearrange("(k p) n -> p k n", p=128)
            w1l = w_ff1[l].rearrange("(k p) n -> p k n", p=128)
            w2l = w_ff2[l]

            R.reset()
            m_base = R.off
            NWS = 4
            wrings = {"H": [R.take([128, 16, 128], BF16) for _ in range(NWS)], "R": [R.take([128, 16, 128], BF16) for _ in range(NWS)]}
            wctr = {"H": 0, "R": 0}
            mstg = [R.take([128, 512], BF16) for _ in range(3)]
            mctr = [0]

            def mix_slot():
                s = mctr[0] % 3
                mctr[0] += 1
                return mstg[s], ("mstg", s)

            def mix_flush(ap, key, chunk, tg):
                P.dma("sp", (lambda: nc.sync.dma_start(out=mixd[chunk][:, tg * 512:(tg + 1) * 512], in_=ap)), reads=[key], writes=[("mixd", chunk, tg)])

            def wpiece(col0, ring="H"):
                s = wctr[ring] % NWS
                wctr[ring] += 1
                wt = wrings[ring][s]
                P.dma("pool", (lambda: nc.gpsimd.dma_start(out=wt, in_=wl[:, :, col0:col0 + 128])), writes=[("wr", ring, s)])
                return wt, ("wr", ring, s)

            def proj_fm(col0, ring="H", pool="a"):
                w, wk = wpiece(col0, ring)

                def get(tg):
                    b, bk = bank(pool)
                    for k in range(16):
                        P.op("pe", (lambda b=b, k=k, tg=tg, w=w: nc.tensor.matmul(ps[b][:], lhsT=w[:, k, :], rhs=A[:, k, tg * 512:(tg + 1) * 512],
                                                                                   start=(k == 0), stop=(k == 15))),
                             reads=[wk] + ([("A", kk, tg) for kk in range(16)] if k == 0 else []), writes=[bk])
                    return (b, bk)
                return get

            tw = [R.take([128, 512], F32) for _ in range(6)]
            m_mid = R.off
            ub = R.take([128, 4, HALO + T], BF16)
            usq = R.take([128, 4, 512], F32)
            yc = R.take([128, 4, 512], F32)
            dg = [R.take([128, 128], BF16) for _ in range(4)]
            gat = R.take([128, 128], F32)
            tail = R.take([128, 128], F32)
            P.op("dve", lambda: nc.vector.memset(tail[:, :], 0.0), writes=["tail"])
            for cc in range(4):
                pa = proj_fm(C_CA + cc * 128)
                pg = proj_fm(C_CG + cc * 128)
                for tg in range(TG):
                    (ba, ka), (bg, kg) = pa(tg), pg(tg)
                    P.op("act", (lambda bg=bg: nc.scalar.activation(out=tw[0], in_=ps[bg][:], func=AF.Sigmoid)), reads=[kg], writes=["tw0"])
                    P.op("dve", (lambda ba=ba, cc=cc, tg=tg: nc.vector.tensor_tensor(out=ub[:, cc, HALO + tg * 512:HALO + (tg + 1) * 512], in0=ps[ba][:], in1=tw[0], op=ALU.mult)),
                         reads=[ka, "tw0"], writes=[("ub", cc, tg)])
                P.op("act", (lambda cc=cc: nc.scalar.activation(out=tail[:, cc * 32:cc * 32 + HALO], in_=ub[:, cc, T:T + HALO], func=AF.Copy)),
                     reads=[("ub", cc, 3)], writes=["tail"])
            exchange(li, 12, tail[:, :], "tail", gat[:, :], "gat")
            for cc in range(4):
                P.op("dve", (lambda cc=cc: nc.vector.tensor_scalar(out=ub[:, cc, 0:HALO], in0=gat[:, cc * 32:cc * 32 + HALO], scalar1=selc, scalar2=None, op0=ALU.mult)),
                     reads=["gat", "cst"], writes=[("ubh", cc)])
            for tg in range(TG):
                cb = []
                for cc in range(4):
                    b, bk = bank()
                    cb.append((b, bk))
                    for j in range(CW):
                        s = (cc * CW + j) % 4
                        P.op("act", (lambda s=s, cc=cc, j=j: nc.scalar.activation(out=dg[s], in_=ident, func=AF.Identity, scale=vconv(l, cc, j))),
                             reads=["cst", "vec"], writes=[("dg", s)])
                        rk = [("ub", cc, tg)] + ([("ub", cc, tg - 1)] if tg > 0 else [("ubh", cc)])
                        P.op("pe", (lambda b=b, s=s, cc=cc, j=j, tg=tg: nc.tensor.matmul(ps[b][:], lhsT=dg[s], rhs=ub[:, cc, tg * 512 + j:tg * 512 + j + 512],
                                                                                         start=(j == 0), stop=(j == CW - 1))),
                             reads=[("dg", s)] + rk, writes=[bk])
                    P.op("act", (lambda b=b, cc=cc: nc.scalar.activation(out=yc[:, cc, :], in_=ps[b][:], func=AF.Identity, bias=vconv(l, cc, CW))),
                         reads=[bk, "vec"], writes=[("yc", cc)])
                    P.op("act", (lambda cc=cc: nc.scalar.activation(out=usq[:, cc, :], in_=yc[:, cc, :], func=AF.Square)),
                         reads=[("yc", cc)], writes=[("usq", cc)])
                b1, k1 = bank()
                b2, k2 = bank()
                for cc in range(4):
                    P.op("pe", (lambda cc=cc: nc.tensor.matmul(ps[b1][:], lhsT=ones, rhs=yc[:, cc, :], start=(cc == 0), stop=(cc == 3))),
                         reads=[("yc", cc), "cst"], writes=[k1])
                    P.op("pe", (lambda cc=cc: nc.tensor.matmul(ps[b2][:], lhsT=ones, rhs=usq[:, cc, :], start=(cc == 0), stop=(cc == 3))),
                         reads=[("usq", cc), "cst"], writes=[k2])
                mu, var = tw[1], tw[2]
                P.op("act", lambda: nc.scalar.activation(out=mu, in_=ps[b1][:], func=AF.Copy, scale=1.0 / 512), reads=[k1], writes=["tw1"])
                P.op("act", lambda: nc.scalar.activation(out=var, in_=ps[b1][:], func=AF.Square, scale=1.0 / 512), reads=[k1], writes=["tw2"])
                P.op("dve", lambda: nc.vector.scalar_tensor_tensor(out=var, in0=ps[b2][:], scalar=1.0 / 512, in1=var, op0=ALU.mult, op1=ALU.subtract),
                     reads=[k2, "tw2"], writes=["tw2"])
                P.op("dve", lambda: nc.vector.tensor_scalar(out=var, in0=var, scalar1=0.0, scalar2=EPS, op0=ALU.max, op1=ALU.add), reads=["tw2"], writes=["tw2"])
                P.op("act", lambda: nc.scalar.activation(out=var, in_=var, func=AF.Sqrt), reads=["tw2"], writes=["tw2"])
                P.op("dve", lambda: nc.vector.reciprocal(out=var, in_=var), reads=["tw2"], writes=["tw2"])
                for cc in range(4):
                    P.op("dve", (lambda cc=cc: nc.vector.tensor_tensor(out=yc[:, cc, :], in0=yc[:, cc, :], in1=mu, op=ALU.subtract)),
                         reads=[("yc", cc), "tw1"], writes=[("yc", cc)])
                    P.op("dve", (lambda cc=cc: nc.vector.tensor_tensor(out=yc[:, cc, :], in0=yc[:, cc, :], in1=var, op=ALU.mult)),
                         reads=[("yc", cc), "tw2"], writes=[("yc", cc)])
                    mo, mk = mix_slot()
                    P.op("act", (lambda: nc.scalar.activation(out=mo, in_=yc[:, cc, :], func=AF.Silu, scale=vconv(l, cc, CW + 1), bias=vconv(l, cc, CW + 2))),
                         reads=[("yc", cc), "vec"], writes=[mk])
                    mix_flush(mo, mk, cc, tg)
            P.barrier()

            if stop_after == "conv":
                P.emit()
                return nc
            R.reset(m_mid)
            qt = R.take([128, T], BF16)
            kt = R.take([128, T], BF16)
            itm = R.take([128, 32, 128], BF16)
            sloc = R.take([128, 32, 128], BF16)
            khat = R.take([128, 32, 128], BF16)
            qg = R.take([128, 512], BF16)
            khf = qg
            pm = R.take([128, 512], BF16)
            Sf = R.take([128, 128], F32)
            sinb = R.take([128, 128], BF16)
            gath = R.take([128, 128], F32)
            hs = R.take([128, 160], F32)
            elast, eb, bv, glast = hs[:, 0:32], hs[:, 32:64], hs[:, 64:72], hs[:, 72:73]
            Gt = tw[3]
            def hgrn_gen():
                for h in range(NH):
                    lbc = lbt[:, l * 6 + h:l * 6 + h + 1]
                    omc = oml[:, l * 6 + h:l * 6 + h + 1]
                    pq = proj_fm(C_HQ + h * 128)
                    pf = proj_fm(C_HF + h * 128)
                    P.op("dve", lambda: nc.vector.memset(sloc[:, 0, :], 0.0), writes=[("sloc", 0)])
                    P.op("dve", lambda: nc.vector.memset(Sf[:, :], 0.0), writes=["Sf"])
                    for tg in range(TG):
                        (bq, kq), (bf_, kf) = pq(tg), pf(tg)
                        P.op("act", (lambda bq=bq: nc.scalar.activation(out=tw[0], in_=ps[bq][:], func=AF.Silu)), reads=[kq], writes=["tw0"])
                        P.op("act", (lambda b=bf_: nc.scalar.activation(out=tw[1], in_=ps[b][:], func=AF.Sigmoid)), reads=[kf], writes=["tw1"])
                        P.op("dve", lambda: nc.vector.tensor_scalar(out=tw[1], in0=tw[1], scalar1=omc, scalar2=lbc, op0=ALU.mult, op1=ALU.add),
                             reads=["tw1", "lbt", "oml"], writes=["tw1"])
                        P.op("act", lambda: nc.scalar.activation(out=tw[2], in_=tw[1], func=AF.Ln), reads=["tw1"], writes=["tw2"])
                        P.op("dve", lambda: nc.vector.tensor_scalar(out=tw[1], in0=tw[1], scalar1=-1.0, scalar2=1.0, op0=ALU.mult, op1=ALU.add),
                             reads=["tw1"], writes=["tw1"])
                        if tg == 0:
                            P.op("dve", lambda: nc.vector.tensor_tensor_scan(out=Gt, data0=ones[:, 0:1].to_broadcast([128, 512]), data1=tw[2], initial=0.0, op0=ALU.mult, op1=ALU.add),
                                 reads=["tw2", "cst"], writes=["Gt"])
                            P.op("dve", lambda: nc.vector.memset(bv[:, 0:1], 0.0), writes=["bv"])
                        else:
                            P.op("dve", lambda: nc.vector.tensor_copy(out=bv[:, 0:1], in_=glast), reads=["glast"], writes=["bv"])
                            P.op("dve", lambda: nc.vector.tensor_tensor_scan(out=Gt, data0=ones[:, 0:1].to_broadcast([128, 512]), data1=tw[2], initial=glast, op0=ALU.mult, op1=ALU.add),
                                 reads=["tw2", "cst", "glast"], writes=["Gt"])
                        Gv = Gt.rearrange("p (c i) -> p c i", i=64)
                        P.op("dve", lambda: nc.vector.tensor_copy(out=bv[:, 1:8], in_=Gv[:, 0:7, 63]), reads=["Gt"], writes=["bv"])
                        P.op("act", lambda: nc.scalar.activation(out=glast, in_=Gt[:, 511:512], func=AF.Copy), reads=["Gt"], writes=["glast"])
                        P.op("act", (lambda tg=tg: nc.scalar.activation(out=eb[:, tg * 8:tg * 8 + 8], in_=bv, func=AF.Exp)), reads=["bv"], writes=["eb"])
                        P.op("dve", lambda: nc.vector.tensor_tensor(out=Gv, in0=Gv, in1=bv.unsqueeze(2).to_broadcast([128, 8, 64]), op=ALU.subtract),
                             reads=["Gt", "bv"], writes=["Gt"])
                        P.op("act", lambda: nc.scalar.activation(out=tw[2], in_=Gt, func=AF.Exp), reads=["Gt"], writes=["tw2"])
                        P.op("act", lambda: nc.scalar.activation(out=tw[4], in_=Gt, func=AF.Exp, scale=-1.0), reads=["Gt"], writes=["tw4"])
                        e3 = tw[2].rearrange("p (c i) -> p c i", i=64)
                        P.op("act", (lambda tg=tg: nc.scalar.activation(out=elast[:, tg * 8:tg * 8 + 8], in_=e3[:, :, 63], func=AF.Copy)), reads=["tw2"], writes=["elast"])
                        P.op("dve", (lambda tg=tg: nc.vector.tensor_tensor(out=qt[:, tg * 512:(tg + 1) * 512], in0=tw[0], in1=tw[2], op=ALU.mult)),
                             reads=["tw0", "tw2"], writes=[("qt", tg)])
                        P.op("dve", lambda: nc.vector.tensor_tensor(out=tw[4], in0=tw[1], in1=tw[4], op=ALU.mult), reads=["tw1", "tw4"], writes=["tw4"])
                        P.op("act", (lambda tg=tg: nc.scalar.activation(out=kt[:, tg * 512:(tg + 1) * 512], in_=tw[4], func=AF.Copy)), reads=["tw4"], writes=[("kt", tg)])
                        k3 = tw[4].rearrange("p (c i) -> p c i", i=64)
                        P.op("dve", (lambda tg=tg: nc.vector.tensor_tensor(out=khf.rearrange("p (c i) -> p c i", i=64), in0=k3,
                                                                            in1=elast[:, tg * 8:tg * 8 + 8].unsqueeze(2).to_broadcast([128, 8, 64]), op=ALU.mult)),
                             reads=["tw4", "elast"], writes=["qg"])
                        for half in range(2):
                            b, bk = bank()
                            pv = ps[b][:].bitcast(BF16)
                            for jj in range(4):
                                j = half * 4 + jj
                                P.op("pe", (lambda pv=pv, jj=jj, j=j: nc.tensor.transpose(pv[0:64, jj * 128:(jj + 1) * 128], khf[:, j * 64:(j + 1) * 64], identb[:])),
                                     reads=["qg", "identb"], writes=[bk])
                            P.op("act", (lambda pv=pv, tg=tg, half=half: nc.scalar.activation(
                                out=khat[0:64, tg * 8 + half * 4:tg * 8 + half * 4 + 4, :], in_=pv[0:64, 0:512].rearrange("p (c d) -> p c d", d=128), func=AF.Copy)),
                                reads=[bk], writes=[("khat", tg)])
                        yield
                    wi, wik = wpiece(C_HI + h * 128)
                    for c4 in range(8):
                        b, bk = bank()
                        for jj in range(4):
                            c = c4 * 4 + jj
                            tg = c // 8
                            for k in range(16):
                                P.op("pe", (lambda b=b, jj=jj, c=c, k=k: nc.tensor.matmul(ps[b][0:64, jj * 128:(jj + 1) * 128], lhsT=A[:, k, c * 64:(c + 1) * 64], rhs=wi[:, k, :],
                                                                                           start=(k == 0), stop=(k == 15))),
                                     reads=[wik] + ([("A", kk, tg) for kk in range(16)] if k == 0 else []), writes=[bk])
                        P.op("act", (lambda b=b, c4=c4: nc.scalar.activation(out=itm[0:64, c4 * 4:c4 * 4 + 4, :], in_=ps[b][0:64, :].rearrange("p (c d) -> p c d", d=128), func=AF.Copy)),
                             reads=[bk], writes=[("itm", c4)])
                        yield
                    for c in range(32):
                        if c % 8 == 0 and c > 0:
                            yield
                        b, bk = bank()
                        P.op("pe", (lambda b=b, c=c: nc.tensor.matmul(ps[b][:, 0:128], lhsT=khat[0:64, c, :], rhs=itm[0:64, c, :], start=True, stop=True)),
                             reads=[("khat", c // 8), ("itm", c // 4)], writes=[bk])
                        P.op("dve", (lambda b=b, c=c: nc.vector.scalar_tensor_tensor(out=Sf[:, :], in0=Sf[:, :], scalar=elast[:, c:c + 1], in1=ps[b][:, 0:128], op0=ALU.mult, op1=ALU.add)),
                             reads=[bk, "Sf", "elast"], writes=["Sf"])
                        if c < 31:
                            P.op("act", (lambda c=c: nc.scalar.activation(out=sloc[:, c + 1, :], in_=Sf[:, :], func=AF.Copy)), reads=["Sf"], writes=[("sloc", c + 1)])
                    exchange(li, h, Sf[:, :], "Sf", gath[:, :], "gath")
                    yield
                    P.op("dve", lambda: nc.vector.tensor_scalar(out=sinb[:, :], in0=gath[:, :], scalar1=selc, scalar2=None, op0=ALU.mult), reads=["gath", "cst"], writes=["sinb"])
                    pgate = proj_fm(C_HG + h * 128)
                    for tg in range(TG):
                        bs, ks = bank()
                        for j in range(8):
                            c = tg * 8 + j
                            P.op("pe", (lambda j=j, c=c: nc.tensor.matmul(ps[bs][0:64, j * 64:(j + 1) * 64], lhsT=kt[:, c * 64:(c + 1) * 64], rhs=qt[:, c * 64:(c + 1) * 64], start=True, stop=True)),
                                 reads=[("kt", tg), ("qt", tg)], writes=[ks])
                        P.op("dve", lambda: nc.vector.tensor_tensor(out=pm[0:64, :].rearrange("p (c i) -> p c i", i=64), in0=ps[bs][0:64, :].rearrange("p (c i) -> p c i", i=64),
                                                                     in1=tri[0:64, 0:64].unsqueeze(1).to_broadcast([64, 8, 64]), op=ALU.mult),
                             reads=[ks, "cst"], writes=["pm"])
                        P.op("dve", (lambda tg=tg: nc.vector.tensor_tensor(out=qg.rearrange("p (c i) -> p c i", i=64), in0=qt[:, tg * 512:(tg + 1) * 512].rearrange("p (c i) -> p c i", i=64),
                                                                            in1=eb[:, tg * 8:tg * 8 + 8].unsqueeze(2).to_broadcast([128, 8, 64]), op=ALU.mult)),
                             reads=[("qt", tg), "eb"], writes=["qg"])
                        bo, ko = bank()
                        for j in range(8):
                            c = tg * 8 + j
                            osl = ps[bo][:, j * 64:(j + 1) * 64]
                            P.op("pe", (lambda osl=osl, c=c, j=j: nc.tensor.matmul(osl, lhsT=itm[0:64, c, :], rhs=pm[0:64, j * 64:(j + 1) * 64], start=True, stop=False)),
                                 reads=[("itm", c // 4), "pm"], writes=[ko])
                            P.op("pe", (lambda osl=osl, c=c: nc.tensor.matmul(osl, lhsT=sloc[:, c, :], rhs=qt[:, c * 64:(c + 1) * 64], start=False, stop=False)),
                                 reads=[("sloc", c), ("qt", tg)], writes=[ko])
                            P.op("pe", (lambda osl=osl, j=j: nc.tensor.matmul(osl, lhsT=sinb[:, :], rhs=qg[:, j * 64:(j + 1) * 64], start=False, stop=True)),
                                 reads=["sinb", "qg"], writes=[ko])
                        P.op("act", lambda: nc.scalar.activation(out=tw[5], in_=ps[bo][:], func=AF.Square), reads=[ko], writes=["tw5"])
                        bn_, kn = bank()
                        P.op("pe", lambda: nc.tensor.matmul(ps[bn_][:], lhsT=ones, rhs=tw[5], start=True, stop=True), reads=["tw5", "cst"], writes=[kn])
                        P.op("dve", lambda: nc.vector.tensor_scalar(out=tw[5], in0=ps[bn_][:], scalar1=1.0 / 128, scalar2=EPS, op0=ALU.mult, op1=ALU.add), reads=[kn], writes=["tw5"])
                        P.op("act", lambda: nc.scalar.activation(out=tw[5], in_=tw[5], func=AF.Sqrt), reads=["tw5"], writes=["tw5"])
                        P.op("dve", lambda: nc.vector.reciprocal(out=tw[5], in_=tw[5]), reads=["tw5"], writes=["tw5"])
                        P.op("dve", lambda: nc.vector.tensor_tensor(out=tw[5], in0=ps[bo][:], in1=tw[5], op=ALU.mult), reads=[ko, "tw5"], writes=["tw5"])
                        bg, kg = pgate(tg)
                        P.op("act", (lambda bg=bg: nc.scalar.activation(out=tw[0], in_=ps[bg][:], func=AF.Sigmoid)), reads=[kg], writes=["tw0"])
                        mo, mk = mix_slot()
                        P.op("dve", (lambda: nc.vector.scalar_tensor_tensor(out=mo, in0=tw[5], scalar=vhead(1, l, h), in1=tw[0], op0=ALU.mult, op1=ALU.mult)),
                             reads=["tw5", "tw0", "vec"], writes=[mk])
                        mix_flush(mo, mk, 4 + h, tg)
                        yield
            rw = [R.take([128, 512], F32) for _ in range(4)]
            qf = R.take([128, T], BF16)
            kf_ = R.take([128, T], BF16)
            qd = R.take([128, T], BF16)
            ktm = R.take([128, NT, 128], BF16)
            vtm = R.take([128, NT, 128], BF16)
            slr = R.take([128, NT, 128], BF16)
            rotL = [R.take([128, 2, 128], F32) for _ in range(2)]
            rtL = [[R.take([128, 2, 64], F32) for _ in range(4)] for _ in range(2)]
            rbfL = [R.take([128, 2, 128], BF16) for _ in range(2)]
            pmr = R.take([128, 512], BF16)
            qgr = R.take([128, 128], BF16)
            dth = R.take([128, 128], F32)
            rdh = R.take([128, 128], F32)
            kdc = R.take([128, 2], F32)
            Sr = R.take([128, 128], F32)
            sinr = R.take([128, 128], BF16)
            gatr = R.take([128, 128], F32)
            osb = R.take([128, 512], F32)
            def ret_gen():
                for h in range(NH):
                    lg = LOGG[h]
                    P.op("act", (lambda lg=lg: nc.scalar.activation(out=dth[:, :], in_=tms, func=AF.Exp, scale=lg)), reads=["cst"], writes=["dth"])
                    P.op("dve", lambda: nc.vector.tensor_tensor(out=dth[:, :], in0=dth[:, :], in1=tri, op=ALU.mult), reads=["dth", "cst"], writes=["dth"])
                    P.op("act", (lambda lg=lg: nc.scalar.activation(out=rdh[:, :], in_=iot1, func=AF.Exp, scale=lg)), reads=["cst"], writes=["rdh"])
                    P.op("act", (lambda lg=lg: nc.scalar.activation(out=kdc[:, 0:1], in_=colr, func=AF.Exp, scale=lg)), reads=["cst"], writes=["kdc"])
                    P.op("act", lambda: nc.scalar.mul(out=kdc[:, 1:2], in_=kdc[:, 0:1], mul=128.0 ** -0.5), reads=["kdc"], writes=["kdc2"])
                    w3 = [wpiece(c0 + h * 128, "R") for c0 in (C_RQ, C_RK, C_RV)]
                    P.op("dve", lambda: nc.vector.memset(slr[:, 0, :], 0.0), writes=[("slr", 0)])
                    P.op("dve", lambda: nc.vector.memset(Sr[:, :], 0.0), writes=["Sr"])
                    for j in range(NT):
                        tg = j // 4
                        b, bk = bank("b")
                        for pi in range(3):
                            wq, wqk = w3[pi]
                            for k in range(16):
                                P.op("pe", (lambda b=b, j=j, k=k, pi=pi, wq=wq: nc.tensor.matmul(ps[b][:, pi * 128:(pi + 1) * 128], lhsT=A[:, k, j * 128:(j + 1) * 128], rhs=wq[:, k, :], start=(k == 0), stop=(k == 15))),
                                     reads=[wqk] + ([("A", kk, tg) for kk in range(16)] if k == 0 else []), writes=[bk])
                        x4 = ps[b][:, 0:256].rearrange("p (w u i) -> p w u i", w=2, u=2)
                        cosb = cs[:, 0, j, :].unsqueeze(1).to_broadcast([128, 2, 64])
                        sinb_ = cs[:, 1, j, :].unsqueeze(1).to_broadcast([128, 2, 64])
                        pp = j % 2
                        rot, rbf = rotL[pp], rbfL[pp]
                        t1, t2, t3, t4 = rtL[pp]
                        kr, kb = ("rot", pp), ("rbf", pp)
                        r4 = rot.rearrange("p w (u i) -> p w u i", u=2)
                        P.op("dve", (lambda: nc.vector.tensor_tensor(out=t1, in0=x4[:, :, 0, :], in1=cosb, op=ALU.mult)), reads=[bk, "cs"], writes=[("rt", pp, 1)])
                        P.op("dve", (lambda: nc.vector.tensor_tensor(out=t2, in0=x4[:, :, 1, :], in1=sinb_, op=ALU.mult)), reads=[bk, "cs"], writes=[("rt", pp, 2)])
                        P.op("dve", (lambda: nc.vector.tensor_tensor(out=t3, in0=x4[:, :, 0, :], in1=sinb_, op=ALU.mult)), reads=[bk, "cs"], writes=[("rt", pp, 3)])
                        P.op("dve", (lambda: nc.vector.tensor_tensor(out=t4, in0=x4[:, :, 1, :], in1=cosb, op=ALU.mult)), reads=[bk, "cs"], writes=[("rt", pp, 4)])
                        P.op("act", (lambda: nc.scalar.activation(out=vtm[:, j, :], in_=ps[b][:, 256:384], func=AF.Copy)), reads=[bk], writes=[("vtm", j)])
                        P.op("dve", (lambda: nc.vector.tensor_tensor(out=r4[:, :, 0, :], in0=t1, in1=t2, op=ALU.subtract)), reads=[("rt", pp, 1), ("rt", pp, 2)], writes=[kr])
                        P.op("dve", (lambda: nc.vector.tensor_tensor(out=r4[:, :, 1, :], in0=t3, in1=t4, op=ALU.add)), reads=[("rt", pp, 3), ("rt", pp, 4)], writes=[kr])
                        P.op("act", lambda: nc.scalar.activation(out=rbf[:, 0, :], in_=rot[:, 0, :], func=AF.Copy), reads=[kr], writes=[kb])
                        P.op("act", lambda: nc.scalar.activation(out=rbf[:, 1, :], in_=rot[:, 1, :], func=AF.Copy, scale=128.0 ** -0.5), reads=[kr], writes=[kb])
                        P.op("act", (lambda: nc.scalar.activation(out=ktm[:, j, :], in_=rot[:, 1, :], func=AF.Identity, scale=kdc[:, 1:2])), reads=[kr, "kdc2"], writes=[("ktm", j)])
                        bt, kt_ = bank("b")
                        pv = ps[bt][:].bitcast(BF16)
                        P.op("pe", (lambda: nc.tensor.transpose(pv[:, 0:128], rbf[:, 0, :], identb[:])), reads=[kb, "identb"], writes=[kt_])
                        P.op("pe", (lambda: nc.tensor.transpose(pv[:, 128:256], rbf[:, 1, :], identb[:])), reads=[kb, "identb"], writes=[kt_])
                        P.op("pe", (lambda: nc.tensor.matmul(ps[bt][:, 256:384], lhsT=ktm[:, j, :], rhs=vtm[:, j, :], start=True, stop=True)),
                             reads=[("ktm", j), ("vtm", j)], writes=[kt_])
                        P.op("act", (lambda: nc.scalar.activation(out=qf[:, j * 128:(j + 1) * 128], in_=pv[:, 0:128], func=AF.Copy)), reads=[kt_], writes=[("qf", j)])
                        P.op("act", (lambda: nc.scalar.activation(out=kf_[:, j * 128:(j + 1) * 128], in_=pv[:, 128:256], func=AF.Copy)), reads=[kt_], writes=[("kf", j)])
                        P.op("dve", (lambda: nc.vector.tensor_tensor(out=qd[:, j * 128:(j + 1) * 128], in0=pv[:, 0:128], in1=rdh[:, :], op=ALU.mult)),
                             reads=[kt_, "rdh"], writes=[("qd", j)])
                        P.op("dve", (lambda: nc.vector.scalar_tensor_tensor(out=Sr[:, :], in0=Sr[:, :], scalar=math.exp(lg * 128), in1=ps[bt][:, 256:384], op0=ALU.mult, op1=ALU.add)),
                             reads=[kt_, "Sr"], writes=["Sr"])
                        if j < NT - 1:
                            P.op("act", (lambda: nc.scalar.activation(out=slr[:, j + 1, :], in_=Sr[:, :], func=AF.Copy)), reads=["Sr"], writes=[("slr", j + 1)])
                        yield
                    exchange(li, 6 + h, Sr[:, :], "Sr", gatr[:, :], "gatr")
                    yield
                    P.op("dve", lambda: nc.vector.tensor_scalar(out=sinr[:, :], in0=gatr[:, :], scalar1=selc, scalar2=None, op0=ALU.mult), reads=["gatr", "cst"], writes=["sinr"])
                    pgate = proj_fm(C_RG + h * 128, "R", "b")
                    for tg in range(TG):
                        bs, ks = bank("b")
                        for jj in range(4):
                            j = tg * 4 + jj
                            P.op("pe", (lambda jj=jj, j=j: nc.tensor.matmul(ps[bs][:, jj * 128:(jj + 1) * 128], lhsT=kf_[:, j * 128:(j + 1) * 128], rhs=qf[:, j * 128:(j + 1) * 128], start=True, stop=True)),
                                 reads=[("kf", j), ("qf", j)], writes=[ks])
                        P.op("dve", lambda: nc.vector.tensor_tensor(out=pmr.rearrange("p (c i) -> p c i", i=128), in0=ps[bs][:].rearrange("p (c i) -> p c i", i=128),
                                                                     in1=dth[:, :].unsqueeze(1).to_broadcast([128, 4, 128]), op=ALU.mult),
                             reads=[ks, "dth"], writes=["pmr"])
                        bo, ko = bank("b")
                        for jj in range(4):
                            j = tg * 4 + jj
                            osl = ps[bo][:, jj * 128:(jj + 1) * 128]
                            P.op("dve", (lambda j=j, lg=lg: nc.vector.tensor_scalar(out=qgr[:, :], in0=qd[:, j * 128:(j + 1) * 128], scalar1=math.exp(lg * 128 * j), scalar2=None, op0=ALU.mult)),
                                 reads=[("qd", j)], writes=["qgr"])
                            P.op("pe", (lambda osl=osl, j=j, jj=jj: nc.tensor.matmul(osl, lhsT=vtm[:, j, :], rhs=pmr[:, jj * 128:(jj + 1) * 128], start=True, stop=False)),
                                 reads=[("vtm", j), "pmr"], writes=[ko])
                            P.op("pe", (lambda osl=osl, j=j: nc.tensor.matmul(osl, lhsT=slr[:, j, :], rhs=qd[:, j * 128:(j + 1) * 128], start=False, stop=False)),
                                 reads=[("slr", j), ("qd", j)], writes=[ko])
                            P.op("pe", (lambda osl=osl: nc.tensor.matmul(osl, lhsT=sinr[:, :], rhs=qgr[:, :], start=False, stop=True)), reads=["sinr", "qgr"], writes=[ko])
                        P.op("act", lambda: nc.scalar.activation(out=osb[:, :], in_=ps[bo][:], func=AF.Copy), reads=[ko], writes=["osb"])
                        P.op("act", lambda: nc.scalar.activation(out=rw[3], in_=ps[bo][:], func=AF.Square), reads=[ko], writes=["rw3"])
                        b1, k1 = bank("b")
                        b2, k2 = bank("b")
                        P.op("pe", lambda: nc.tensor.matmul(ps[b1][:], lhsT=ones, rhs=osb[:, :], start=True, stop=True), reads=["osb", "cst"], writes=[k1])
                        P.op("pe", lambda: nc.tensor.matmul(ps[b2][:], lhsT=ones, rhs=rw[3], start=True, stop=True), reads=["rw3", "cst"], writes=[k2])
                        mu, var = rw[1], rw[2]
                        P.op("act", lambda: nc.scalar.activation(out=mu, in_=ps[b1][:], func=AF.Copy, scale=1.0 / 128), reads=[k1], writes=["rw1"])
                        P.op("act", lambda: nc.scalar.activation(out=var, in_=ps[b1][:], func=AF.Square, scale=1.0 / 128), reads=[k1], writes=["rw2"])
                        P.op("dve", lambda: nc.vector.scalar_tensor_tensor(out=var, in0=ps[b2][:], scalar=1.0 / 128, in1=var, op0=ALU.mult, op1=ALU.subtract), reads=[k2, "rw2"], writes=["rw2"])
                        P.op("dve", lambda: nc.vector.tensor_scalar(out=var, in0=var, scalar1=0.0, scalar2=EPS, op0=ALU.max, op1=ALU.add), reads=["rw2"], writes=["rw2"])
                        P.op("act", lambda: nc.scalar.activation(out=var, in_=var, func=AF.Sqrt), reads=["rw2"], writes=["rw2"])
                        P.op("dve", lambda: nc.vector.reciprocal(out=var, in_=var), reads=["rw2"], writes=["rw2"])
                        P.op("dve", lambda: nc.vector.tensor_tensor(out=osb[:, :], in0=osb[:, :], in1=mu, op=ALU.subtract), reads=["osb", "rw1"], writes=["osb"])
                        P.op("dve", lambda: nc.vector.tensor_tensor(out=osb[:, :], in0=osb[:, :], in1=var, op=ALU.mult), reads=["osb", "rw2"], writes=["osb"])
                        P.op("act", (lambda h=h: nc.scalar.activation(out=osb[:, :], in_=osb[:, :], func=AF.Identity, scale=vhead(2, l, h), bias=vhead(3, l, h))),
                             reads=["osb", "vec"], writes=["osb"])
                        bg, kg = pgate(tg)
                        P.op("act", (lambda bg=bg: nc.scalar.activation(out=rw[0], in_=ps[bg][:], func=AF.Silu)), reads=[kg], writes=["rw0"])
                        mo, mk = mix_slot()
                        P.op("dve", (lambda: nc.vector.tensor_tensor(out=mo, in0=osb[:, :], in1=rw[0], op=ALU.mult)), reads=["osb", "rw0"], writes=[mk])
                        mix_flush(mo, mk, 10 + h, tg)
                        yield
            gens = [hgrn_gen(), ret_gen()]
            while gens:
                for g in list(gens):
                    try:
                        next(g)
                    except StopIteration:
                        gens.remove(g)
            P.barrier()
            R.reset(m_base)
            rt2 = [R.take([128, 16, 512], F32) for _ in range(2)]
            mixs = R.take([128, 16, 1024], BF16)
            wor = [R.take([128, 16, 256], BF16) for _ in range(2)]
            xr = [R.take([128, 512], F32) for _ in range(3)]
            so = [R.take([128, 512], F32) for _ in range(3)]
            n_w = 0
            n_s = 0
            n_x = 0
            for pp in range(2):
                P.dma("sp", (lambda: nc.sync.dma_start(out=mixs, in_=mixd[:, :, pp * 1024:(pp + 1) * 1024].rearrange("c p t -> p c t"))),
                      reads=[("mixd", cc_, pp * 2 + tl_) for cc_ in range(16) for tl_ in range(2)], writes=["mixs"])
                stbs = [stat_banks(), stat_banks()]
                pend_ln = []
                for c2 in range(8):
                    s = n_w % 2
                    n_w += 1
                    P.dma("pool", (lambda: nc.gpsimd.dma_start(out=wor[s], in_=wol[:, :, c2 * 256:(c2 + 1) * 256])), writes=[("wor", s)])
                    for cc in range(2):
                        c = c2 * 2 + cc
                        for tl in range(2):
                            tg = pp * 2 + tl
                            xs = n_x % 3
                            n_x += 1
                            P.dma("sp", (lambda: nc.sync.dma_start(out=xr[xs], in_=xres[c][:, tg * 512:(tg + 1) * 512])), reads=[("xres", c, tg)], writes=[("xr", xs)])
                            b, bk = bank()
                            for k in range(16):
                                P.op("pe", (lambda: nc.tensor.matmul(ps[b][:], lhsT=wor[s][:, k, cc * 128:(cc + 1) * 128], rhs=mixs[:, k, tl * 512:(tl + 1) * 512],
                                                                     start=(k == 0), stop=(k == 15))),
                                     reads=[("wor", s), "mixs"], writes=[bk])
                            P.op("dve", (lambda: nc.vector.scalar_tensor_tensor(out=rt2[tl][:, c, :], in0=xr[xs], scalar=ALPHA, in1=ps[b][:], op0=ALU.mult, op1=ALU.add)),
                                 reads=[bk, ("xr", xs)], writes=[("rt", tl, c)])
                            pend_ln.append((stbs[tl], c, rt2[tl][:, c, :], ("rt", tl, c)))
                            if len(pend_ln) > 1:
                                ln_add(*pend_ln.pop(0))
                while pend_ln:
                    ln_add(*pend_ln.pop(0))
                for tl in range(2):
                    tg = pp * 2 + tl

                    def emit_out(c, tg=tg, tl=tl):
                        nonlocal n_s
                        s = n_s % 3
                        n_s += 1
                        P.op("act", (lambda: nc.scalar.activation(out=A[:, c, tg * 512:(tg + 1) * 512], in_=rt2[tl][:, c, :], func=AF.Identity, scale=vln(0, l, c), bias=vln(1, l, c))),
                             reads=[("rt", tl, c), "vec"], writes=[("A", c, tg)])
                        P.op("act", (lambda: nc.scalar.activation(out=so[s], in_=rt2[tl][:, c, :], func=AF.Identity, scale=agcol(0, l, c), bias=agcol(1, l, c))),
                             reads=[("rt", tl, c), "agb"], writes=[("so", s)])
                        P.dma("sp", (lambda: nc.sync.dma_start(out=x1res[c][:, tg * 512:(tg + 1) * 512], in_=so[s])), reads=[("so", s)], writes=[("x1res", c, tg)])
                    ln_finish(stbs[tl], (lambda c, tl=tl: rt2[tl][:, c, :]), (lambda c, tl=tl: ("rt", tl, c)), emit_out)
            P.barrier()

            if stop_after == "O":
                P.emit()
                return nc
            R.reset()
            w1r = [R.take([128, 16, 256], BF16) for _ in range(3)]
            ft = [R.take([128, T], BF16) for _ in range(3)]
            rl = [R.take([128, 512], F32) for _ in range(3)]
            n_r = 0
            for pi in range(32):
                s = pi % 3
                P.dma("pool", (lambda s=s, pi=pi: nc.gpsimd.dma_start(out=w1r[s], in_=w1l[:, :, pi * 256:(pi + 1) * 256])), writes=[("w1r", s)])
                for mm in range(2):
                    m = pi * 2 + mm
                    fs = m % 3
                    for tg in range(TG):
                        b, bk = bank()
                        for k in range(16):
                            P.op("pe", (lambda b=b, s=s, k=k, mm=mm, tg=tg: nc.tensor.matmul(ps[b][:], lhsT=w1r[s][:, k, mm * 128:(mm + 1) * 128], rhs=A[:, k, tg * 512:(tg + 1) * 512],
                                                                                             start=(k == 0), stop=(k == 15))),
                                 reads=[("w1r", s)] + ([("A", kk, tg) for kk in range(16)] if k == 0 else []), writes=[bk])
                        rs = n_r % 3
                        n_r += 1
                        P.op("act", (lambda b=b, rs=rs: nc.scalar.activation(out=rl[rs], in_=ps[b][:], func=AF.Relu)), reads=[bk], writes=[("rl", rs)])
                        P.op("dve", (lambda b=b, rs=rs, fs=fs, tg=tg: nc.vector.tensor_tensor(out=ft[fs][:, tg * 512:(tg + 1) * 512], in0=rl[rs], in1=ps[b][:], op=ALU.mult)),
                             reads=[bk, ("rl", rs)], writes=[("ft", fs, tg)])
                    P.dma("sp", (lambda m=m, fs=fs: nc.sync.dma_start(out=fd[m], in_=ft[fs])), reads=[("ft", fs, tg) for tg in range(TG)], writes=[("fd", m)])
            P.barrier()

            if stop_after == "F1":
                P.emit()
                return nc
            R.reset()
            acc = R.take([128, 16, T], F32)
            Ab = A[:].rearrange("p c t -> p (c t)")
            fr = [Ab[:, i * 8192:(i + 1) * 8192].rearrange("p (k t) -> p k t", k=4) for i in range(2)]
            w2r = [Ab[:, 16384 + i * 8192:16384 + (i + 1) * 8192].rearrange("p (k t) -> p k t", k=4) for i in range(2)]
            for c in range(16):
                P.dma("sp", (lambda c=c: nc.sync.dma_start(out=acc[:, c, :], in_=x1res[c])), reads=[("x1res", c, tg) for tg in range(TG)], writes=[("acc", c, tg) for tg in range(TG)])
            def f2_group(s, c, tg):
                b, bk = bank()
                for kk in range(4):
                    P.op("pe", (lambda b=b, s=s, kk=kk, c=c, tg=tg: nc.tensor.matmul(ps[b][:], lhsT=w2r[s][:, kk, c * 128:(c + 1) * 128], rhs=fr[s][:, kk, tg * 512:(tg + 1) * 512],
                                                                                     start=(kk == 0), stop=(kk == 3))),
                         reads=[("w2r", s), ("fr", s)], writes=[bk])
                P.op("dve", (lambda b=b, c=c, tg=tg: nc.vector.tensor_tensor(out=acc[:, c, tg * 512:(tg + 1) * 512], in0=acc[:, c, tg * 512:(tg + 1) * 512], in1=ps[b][:], op=ALU.add)),
                     reads=[bk, ("acc", c, tg)], writes=[("acc", c, tg)])

            for jb in range(16):
                s = jb % 2
                P.dma("sp", (lambda s=s, jb=jb: nc.sync.dma_start(out=fr[s], in_=fd[jb * 4:(jb + 1) * 4].rearrange("k p t -> p k t"))),
                      reads=[("fd", jb * 4 + kk) for kk in range(4)], writes=[("fr", s)])
                P.dma("pool", (lambda s=s, jb=jb: nc.gpsimd.dma_start(out=w2r[s], in_=w2l[jb * 512:(jb + 1) * 512, :].rearrange("(k p) n -> p k n", p=128))), writes=[("w2r", s)])
                if jb < 15:
                    for c in range(16):
                        for tg in range(TG):
                            f2_group(s, c, tg)
                else:
                    for tg in range(TG):
                        stb = stat_banks()
                        pend_ln = []
                        for c in range(16):
                            f2_group(s, c, tg)
                            pend_ln.append((stb, c, acc[:, c, tg * 512:(tg + 1) * 512], ("acc", c, tg)))
                            if len(pend_ln) > 2:
                                ln_add(*pend_ln.pop(0))
                        while pend_ln:
                            ln_add(*pend_ln.pop(0))

                        def emit_out2(c, tg=tg):
                            accs = acc[:, c, tg * 512:(tg + 1) * 512]
                            P.op("act", (lambda c=c, tg=tg, accs=accs: nc.scalar.activation(out=accs, in_=accs, func=AF.Identity, scale=vln(2, l, c), bias=vln(3, l, c))),
                                 reads=[("acc", c, tg), "vec"], writes=[("acc", c, tg)])
                            if not (last and final_out):
                                P.dma("sp", (lambda c=c, tg=tg, accs=accs: nc.sync.dma_start(out=xres[c][:, tg * 512:(tg + 1) * 512], in_=accs)),
                                      reads=[("acc", c, tg)], writes=[("xres", c, tg)])
                        ln_finish(stb, lambda c, tg=tg: acc[:, c, tg * 512:(tg + 1) * 512], lambda c, tg=tg: ("acc", c, tg), emit_out2)
            P.barrier()
            if not (last and final_out):
                for tg in range(TG):
                    for c in range(16):
                        eng = "dve" if (c % 2 == 0) else "act"
                        if eng == "dve":
                            P.op("dve", (lambda c=c, tg=tg: nc.vector.tensor_copy(out=A[:, c, tg * 512:(tg + 1) * 512], in_=acc[:, c, tg * 512:(tg + 1) * 512])),
                                 reads=[("acc", c, tg)], writes=[("A", c, tg)])
                        else:
                            P.op("act", (lambda c=c, tg=tg: nc.scalar.activation(out=A[:, c, tg * 512:(tg + 1) * 512], in_=acc[:, c, tg * 512:(tg + 1) * 512], func=AF.Copy)),
                                 reads=[("acc", c, tg)], writes=[("A", c, tg)])
            P.barrier()
            if last and final_out:
                Af = A[:].rearrange("p c t -> p (c t)").bitcast(F32)
                ot = [Af[:, i * 2048:(i + 1) * 2048] for i in range(4)]
                n = 0
                for j in range(NT):
                    s = n % 4
                    n += 1
                    for c4 in range(4):
                        b, bk = bank()
                        for cc in range(4):
                            c = c4 * 4 + cc
                            P.op("pe", (lambda b=b, cc=cc, c=c, j=j: nc.tensor.transpose(ps[b][:, cc * 128:(cc + 1) * 128], acc[:, c, j * 128:(j + 1) * 128], ident)),
                                 reads=[("acc", c, j // 4), "cst"], writes=[bk])
                        P.op("act" if c4 % 2 else "dve",
                             (lambda b=b, s=s, c4=c4: (nc.scalar.activation(out=ot[s][:, c4 * 512:(c4 + 1) * 512], in_=ps[b][:], func=AF.Copy) if c4 % 2
                                                        else nc.vector.tensor_copy(out=ot[s][:, c4 * 512:(c4 + 1) * 512], in_=ps[b][:]))),
                             reads=[bk], writes=[("ot", s, c4)])
                    out_ops.append(P.dma("sp", (lambda j=j, s=s: nc.sync.dma_start(out=y_out[j * 128:(j + 1) * 128, :], in_=ot[s])),
                                         reads=[("ot", s, c4) for c4 in range(4)], writes=[("yout", j)]))
        if not final_out:
            pass
        P.emit(final_wait_ops=out_ops)
    return nc


def _host_inputs(inputs):
    f32 = np.float32
    x = np.asarray(inputs["x"], f32)
    pos = np.asarray(inputs["positions"]).astype(np.int32)
    Lh = L_ALL

    def cols(v, nchunk):
        v = np.asarray(v, f32).reshape(Lh, nchunk, 128)
        return np.ascontiguousarray(v.transpose(2, 0, 1).reshape(128, Lh * nchunk))
    parts = [cols(inputs["ln1_g"], 16), cols(inputs["ln1_b"], 16), cols(inputs["ln2_g"], 16), cols(inputs["ln2_b"], 16)]
    wdw = np.asarray(inputs["w_dw"], f32).reshape(Lh, CW, 4, 128)
    conv = np.concatenate([wdw.transpose(3, 0, 2, 1),
                           np.asarray(inputs["b_dw"], f32).reshape(Lh, 4, 128).transpose(2, 0, 1)[..., None],
                           np.asarray(inputs["conv_ln_g"], f32).reshape(Lh, 4, 128).transpose(2, 0, 1)[..., None],
                           np.asarray(inputs["conv_ln_b"], f32).reshape(Lh, 4, 128).transpose(2, 0, 1)[..., None]], axis=3)
    parts.append(np.ascontiguousarray(conv.reshape(128, Lh * 4 * (CW + 3))))
    parts += [cols(inputs["hgrn_lb"], 6), cols(inputs["hgrn_norm_g"], 6), cols(inputs["ret_gn_g"], 6), cols(inputs["ret_gn_b"], 6)]
    vec = np.ascontiguousarray(np.concatenate(parts, axis=1))
    p = np.arange(128)
    ident = np.eye(128, dtype=f32)
    ones = np.ones((128, 128), f32)
    tri = (p[None, :] >= p[:, None]).astype(f32)
    tms = np.maximum(p[None, :] - p[:, None], 0).astype(f32)
    iot1 = np.broadcast_to((p + 1).astype(f32)[None, :], (128, 128))
    half = 64
    invf = (np.float32(10000.0) ** (-(np.arange(half, dtype=f32)) / np.float32(half))).astype(f32)
    invf = np.broadcast_to(invf[None, :], (128, 64))
    colr = (127 - p).astype(f32)[:, None]
    per_core = []
    for c in range(8):
        b, hf = c // 2, c % 2
        sel = np.full((128, 1), float(hf), f32)
        const = np.ascontiguousarray(np.concatenate([ident, ones, tri, tms, iot1, invf, colr, sel], axis=1).astype(f32))
        xc = np.ascontiguousarray(x[b, hf * T:(hf + 1) * T, :])
        pc = np.ascontiguousarray(pos[b, hf * T:(hf + 1) * T].reshape(NT, 128).T)
        per_core.append(dict(x_in=xc, pos_in=pc, vec_in=vec, const_in=const))
    return per_core


_NC_CACHE = {}


def kernel(**inputs):
    per_core = _host_inputs(inputs)
    w = {k: np.ascontiguousarray(np.asarray(inputs[k], np.float32)) for k in ("w_in", "w_out", "w_ff1", "w_ff2")}
    if "nc" not in _NC_CACHE:
        _NC_CACHE["nc"] = build_nc(L_ALL)
    nc = _NC_CACHE["nc"]
    in_maps = []
    for c in range(8):
        m = dict(per_core[c])
        m.update(w)
        in_maps.append(m)
    res = run_bass_kernel_spmd(nc, in_maps, core_ids=list(range(8)))
    out = np.empty((4, 4096, D), np.float32)
    for c in range(8):
        out[c // 2, (c % 2) * T:(c % 2 + 1) * T, :] = res.results[c]["y_out"]
    return out
```

```python
import contextlib
import types
import math
import numpy as np
import ml_dtypes
import concourse.bass as bass
import concourse.mybir as mybir
from concourse.bass_utils import run_bass_kernel_spmd

F32 = mybir.dt.float32
BF16 = mybir.dt.bfloat16
I32 = mybir.dt.int32
U8 = mybir.dt.uint8
AF = mybir.ActivationFunctionType
ALU = mybir.AluOpType

D = 2048
T = 2048
NT = 16
TG = 4
DIN = 7168
DFF = 8192
L_ALL = 4
EPS = 1e-5
ALPHA = (2.0 * L_ALL) ** 0.25
CW = 31
HALO = CW - 1
NH = 6
LOGG = [math.log1p(-(2.0 ** (-5.0 - h))) for h in range(NH)]
C_CA, C_CG, C_HQ, C_HF, C_HI, C_HG, C_RQ, C_RK, C_RV, C_RG = 0, 512, 1024, 1792, 2560, 3328, 4096, 4864, 5632, 6400


import os
KVAR = os.environ.get('KVAR', '')


class Prog:
    ENG = ("pe", "act", "dve", "pool", "sp")

    def __init__(self, nc, n_dma_sems=10):
        self.nc = nc
        self.ops = []
        self.last_write = {}
        self.readers = {}
        self.nd = n_dma_sems
        self.pending_bar = {}

    @staticmethod
    def _freeze(fn):
        if fn.__closure__ is None:
            return fn
        cells = []
        for c in fn.__closure__:
            try:
                cells.append(types.CellType(c.cell_contents))
            except ValueError:
                cells.append(c)
        return types.FunctionType(fn.__code__, fn.__globals__, fn.__name__, fn.__defaults__, tuple(cells))

    def op(self, eng, fn, reads=(), writes=(), kind="c"):
        fn = self._freeze(fn)
        deps = set()
        for k in reads:
            w = self.last_write.get(k)
            if w is not None:
                deps.add(w)
            if isinstance(k, tuple) and k[0] == "ps":
                for r in self.readers.get(k, ()):
                    if self.ops[r]["eng"] != eng:
                        deps.add(r)
        for k in writes:
            w = self.last_write.get(k)
            if w is not None:
                deps.add(w)
            deps.update(self.readers.get(k, ()))
        if eng in self.pending_bar:
            deps.update(self.pending_bar.pop(eng))
        idx = len(self.ops)
        self.ops.append(dict(eng=eng, fn=fn, deps=deps, kind=kind, sig=False))
        for k in reads:
            self.readers.setdefault(k, []).append(idx)
        for k in writes:
            self.last_write[k] = idx
            self.readers[k] = []
        return idx

    def dma(self, eng, fn, reads=(), writes=()):
        return self.op(eng, fn, reads, writes, kind="d")

    def cc(self, fn, reads=(), writes=()):
        return self.op("pool", fn, reads, writes, kind="cc")

    def barrier(self):
        last = {}
        asyncs = []
        for i, o in enumerate(self.ops):
            if o["kind"] == "c":
                last[o["eng"]] = i
            else:
                asyncs.append(i)
        start = getattr(self, "_bar_from", 0)
        dep = set(last.values()) | set(i for i in asyncs if i >= start)
        self._bar_from = len(self.ops)
        for e in self.ENG:
            self.pending_bar[e] = set(dep) | self.pending_bar.get(e, set())

    def emit(self, final_wait_ops=()):
        nc = self.nc
        ops = self.ops

        def skip(o, od):
            return od["kind"] == "c" and o["kind"] == "c" and od["eng"] == o["eng"] == "pe"

        for o in ops:
            for d in o["deps"]:
                od = ops[d]
                if od["kind"] == "c" and not skip(o, od):
                    od["sig"] = True
        with contextlib.ExitStack() as st:
            csem = {e: st.enter_context(nc.semaphore("c_" + e)) for e in self.ENG}
            qs = ("sp", "pool", "cc")
            dsem = {q: [st.enter_context(nc.semaphore("d_%s_%d" % (q, j))) for j in range(self.nd)] for q in qs}
            ccount = {e: 0 for e in self.ENG}
            dcount = {q: 0 for q in qs}
            for o in ops:
                if o["kind"] != "c":
                    q = "cc" if o["kind"] == "cc" else o["eng"]
                    inc = 1 if o["kind"] == "cc" else 16
                    n = dcount[q]
                    dcount[q] += 1
                    o["sem"] = dsem[q][n % self.nd]
                    o["val"] = inc * (n // self.nd + 1)
                    o["inc"] = inc
                    o["n"] = n
                elif o["sig"]:
                    ccount[o["eng"]] += 1
                    o["sem"] = csem[o["eng"]]
                    o["val"] = ccount[o["eng"]]
            per_eng = {e: [] for e in self.ENG}
            for i, o in enumerate(ops):
                per_eng[o["eng"]].append(i)
            self.stats = dict(ccount)

            def run(e, handle):
                waited = {}
                for i in per_eng[e]:
                    o = ops[i]
                    need = {}

                    def want(s, v):
                        k = id(s)
                        if need.get(k, (None, 0))[1] < v:
                            need[k] = (s, v)
                    for d in o["deps"]:
                        od = ops[d]
                        if skip(o, od):
                            continue
                        want(od["sem"], od["val"])
                    if o["kind"] != "c" and o["n"] >= self.nd:
                        want(o["sem"], o["val"] - o["inc"])
                    for k, (s, v) in need.items():
                        if waited.get(k, 0) >= v:
                            continue
                        handle.wait_ge(s, v)
                        waited[k] = v
                    inst = o["fn"]()
                    if o["kind"] != "c":
                        inst.then_inc(o["sem"], o["inc"])
                    elif o["sig"]:
                        inst.then_inc(o["sem"], 1)
                if e == "sp":
                    for i in final_wait_ops:
                        o = ops[i]
                        handle.wait_ge(o["sem"], o["val"])

            with nc.Block() as block:
                @block.tensor
                def _(h):
                    run("pe", h)

                @block.scalar
                def _(h):
                    run("act", h)

                @block.vector
                def _(h):
                    run("dve", h)

                @block.gpsimd
                def _(h):
                    run("pool", h)

                @block.sync
                def _(h):
                    run("sp", h)


class Carver:
    def __init__(self, big, nbytes):
        self.big = big
        self.n = nbytes
        self.off = 0

    def reset(self, off=0):
        self.off = off

    def take(self, shape, dt, parts=128):
        esz = {F32: 4, BF16: 2, I32: 4}[dt]
        n = esz
        for s in shape[1:]:
            n *= s
        off = (self.off + 31) // 32 * 32
        assert off + n <= self.n, ("SBUF carve overflow", off, n, self.n)
        self.off = off + n
        v = self.big[0:parts, off:off + n].bitcast(dt)
        if len(shape) > 2:
            names = " ".join("a%d" % i for i in range(len(shape) - 1))
            kw = {"a%d" % i: shape[i + 1] for i in range(len(shape) - 1)}
            v = v.rearrange("p (%s) -> p %s" % (names, names), **kw)
        return v


def build_nc(n_layers, layer0=0, x_fm_in=False, final_out=True, stop_after=None, wlayers=L_ALL, wrows=None):
    nc = bass.Bass("TRN2", target_bir_lowering=False)
    L = n_layers
    dr = lambda name, shape, dt, kind=None: (nc.dram_tensor(name, shape, dt, kind=kind) if kind else nc.dram_tensor(name, shape, dt))
    x_in = dr("x_in", [T, D], F32, "ExternalInput").ap()
    pos_in = dr("pos_in", [128, NT], I32, "ExternalInput").ap()
    w_in = dr("w_in", [wlayers, wrows or D, DIN], F32, "ExternalInput").ap()
    w_out = dr("w_out", [wlayers, wrows or D, D], F32, "ExternalInput").ap()
    w_ff1 = dr("w_ff1", [wlayers, wrows or D, DFF], F32, "ExternalInput").ap()
    w_ff2 = dr("w_ff2", [wlayers, wrows or DFF, D], F32, "ExternalInput").ap()
    NV = 4 * L_ALL * 16 + L_ALL * 4 * (CW + 3) + L_ALL * 6 * 4
    vec_in = dr("vec_in", [128, NV], F32, "ExternalInput").ap()
    NCONST = 128 * 5 + 64 + 2
    const_in = dr("const_in", [128, NCONST], F32, "ExternalInput").ap()
    y_out = dr("y_out", [T, D], F32, "ExternalOutput").ap()
    xres = dr("xres", [16, 128, T], F32).ap()
    x1res = dr("x1res", [16, 128, T], F32).ap()
    fd = dr("fd", [64, 128, T], BF16).ap()
    mixd = dr("mixd", [16, 128, T], BF16).ap()
    NX = 13
    cc_i = [[dr("cci_%d_%d" % (l, u), [128, 128], F32) for u in range(NX)] for l in range(L)]
    cc_o = [[dr("cco_%d_%d" % (l, u), [256, 128], F32) for u in range(NX)] for l in range(L)]

    P = Prog(nc)
    with contextlib.ExitStack() as st:
        A = st.enter_context(nc.sbuf_tensor("A", [128, 16, T], BF16))
        RB = 128 * 1024
        Rbig = st.enter_context(nc.sbuf_tensor("Rbig", [128, RB], U8))
        R = Carver(Rbig, RB)
        Abig_view = None
        cst = st.enter_context(nc.sbuf_tensor("cst", [128, NCONST], F32))
        vec = st.enter_context(nc.sbuf_tensor("vec", [128, NV], F32))
        identb = st.enter_context(nc.sbuf_tensor("identb", [128, 128], BF16))
        cs = st.enter_context(nc.sbuf_tensor("cs", [128, 2, NT, 64], BF16))
        sm = st.enter_context(nc.sbuf_tensor("sm", [128, 512], F32))
        lnsq = st.enter_context(nc.sbuf_tensor("lnsq", [128, 512], F32))
        ps = [st.enter_context(nc.psum_tensor("ps%d" % i, [128, 512], F32)) for i in range(8)]
        bank_ctr = [0]

        def bank(pool="a"):
            i = 0 if pool == "a" else 1
            if len(bank_ctr) < 2:
                bank_ctr.append(0)
            b = 4 * i + bank_ctr[i] % 4
            bank_ctr[i] += 1
            return b, ("ps", b)
        stat_ctr = [0]

        def stat_banks():
            p = stat_ctr[0] % 2
            stat_ctr[0] += 1
            return (4 + 2 * p, ("ps", 4 + 2 * p)), (5 + 2 * p, ("ps", 5 + 2 * p))

        ident = cst[:, 0:128]
        ones = cst[:, 128:256]
        tri = cst[:, 256:384]
        tms = cst[:, 384:512]
        iot1 = cst[:, 512:640]
        invf = cst[:, 640:704]
        colr = cst[:, 704:705]
        selc = cst[:, 705:706]

        def vln(which, l, c):
            o = (which * L_ALL + l) * 16 + c
            return vec[:, o:o + 1]
        VB = 4 * L_ALL * 16

        def vconv(l, cc, j):
            o = VB + (l * 4 + cc) * (CW + 3) + j
            return vec[:, o:o + 1]
        VH = VB + L_ALL * 4 * (CW + 3)

        def vhead(which, l, h):
            o = VH + (which * L_ALL + l) * 6 + h
            return vec[:, o:o + 1]

        lbt = sm[:, 0:24]
        oml = sm[:, 24:48]
        agb = sm[:, 48:176]
        smx = sm[:, 176:512]

        def agcol(which, l, c):
            o = 48 + (which * L_ALL + l) * 16 + c
            return sm[:, o:o + 1]

        P.dma("sp", lambda: nc.sync.dma_start(out=cst[:], in_=const_in), writes=["cst"])
        P.dma("sp", lambda: nc.sync.dma_start(out=vec[:], in_=vec_in), writes=["vec"])
        P.op("act", lambda: nc.scalar.activation(out=identb[:], in_=ident, func=AF.Copy), reads=["cst"], writes=["identb"])
        P.op("act", lambda: nc.scalar.activation(out=sm[:, 48:176], in_=vec[:, 0:2 * L_ALL * 16], func=AF.Copy, scale=ALPHA),
             reads=["vec"], writes=["agb"])
        lbr = vec[:, VH:VH + 24].rearrange("p (l h) -> p h l", l=L_ALL)
        e4 = smx[:, 0:24].rearrange("p (h l) -> p h l", l=L_ALL)
        mx = smx[:, 24:30]
        P.op("dve", lambda: nc.vector.tensor_reduce(out=mx, in_=lbr, axis=mybir.AxisListType.X, op=ALU.max), reads=["vec"], writes=["mx"])
        P.op("dve", lambda: nc.vector.tensor_tensor(out=e4, in0=lbr, in1=mx.unsqueeze(2).to_broadcast([128, 6, L_ALL]), op=ALU.subtract),
             reads=["mx", "vec"], writes=["e4"])
        P.op("act", lambda: nc.scalar.activation(out=e4, in_=e4, func=AF.Exp), reads=["e4"], writes=["e4"])
        sm6 = smx[:, 30:36]
        P.op("dve", lambda: nc.vector.tensor_reduce(out=sm6, in_=e4, axis=mybir.AxisListType.X, op=ALU.add), reads=["e4"], writes=["sm6"])
        P.op("dve", lambda: nc.vector.reciprocal(out=sm6, in_=sm6), reads=["sm6"], writes=["sm6"])
        P.op("dve", lambda: nc.vector.tensor_tensor(out=e4, in0=e4, in1=sm6.unsqueeze(2).to_broadcast([128, 6, L_ALL]), op=ALU.mult),
             reads=["sm6", "e4"], writes=["e4"])
        lbv = lbt.rearrange("p (l h) -> p h l", l=L_ALL)
        P.op("dve", lambda: nc.vector.memset(lbv[:, :, 0:1], 0.0), writes=["lbt"])
        for l in range(1, L_ALL):
            P.op("dve", (lambda l=l: nc.vector.tensor_tensor(out=lbv[:, :, l:l + 1], in0=lbv[:, :, l - 1:l], in1=e4[:, :, l:l + 1], op=ALU.add)),
                 reads=["e4", "lbt"], writes=["lbt"])
        P.op("dve", lambda: nc.vector.tensor_scalar(out=lbt, in0=lbt, scalar1=0.0, scalar2=1.0 - 1e-6, op0=ALU.max, op1=ALU.min),
             reads=["lbt"], writes=["lbt"])
        P.op("dve", lambda: nc.vector.tensor_scalar(out=oml, in0=lbt, scalar1=-1.0, scalar2=1.0, op0=ALU.mult, op1=ALU.add),
             reads=["lbt"], writes=["oml"])
        if stop_after == "s0":
            P.emit()
            return nc
        posf = smx[:, 40:56]
        P.dma("sp", lambda: nc.sync.dma_start(out=smx[:, 56:72].bitcast(I32), in_=pos_in), writes=["posi"])
        P.op("dve", lambda: nc.vector.tensor_copy(out=posf, in_=smx[:, 56:72].bitcast(I32)), reads=["posi"], writes=["posf"])
        R.reset()
        ang = R.take([128, NT, 64], F32)
        tq = R.take([128, NT, 64], F32)
        ti = R.take([128, NT, 64], I32)
        TWO_PI = 2.0 * math.pi
        for j in range(NT):
            P.op("dve", (lambda j=j: nc.vector.tensor_scalar(out=ang[:, j, :], in0=invf, scalar1=posf[:, j:j + 1], scalar2=None, op0=ALU.mult)),
                 reads=["posf", "cst"], writes=["ang"])
        for which in (1, 0):
            shift = 0.0 if which == 1 else math.pi / 2
            P.op("dve", (lambda s=shift: nc.vector.tensor_scalar(out=tq, in0=ang, scalar1=s, scalar2=1.0 / TWO_PI, op0=ALU.add, op1=ALU.mult)),
                 reads=["ang"], writes=["tq"])
            P.op("dve", lambda: nc.vector.tensor_copy(out=ti, in_=tq), reads=["tq"], writes=["ti"])
            P.op("dve", lambda: nc.vector.tensor_copy(out=tq, in_=ti), reads=["ti"], writes=["tq"])
            P.op("dve", lambda: nc.vector.tensor_scalar(out=tq, in0=tq, scalar1=-TWO_PI, scalar2=None, op0=ALU.mult), reads=["tq"], writes=["tq"])
            P.op("dve", (lambda s=shift: nc.vector.scalar_tensor_tensor(out=tq, in0=ang, scalar=s, in1=tq, op0=ALU.add, op1=ALU.add)),
                 reads=["ang", "tq"], writes=["tq"])
            tf = ti.bitcast(F32)
            P.op("dve", lambda: nc.vector.tensor_scalar(out=tf, in0=tq, scalar1=math.pi, scalar2=-TWO_PI, op0=ALU.is_gt, op1=ALU.mult),
                 reads=["tq"], writes=["ti"])
            P.op("dve", lambda: nc.vector.tensor_tensor(out=tq, in0=tq, in1=tf, op=ALU.add), reads=["tq", "ti"], writes=["tq"])
            P.op("dve", lambda: nc.vector.tensor_scalar(out=tf, in0=tq, scalar1=-math.pi, scalar2=TWO_PI, op0=ALU.is_lt, op1=ALU.mult),
                 reads=["tq"], writes=["ti"])
            P.op("dve", lambda: nc.vector.tensor_tensor(out=tq, in0=tq, in1=tf, op=ALU.add), reads=["tq", "ti"], writes=["tq"])
            P.op("dve", lambda: nc.vector.tensor_scalar(out=tq, in0=tq, scalar1=-math.pi, scalar2=math.pi, op0=ALU.max, op1=ALU.min),
                 reads=["tq"], writes=["tq"])
            P.op("act", (lambda w=which: nc.scalar.activation(out=cs[:, w, :, :], in_=tq, func=AF.Sin)), reads=["tq"], writes=["cs"])
        P.barrier()

        if stop_after == "s1":
            P.emit()
            return nc
        def load_x():
            R.reset()
            xt = [R.take([128, D], F32) for _ in range(4)]
            stg = [R.take([128, 512], F32) for _ in range(4)]
            n = 0
            for tg in range(TG):
                for jj in range(4):
                    j = tg * 4 + jj
                    if KVAR != "nodma":
                        P.dma("sp", (lambda j=j, jj=jj: nc.sync.dma_start(out=xt[jj], in_=x_in[j * 128:(j + 1) * 128, :])), writes=[("xt", jj)])
                for c in range(16):
                    b, bk = bank()
                    for jj in range(4):
                        if KVAR == "notr":
                            continue
                        P.op("pe", (lambda b=b, jj=jj, c=c: nc.tensor.transpose(ps[b][:, jj * 128:(jj + 1) * 128], xt[jj][:, c * 128:(c + 1) * 128], ident)),
                             reads=[("xt", jj), "cst"], writes=[bk])
                    P.op("act", (lambda b=b, c=c, tg=tg: nc.scalar.activation(out=A[:, c, tg * 512:(tg + 1) * 512], in_=ps[b][:], func=AF.Copy)),
                         reads=[bk], writes=[("A", c, tg)])
                    s = n % 4
                    n += 1
                    P.op("dve", (lambda b=b, s=s: nc.vector.tensor_copy(out=stg[s], in_=ps[b][:])), reads=[bk], writes=[("stg", s)])
                    if KVAR != "noxres":
                        P.dma("sp", (lambda s=s, c=c, tg=tg: nc.sync.dma_start(out=xres[c][:, tg * 512:(tg + 1) * 512], in_=stg[s])),
                          reads=[("stg", s)], writes=[("xres", c, tg)])
            P.barrier()

        load_x()
        if stop_after == "setup":
            P.emit()
            return nc

        def ln_add(stb, c, r_ap, rkey):
            (b1, k1), (b2, k2) = stb
            P.op("act", (lambda: nc.scalar.activation(out=lnsq[:], in_=r_ap, func=AF.Square)), reads=[rkey], writes=["lnsq"])
            P.op("pe", (lambda: nc.tensor.matmul(ps[b1][:], lhsT=ones, rhs=r_ap, start=(c == 0), stop=(c == 15))), reads=[rkey, "cst"], writes=[k1])
            P.op("pe", (lambda: nc.tensor.matmul(ps[b2][:], lhsT=ones, rhs=lnsq[:], start=(c == 0), stop=(c == 15))), reads=["lnsq", "cst"], writes=[k2])

        def ln_finish(stb, r_of, rkeys, emit_out):
            (b1, k1), (b2, k2) = stb
            P.op("act", lambda: nc.scalar.activation(out=ps[b1][:], in_=ps[b1][:], func=AF.Copy, scale=1.0 / D), reads=[k1], writes=[k1])
            P.op("act", lambda: nc.scalar.activation(out=lnsq[:], in_=ps[b1][:], func=AF.Square), reads=[k1], writes=["lnsq"])
            P.op("dve", lambda: nc.vector.scalar_tensor_tensor(out=ps[b2][:], in0=ps[b2][:], scalar=1.0 / D, in1=lnsq[:], op0=ALU.mult, op1=ALU.subtract),
                 reads=[k2, "lnsq"], writes=[k2])
            P.op("dve", lambda: nc.vector.tensor_scalar(out=ps[b2][:], in0=ps[b2][:], scalar1=0.0, scalar2=EPS, op0=ALU.max, op1=ALU.add), reads=[k2], writes=[k2])
            P.op("act", lambda: nc.scalar.activation(out=ps[b2][:], in_=ps[b2][:], func=AF.Sqrt), reads=[k2], writes=[k2])
            P.op("dve", lambda: nc.vector.reciprocal(out=ps[b2][:], in_=ps[b2][:]), reads=[k2], writes=[k2])
            for c in range(16):
                P.op("dve", (lambda c=c: nc.vector.tensor_tensor(out=r_of(c), in0=r_of(c), in1=ps[b1][:], op=ALU.subtract)),
                     reads=[rkeys(c), k1], writes=[rkeys(c)])
                P.op("dve", (lambda c=c: nc.vector.tensor_tensor(out=r_of(c), in0=r_of(c), in1=ps[b2][:], op=ALU.mult)),
                     reads=[rkeys(c), k2], writes=[rkeys(c)])
                emit_out(c)

        groups = [[0, 1], [2, 3], [4, 5], [6, 7]]

        def exchange(l, u, src, srckey, gat, gatkey, n=128):
            ci, co = cc_i[l][u], cc_o[l][u]
            P.dma("sp", lambda: nc.sync.dma_start(out=ci[:, 0:n], in_=src), reads=[srckey], writes=[("cci", l, u)])
            P.cc(lambda: nc.gpsimd.collective_compute("AllGather", ALU.bypass, replica_groups=groups,
                                                      ins=[ci.ap().opt()], outs=[co.ap().opt()]),
                 reads=[("cci", l, u)], writes=[("cco", l, u)])
            P.dma("sp", lambda: nc.sync.dma_start(out=gat, in_=co[0:128, 0:n]), reads=[("cco", l, u)], writes=[gatkey])

        out_ops = []
        for li in range(L):
            l = layer0 + li
            last = (li == L - 1)
            wl = w_in[l].rearrange("(k p) n -> p k n", p=128)
            wol = w_out[l].rearrange("(k p) n -> p k n", p=128)
            w1l = w_ff1[l].rearrange("(k p) n -> p k n", p=128)
            w2l = w_ff2[l]

            R.reset()
            m_base = R.off
            NWS = 4
            wrings = {"H": [R.take([128, 16, 128], BF16) for _ in range(NWS)], "R": [R.take([128, 16, 128], BF16) for _ in range(NWS)]}
            wctr = {"H": 0, "R": 0}
            mstg = [R.take([128, 512], BF16) for _ in range(3)]
            mctr = [0]

            def mix_slot():
                s = mctr[0] % 3
                mctr[0] += 1
                return mstg[s], ("mstg", s)

            def mix_flush(ap, key, chunk, tg):
                P.dma("sp", (lambda: nc.sync.dma_start(out=mixd[chunk][:, tg * 512:(tg + 1) * 512], in_=ap)), reads=[key], writes=[("mixd", chunk, tg)])

            def wpiece(col0, ring="H"):
                s = wctr[ring] % NWS
                wctr[ring] += 1
                wt = wrings[ring][s]
                P.dma("pool", (lambda: nc.gpsimd.dma_start(out=wt, in_=wl[:, :, col0:col0 + 128])), writes=[("wr", ring, s)])
                return wt, ("wr", ring, s)

            def proj_fm(col0, ring="H", pool="a"):
                w, wk = wpiece(col0, ring)

                def get(tg):
                    b, bk = bank(pool)
                    for k in range(16):
                        P.op("pe", (lambda b=b, k=k, tg=tg, w=w: nc.tensor.matmul(ps[b][:], lhsT=w[:, k, :], rhs=A[:, k, tg * 512:(tg + 1) * 512],
                                                                                   start=(k == 0), stop=(k == 15))),
                             reads=[wk] + ([("A", kk, tg) for kk in range(16)] if k == 0 else []), writes=[bk])
                    return (b, bk)
                return get

            tw = [R.take([128, 512], F32) for _ in range(6)]
            m_mid = R.off
            ub = R.take([128, 4, HALO + T], BF16)
            usq = R.take([128, 4, 512], F32)
            yc = R.take([128, 4, 512], F32)
            dg = [R.take([128, 128], BF16) for _ in range(16)]
            gat = R.take([128, 128], F32)
            tail = R.take([128, 128], F32)
            P.op("dve", lambda: nc.vector.memset(tail[:, :], 0.0), writes=["tail"])
            for cc in range(4):
                pa = proj_fm(C_CA + cc * 128)
                pg = proj_fm(C_CG + cc * 128)
                for tg in range(TG):
                    (ba, ka), (bg, kg) = pa(tg), pg(tg)
                    P.op("act", (lambda bg=bg: nc.scalar.activation(out=tw[0], in_=ps[bg][:], func=AF.Sigmoid)), reads=[kg], writes=["tw0"])
                    P.op("dve", (lambda ba=ba, cc=cc, tg=tg: nc.vector.tensor_tensor(out=ub[:, cc, HALO + tg * 512:HALO + (tg + 1) * 512], in0=ps[ba][:], in1=tw[0], op=ALU.mult)),
                         reads=[ka, "tw0"], writes=[("ub", cc, tg)])
                P.op("act", (lambda cc=cc: nc.scalar.activation(out=tail[:, cc * 32:cc * 32 + HALO], in_=ub[:, cc, T:T + HALO], func=AF.Copy)),
                     reads=[("ub", cc, 3)], writes=["tail"])
            exchange(li, 12, tail[:, :], "tail", gat[:, :], "gat")
            for cc in range(4):
                P.op("dve", (lambda cc=cc: nc.vector.tensor_scalar(out=ub[:, cc, 0:HALO], in0=gat[:, cc * 32:cc * 32 + HALO], scalar1=selc, scalar2=None, op0=ALU.mult)),
                     reads=["gat", "cst"], writes=[("ubh", cc)])
            for tg in range(TG):
                cb = []
                for cc in range(4):
                    b, bk = bank()
                    cb.append((b, bk))
                    for j in range(CW):
                        s = (cc * CW + j) % 16
                        P.op("act", (lambda s=s, cc=cc, j=j: nc.scalar.activation(out=dg[s], in_=ident, func=AF.Identity, scale=vconv(l, cc, j))),
                             reads=["cst", "vec"], writes=[("dg", s)])
                        rk = [("ub", cc, tg)] + ([("ub", cc, tg - 1)] if tg > 0 else [("ubh", cc)])
                        P.op("pe", (lambda b=b, s=s, cc=cc, j=j, tg=tg: nc.tensor.matmul(ps[b][:], lhsT=dg[s], rhs=ub[:, cc, tg * 512 + j:tg * 512 + j + 512],
                                                                                         start=(j == 0), stop=(j == CW - 1))),
                             reads=[("dg", s)] + rk, writes=[bk])
                    P.op("act", (lambda b=b, cc=cc: nc.scalar.activation(out=yc[:, cc, :], in_=ps[b][:], func=AF.Identity, bias=vconv(l, cc, CW))),
                         reads=[bk, "vec"], writes=[("yc", cc)])
                    P.op("act", (lambda cc=cc: nc.scalar.activation(out=usq[:, cc, :], in_=yc[:, cc, :], func=AF.Square)),
                         reads=[("yc", cc)], writes=[("usq", cc)])
                b1, k1 = bank()
                b2, k2 = bank()
                for cc in range(4):
                    P.op("pe", (lambda cc=cc: nc.tensor.matmul(ps[b1][:], lhsT=ones, rhs=yc[:, cc, :], start=(cc == 0), stop=(cc == 3))),
                         reads=[("yc", cc), "cst"], writes=[k1])
                    P.op("pe", (lambda cc=cc: nc.tensor.matmul(ps[b2][:], lhsT=ones, rhs=usq[:, cc, :], start=(cc == 0), stop=(cc == 3))),
                         reads=[("usq", cc), "cst"], writes=[k2])
                mu, var = tw[1], tw[2]
                P.op("act", lambda: nc.scalar.activation(out=mu, in_=ps[b1][:], func=AF.Copy, scale=1.0 / 512), reads=[k1], writes=["tw1"])
                P.op("act", lambda: nc.scalar.activation(out=var, in_=ps[b1][:], func=AF.Square, scale=1.0 / 512), reads=[k1], writes=["tw2"])
                P.op("dve", lambda: nc.vector.scalar_tensor_tensor(out=var, in0=ps[b2][:], scalar=1.0 / 512, in1=var, op0=ALU.mult, op1=ALU.subtract),
                     reads=[k2, "tw2"], writes=["tw2"])
                P.op("dve", lambda: nc.vector.tensor_scalar(out=var, in0=var, scalar1=0.0, scalar2=EPS, op0=ALU.max, op1=ALU.add), reads=["tw2"], writes=["tw2"])
                P.op("act", lambda: nc.scalar.activation(out=var, in_=var, func=AF.Sqrt), reads=["tw2"], writes=["tw2"])
                P.op("dve", lambda: nc.vector.reciprocal(out=var, in_=var), reads=["tw2"], writes=["tw2"])
                for cc in range(4):
                    P.op("dve", (lambda cc=cc: nc.vector.tensor_tensor(out=yc[:, cc, :], in0=yc[:, cc, :], in1=mu, op=ALU.subtract)),
                         reads=[("yc", cc), "tw1"], writes=[("yc", cc)])
                    P.op("dve", (lambda cc=cc: nc.vector.tensor_tensor(out=yc[:, cc, :], in0=yc[:, cc, :], in1=var, op=ALU.mult)),
                         reads=[("yc", cc), "tw2"], writes=[("yc", cc)])
                    mo, mk = mix_slot()
                    P.op("act", (lambda: nc.scalar.activation(out=mo, in_=yc[:, cc, :], func=AF.Silu, scale=vconv(l, cc, CW + 1), bias=vconv(l, cc, CW + 2))),
                         reads=[("yc", cc), "vec"], writes=[mk])
                    mix_flush(mo, mk, cc, tg)
            P.barrier()

            if stop_after == "conv":
                P.emit()
                return nc
            R.reset(m_mid)
            qt = R.take([128, T], BF16)
            kt = R.take([128, T], BF16)
            itm = R.take([128, 32, 128], BF16)
            sloc = R.take([128, 32, 128], BF16)
            khat = R.take([128, 32, 128], BF16)
            qg = R.take([128, 512], BF16)
            khf = qg
            pm = R.take([128, 512], BF16)
            Sf = R.take([128, 128], F32)
            sinb = R.take([128, 128], BF16)
            gath = R.take([128, 128], F32)
            hs = R.take([128, 160], F32)
            elast, eb, bv, glast = hs[:, 0:32], hs[:, 32:64], hs[:, 64:72], hs[:, 72:73]
            Gt = tw[3]
            def hgrn_gen():
                for h in range(NH):
                    lbc = lbt[:, l * 6 + h:l * 6 + h + 1]
                    omc = oml[:, l * 6 + h:l * 6 + h + 1]
                    pq = proj_fm(C_HQ + h * 128)
                    pf = proj_fm(C_HF + h * 128)
                    P.op("dve", lambda: nc.vector.memset(sloc[:, 0, :], 0.0), writes=[("sloc", 0)])
                    P.op("dve", lambda: nc.vector.memset(Sf[:, :], 0.0), writes=["Sf"])
                    for tg in range(TG):
                        (bq, kq), (bf_, kf) = pq(tg), pf(tg)
                        P.op("act", (lambda bq=bq: nc.scalar.activation(out=tw[0], in_=ps[bq][:], func=AF.Silu)), reads=[kq], writes=["tw0"])
                        P.op("act", (lambda b=bf_: nc.scalar.activation(out=tw[1], in_=ps[b][:], func=AF.Sigmoid)), reads=[kf], writes=["tw1"])
                        P.op("dve", lambda: nc.vector.tensor_scalar(out=tw[1], in0=tw[1], scalar1=omc, scalar2=lbc, op0=ALU.mult, op1=ALU.add),
                             reads=["tw1", "lbt", "oml"], writes=["tw1"])
                        P.op("act", lambda: nc.scalar.activation(out=tw[2], in_=tw[1], func=AF.Ln), reads=["tw1"], writes=["tw2"])
                        P.op("dve", lambda: nc.vector.tensor_scalar(out=tw[1], in0=tw[1], scalar1=-1.0, scalar2=1.0, op0=ALU.mult, op1=ALU.add),
                             reads=["tw1"], writes=["tw1"])
                        if tg == 0:
                            P.op("dve", lambda: nc.vector.tensor_tensor_scan(out=Gt, data0=ones[:, 0:1].to_broadcast([128, 512]), data1=tw[2], initial=0.0, op0=ALU.mult, op1=ALU.add),
                                 reads=["tw2", "cst"], writes=["Gt"])
                            P.op("dve", lambda: nc.vector.memset(bv[:, 0:1], 0.0), writes=["bv"])
                        else:
                            P.op("dve", lambda: nc.vector.tensor_copy(out=bv[:, 0:1], in_=glast), reads=["glast"], writes=["bv"])
                            P.op("dve", lambda: nc.vector.tensor_tensor_scan(out=Gt, data0=ones[:, 0:1].to_broadcast([128, 512]), data1=tw[2], initial=glast, op0=ALU.mult, op1=ALU.add),
                                 reads=["tw2", "cst", "glast"], writes=["Gt"])
                        Gv = Gt.rearrange("p (c i) -> p c i", i=64)
                        P.op("dve", lambda: nc.vector.tensor_copy(out=bv[:, 1:8], in_=Gv[:, 0:7, 63]), reads=["Gt"], writes=["bv"])
                        P.op("act", lambda: nc.scalar.activation(out=glast, in_=Gt[:, 511:512], func=AF.Copy), reads=["Gt"], writes=["glast"])
                        P.op("act", (lambda tg=tg: nc.scalar.activation(out=eb[:, tg * 8:tg * 8 + 8], in_=bv, func=AF.Exp)), reads=["bv"], writes=["eb"])
                        P.op("dve", lambda: nc.vector.tensor_tensor(out=Gv, in0=Gv, in1=bv.unsqueeze(2).to_broadcast([128, 8, 64]), op=ALU.subtract),
                             reads=["Gt", "bv"], writes=["Gt"])
                        P.op("act", lambda: nc.scalar.activation(out=tw[2], in_=Gt, func=AF.Exp), reads=["Gt"], writes=["tw2"])
                        P.op("act", lambda: nc.scalar.activation(out=tw[4], in_=Gt, func=AF.Exp, scale=-1.0), reads=["Gt"], writes=["tw4"])
                        e3 = tw[2].rearrange("p (c i) -> p c i", i=64)
                        P.op("act", (lambda tg=tg: nc.scalar.activation(out=elast[:, tg * 8:tg * 8 + 8], in_=e3[:, :, 63], func=AF.Copy)), reads=["tw2"], writes=["elast"])
                        P.op("dve", (lambda tg=tg: nc.vector.tensor_tensor(out=qt[:, tg * 512:(tg + 1) * 512], in0=tw[0], in1=tw[2], op=ALU.mult)),
                             reads=["tw0", "tw2"], writes=[("qt", tg)])
                        P.op("dve", lambda: nc.vector.tensor_tensor(out=tw[4], in0=tw[1], in1=tw[4], op=ALU.mult), reads=["tw1", "tw4"], writes=["tw4"])
                        P.op("act", (lambda tg=tg: nc.scalar.activation(out=kt[:, tg * 512:(tg + 1) * 512], in_=tw[4], func=AF.Copy)), reads=["tw4"], writes=[("kt", tg)])
                        k3 = tw[4].rearrange("p (c i) -> p c i", i=64)
                        P.op("dve", (lambda tg=tg: nc.vector.tensor_tensor(out=khf.rearrange("p (c i) -> p c i", i=64), in0=k3,
                                                                            in1=elast[:, tg * 8:tg * 8 + 8].unsqueeze(2).to_broadcast([128, 8, 64]), op=ALU.mult)),
                             reads=["tw4", "elast"], writes=["qg"])
                        for half in range(2):
                            b, bk = bank()
                            pv = ps[b][:].bitcast(BF16)
                            for jj in range(4):
                                j = half * 4 + jj
                                P.op("pe", (lambda pv=pv, jj=jj, j=j: nc.tensor.transpose(pv[0:64, jj * 128:(jj + 1) * 128], khf[:, j * 64:(j + 1) * 64], identb[:])),
                                     reads=["qg", "identb"], writes=[bk])
                            P.op("act", (lambda pv=pv, tg=tg, half=half: nc.scalar.activation(
                                out=khat[0:64, tg * 8 + half * 4:tg * 8 + half * 4 + 4, :], in_=pv[0:64, 0:512].rearrange("p (c d) -> p c d", d=128), func=AF.Copy)),
                                reads=[bk], writes=[("khat", tg)])
                        yield
                    wi, wik = wpiece(C_HI + h * 128)
                    for c4 in range(8):
                        b, bk = bank()
                        for jj in range(4):
                            c = c4 * 4 + jj
                            tg = c // 8
                            for k in range(16):
                                P.op("pe", (lambda b=b, jj=jj, c=c, k=k: nc.tensor.matmul(ps[b][0:64, jj * 128:(jj + 1) * 128], lhsT=A[:, k, c * 64:(c + 1) * 64], rhs=wi[:, k, :],
                                                                                           start=(k == 0), stop=(k == 15))),
                                     reads=[wik] + ([("A", kk, tg) for kk in range(16)] if k == 0 else []), writes=[bk])
                        P.op("act", (lambda b=b, c4=c4: nc.scalar.activation(out=itm[0:64, c4 * 4:c4 * 4 + 4, :], in_=ps[b][0:64, :].rearrange("p (c d) -> p c d", d=128), func=AF.Copy)),
                             reads=[bk], writes=[("itm", c4)])
                        yield
                    for c in range(32):
                        if c % 8 == 0 and c > 0:
                            yield
                        b, bk = bank()
                        P.op("pe", (lambda b=b, c=c: nc.tensor.matmul(ps[b][:, 0:128], lhsT=khat[0:64, c, :], rhs=itm[0:64, c, :], start=True, stop=True)),
                             reads=[("khat", c // 8), ("itm", c // 4)], writes=[bk])
                        P.op("dve", (lambda b=b, c=c: nc.vector.scalar_tensor_tensor(out=Sf[:, :], in0=Sf[:, :], scalar=elast[:, c:c + 1], in1=ps[b][:, 0:128], op0=ALU.mult, op1=ALU.add)),
                             reads=[bk, "Sf", "elast"], writes=["Sf"])
                        if c < 31:
                            P.op("act", (lambda c=c: nc.scalar.activation(out=sloc[:, c + 1, :], in_=Sf[:, :], func=AF.Copy)), reads=["Sf"], writes=[("sloc", c + 1)])
                    exchange(li, h, Sf[:, :], "Sf", gath[:, :], "gath")
                    yield
                    P.op("dve", lambda: nc.vector.tensor_scalar(out=sinb[:, :], in0=gath[:, :], scalar1=selc, scalar2=None, op0=ALU.mult), reads=["gath", "cst"], writes=["sinb"])
                    pgate = proj_fm(C_HG + h * 128)
                    for tg in range(TG):
                        bs, ks = bank()
                        for j in range(8):
                            c = tg * 8 + j
                            P.op("pe", (lambda j=j, c=c: nc.tensor.matmul(ps[bs][0:64, j * 64:(j + 1) * 64], lhsT=kt[:, c * 64:(c + 1) * 64], rhs=qt[:, c * 64:(c + 1) * 64], start=True, stop=True)),
                                 reads=[("kt", tg), ("qt", tg)], writes=[ks])
                        P.op("dve", lambda: nc.vector.tensor_tensor(out=pm[0:64, :].rearrange("p (c i) -> p c i", i=64), in0=ps[bs][0:64, :].rearrange("p (c i) -> p c i", i=64),
                                                                     in1=tri[0:64, 0:64].unsqueeze(1).to_broadcast([64, 8, 64]), op=ALU.mult),
                             reads=[ks, "cst"], writes=["pm"])
                        P.op("dve", (lambda tg=tg: nc.vector.tensor_tensor(out=qg.rearrange("p (c i) -> p c i", i=64), in0=qt[:, tg * 512:(tg + 1) * 512].rearrange("p (c i) -> p c i", i=64),
                                                                            in1=eb[:, tg * 8:tg * 8 + 8].unsqueeze(2).to_broadcast([128, 8, 64]), op=ALU.mult)),
                             reads=[("qt", tg), "eb"], writes=["qg"])
                        bo, ko = bank()
                        for j in range(8):
                            c = tg * 8 + j
                            osl = ps[bo][:, j * 64:(j + 1) * 64]
                            P.op("pe", (lambda osl=osl, c=c, j=j: nc.tensor.matmul(osl, lhsT=itm[0:64, c, :], rhs=pm[0:64, j * 64:(j + 1) * 64], start=True, stop=False)),
                                 reads=[("itm", c // 4), "pm"], writes=[ko])
                            P.op("pe", (lambda osl=osl, c=c: nc.tensor.matmul(osl, lhsT=sloc[:, c, :], rhs=qt[:, c * 64:(c + 1) * 64], start=False, stop=False)),
                                 reads=[("sloc", c), ("qt", tg)], writes=[ko])
                            P.op("pe", (lambda osl=osl, j=j: nc.tensor.matmul(osl, lhsT=sinb[:, :], rhs=qg[:, j * 64:(j + 1) * 64], start=False, stop=True)),
                                 reads=["sinb", "qg"], writes=[ko])
                        P.op("act", lambda: nc.scalar.activation(out=tw[5], in_=ps[bo][:], func=AF.Square), reads=[ko], writes=["tw5"])
                        bn_, kn = bank()
                        P.op("pe", lambda: nc.tensor.matmul(ps[bn_][:], lhsT=ones, rhs=tw[5], start=True, stop=True), reads=["tw5", "cst"], writes=[kn])
                        P.op("dve", lambda: nc.vector.tensor_scalar(out=tw[5], in0=ps[bn_][:], scalar1=1.0 / 128, scalar2=EPS, op0=ALU.mult, op1=ALU.add), reads=[kn], writes=["tw5"])
                        P.op("act", lambda: nc.scalar.activation(out=tw[5], in_=tw[5], func=AF.Sqrt), reads=["tw5"], writes=["tw5"])
                        P.op("dve", lambda: nc.vector.reciprocal(out=tw[5], in_=tw[5]), reads=["tw5"], writes=["tw5"])
                        P.op("dve", lambda: nc.vector.tensor_tensor(out=tw[5], in0=ps[bo][:], in1=tw[5], op=ALU.mult), reads=[ko, "tw5"], writes=["tw5"])
                        bg, kg = pgate(tg)
                        P.op("act", (lambda bg=bg: nc.scalar.activation(out=tw[0], in_=ps[bg][:], func=AF.Sigmoid)), reads=[kg], writes=["tw0"])
                        mo, mk = mix_slot()
                        P.op("dve", (lambda: nc.vector.scalar_tensor_tensor(out=mo, in0=tw[5], scalar=vhead(1, l, h), in1=tw[0], op0=ALU.mult, op1=ALU.mult)),
                             reads=["tw5", "tw0", "vec"], writes=[mk])
                        mix_flush(mo, mk, 4 + h, tg)
                        yield
            rw = [R.take([128, 512], F32) for _ in range(4)]
            qf = R.take([128, T], BF16)
            kf_ = R.take([128, T], BF16)
            qd = R.take([128, T], BF16)
            ktm = R.take([128, NT, 128], BF16)
            vtm = R.take([128, NT, 128], BF16)
            slr = R.take([128, NT, 128], BF16)
            rotL = [R.take([128, 2, 128], F32) for _ in range(2)]
            rtL = [[R.take([128, 2, 64], F32) for _ in range(4)] for _ in range(2)]
            rbfL = [R.take([128, 2, 128], BF16) for _ in range(2)]
            pmr = R.take([128, 512], BF16)
            qgr = R.take([128, 128], BF16)
            dth = R.take([128, 128], F32)
            rdh = R.take([128, 128], F32)
            kdc = R.take([128, 2], F32)
            Sr = R.take([128, 128], F32)
            sinr = R.take([128, 128], BF16)
            gatr = R.take([128, 128], F32)
            osb = R.take([128, 512], F32)
            def ret_gen():
                for h in range(NH):
                    lg = LOGG[h]
                    P.op("act", (lambda lg=lg: nc.scalar.activation(out=dth[:, :], in_=tms, func=AF.Exp, scale=lg)), reads=["cst"], writes=["dth"])
                    P.op("dve", lambda: nc.vector.tensor_tensor(out=dth[:, :], in0=dth[:, :], in1=tri, op=ALU.mult), reads=["dth", "cst"], writes=["dth"])
                    P.op("act", (lambda lg=lg: nc.scalar.activation(out=rdh[:, :], in_=iot1, func=AF.Exp, scale=lg)), reads=["cst"], writes=["rdh"])
                    P.op("act", (lambda lg=lg: nc.scalar.activation(out=kdc[:, 0:1], in_=colr, func=AF.Exp, scale=lg)), reads=["cst"], writes=["kdc"])
                    P.op("act", lambda: nc.scalar.mul(out=kdc[:, 1:2], in_=kdc[:, 0:1], mul=128.0 ** -0.5), reads=["kdc"], writes=["kdc2"])
                    w3 = [wpiece(c0 + h * 128, "R") for c0 in (C_RQ, C_RK, C_RV)]
                    P.op("dve", lambda: nc.vector.memset(slr[:, 0, :], 0.0), writes=[("slr", 0)])
                    P.op("dve", lambda: nc.vector.memset(Sr[:, :], 0.0), writes=["Sr"])
                    for j in range(NT):
                        tg = j // 4
                        b, bk = bank("b")
                        for pi in range(3):
                            wq, wqk = w3[pi]
                            for k in range(16):
                                P.op("pe", (lambda b=b, j=j, k=k, pi=pi, wq=wq: nc.tensor.matmul(ps[b][:, pi * 128:(pi + 1) * 128], lhsT=A[:, k, j * 128:(j + 1) * 128], rhs=wq[:, k, :], start=(k == 0), stop=(k == 15))),
                                     reads=[wqk] + ([("A", kk, tg) for kk in range(16)] if k == 0 else []), writes=[bk])
                        x4 = ps[b][:, 0:256].rearrange("p (w u i) -> p w u i", w=2, u=2)
                        cosb = cs[:, 0, j, :].unsqueeze(1).to_broadcast([128, 2, 64])
                        sinb_ = cs[:, 1, j, :].unsqueeze(1).to_broadcast([128, 2, 64])
                        pp = j % 2
                        rot, rbf = rotL[pp], rbfL[pp]
                        t1, t2, t3, t4 = rtL[pp]
                        kr, kb = ("rot", pp), ("rbf", pp)
                        r4 = rot.rearrange("p w (u i) -> p w u i", u=2)
                        P.op("dve", (lambda: nc.vector.tensor_tensor(out=t1, in0=x4[:, :, 0, :], in1=cosb, op=ALU.mult)), reads=[bk, "cs"], writes=[("rt", pp, 1)])
                        P.op("dve", (lambda: nc.vector.tensor_tensor(out=t2, in0=x4[:, :, 1, :], in1=sinb_, op=ALU.mult)), reads=[bk, "cs"], writes=[("rt", pp, 2)])
                        P.op("dve", (lambda: nc.vector.tensor_tensor(out=t3, in0=x4[:, :, 0, :], in1=sinb_, op=ALU.mult)), reads=[bk, "cs"], writes=[("rt", pp, 3)])
                        P.op("dve", (lambda: nc.vector.tensor_tensor(out=t4, in0=x4[:, :, 1, :], in1=cosb, op=ALU.mult)), reads=[bk, "cs"], writes=[("rt", pp, 4)])
                        P.op("act", (lambda: nc.scalar.activation(out=vtm[:, j, :], in_=ps[b][:, 256:384], func=AF.Copy)), reads=[bk], writes=[("vtm", j)])
                        P.op("dve", (lambda: nc.vector.tensor_tensor(out=r4[:, :, 0, :], in0=t1, in1=t2, op=ALU.subtract)), reads=[("rt", pp, 1), ("rt", pp, 2)], writes=[kr])
                        P.op("dve", (lambda: nc.vector.tensor_tensor(out=r4[:, :, 1, :], in0=t3, in1=t4, op=ALU.add)), reads=[("rt", pp, 3), ("rt", pp, 4)], writes=[kr])
                        P.op("act", lambda: nc.scalar.activation(out=rbf[:, 0, :], in_=rot[:, 0, :], func=AF.Copy), reads=[kr], writes=[kb])
                        P.op("act", lambda: nc.scalar.activation(out=rbf[:, 1, :], in_=rot[:, 1, :], func=AF.Copy, scale=128.0 ** -0.5), reads=[kr], writes=[kb])
                        P.op("act", (lambda: nc.scalar.activation(out=ktm[:, j, :], in_=rot[:, 1, :], func=AF.Identity, scale=kdc[:, 1:2])), reads=[kr, "kdc2"], writes=[("ktm", j)])
                        bt, kt_ = bank("b")
                        pv = ps[bt][:].bitcast(BF16)
                        P.op("pe", (lambda: nc.tensor.transpose(pv[:, 0:128], rbf[:, 0, :], identb[:])), reads=[kb, "identb"], writes=[kt_])
                        P.op("pe", (lambda: nc.tensor.transpose(pv[:, 128:256], rbf[:, 1, :], identb[:])), reads=[kb, "identb"], writes=[kt_])
                        P.op("pe", (lambda: nc.tensor.matmul(ps[bt][:, 256:384], lhsT=ktm[:, j, :], rhs=vtm[:, j, :], start=True, stop=True)),
                             reads=[("ktm", j), ("vtm", j)], writes=[kt_])
                        P.op("act", (lambda: nc.scalar.activation(out=qf[:, j * 128:(j + 1) * 128], in_=pv[:, 0:128], func=AF.Copy)), reads=[kt_], writes=[("qf", j)])
                        P.op("act", (lambda: nc.scalar.activation(out=kf_[:, j * 128:(j + 1) * 128], in_=pv[:, 128:256], func=AF.Copy)), reads=[kt_], writes=[("kf", j)])
                        P.op("dve", (lambda: nc.vector.tensor_tensor(out=qd[:, j * 128:(j + 1) * 128], in0=pv[:, 0:128], in1=rdh[:, :], op=ALU.mult)),
                             reads=[kt_, "rdh"], writes=[("qd", j)])
                        P.op("dve", (lambda: nc.vector.scalar_tensor_tensor(out=Sr[:, :], in0=Sr[:, :], scalar=math.exp(lg * 128), in1=ps[bt][:, 256:384], op0=ALU.mult, op1=ALU.add)),
                             reads=[kt_, "Sr"], writes=["Sr"])
                        if j < NT - 1:
                            P.op("act", (lambda: nc.scalar.activation(out=slr[:, j + 1, :], in_=Sr[:, :], func=AF.Copy)), reads=["Sr"], writes=[("slr", j + 1)])
                        yield
                    exchange(li, 6 + h, Sr[:, :], "Sr", gatr[:, :], "gatr")
                    yield
                    P.op("dve", lambda: nc.vector.tensor_scalar(out=sinr[:, :], in0=gatr[:, :], scalar1=selc, scalar2=None, op0=ALU.mult), reads=["gatr", "cst"], writes=["sinr"])
                    pgate = proj_fm(C_RG + h * 128, "R", "b")
                    for tg in range(TG):
                        bs, ks = bank("b")
                        for jj in range(4):
                            j = tg * 4 + jj
                            P.op("pe", (lambda jj=jj, j=j: nc.tensor.matmul(ps[bs][:, jj * 128:(jj + 1) * 128], lhsT=kf_[:, j * 128:(j + 1) * 128], rhs=qf[:, j * 128:(j + 1) * 128], start=True, stop=True)),
                                 reads=[("kf", j), ("qf", j)], writes=[ks])
                        P.op("dve", lambda: nc.vector.tensor_tensor(out=pmr.rearrange("p (c i) -> p c i", i=128), in0=ps[bs][:].rearrange("p (c i) -> p c i", i=128),
                                                                     in1=dth[:, :].unsqueeze(1).to_broadcast([128, 4, 128]), op=ALU.mult),
                             reads=[ks, "dth"], writes=["pmr"])
                        bo, ko = bank("b")
                        for jj in range(4):
                            j = tg * 4 + jj
                            osl = ps[bo][:, jj * 128:(jj + 1) * 128]
                            P.op("dve", (lambda j=j, lg=lg: nc.vector.tensor_scalar(out=qgr[:, :], in0=qd[:, j * 128:(j + 1) * 128], scalar1=math.exp(lg * 128 * j), scalar2=None, op0=ALU.mult)),
                                 reads=[("qd", j)], writes=["qgr"])
                            P.op("pe", (lambda osl=osl, j=j, jj=jj: nc.tensor.matmul(osl, lhsT=vtm[:, j, :], rhs=pmr[:, jj * 128:(jj + 1) * 128], start=True, stop=False)),
                                 reads=[("vtm", j), "pmr"], writes=[ko])
                            P.op("pe", (lambda osl=osl, j=j: nc.tensor.matmul(osl, lhsT=slr[:, j, :], rhs=qd[:, j * 128:(j + 1) * 128], start=False, stop=False)),
                                 reads=[("slr", j), ("qd", j)], writes=[ko])
                            P.op("pe", (lambda osl=osl: nc.tensor.matmul(osl, lhsT=sinr[:, :], rhs=qgr[:, :], start=False, stop=True)), reads=["sinr", "qgr"], writes=[ko])
                        P.op("act", lambda: nc.scalar.activation(out=osb[:, :], in_=ps[bo][:], func=AF.Copy), reads=[ko], writes=["osb"])
                        P.op("act", lambda: nc.scalar.activation(out=rw[3], in_=ps[bo][:], func=AF.Square), reads=[ko], writes=["rw3"])
                        b1, k1 = bank("b")
                        b2, k2 = bank("b")
                        P.op("pe", lambda: nc.tensor.matmul(ps[b1][:], lhsT=ones, rhs=osb[:, :], start=True, stop=True), reads=["osb", "cst"], writes=[k1])
                        P.op("pe", lambda: nc.tensor.matmul(ps[b2][:], lhsT=ones, rhs=rw[3], start=True, stop=True), reads=["rw3", "cst"], writes=[k2])
                        mu, var = rw[1], rw[2]
                        P.op("act", lambda: nc.scalar.activation(out=mu, in_=ps[b1][:], func=AF.Copy, scale=1.0 / 128), reads=[k1], writes=["rw1"])
                        P.op("act", lambda: nc.scalar.activation(out=var, in_=ps[b1][:], func=AF.Square, scale=1.0 / 128), reads=[k1], writes=["rw2"])
                        P.op("dve", lambda: nc.vector.scalar_tensor_tensor(out=var, in0=ps[b2][:], scalar=1.0 / 128, in1=var, op0=ALU.mult, op1=ALU.subtract), reads=[k2, "rw2"], writes=["rw2"])
                        P.op("dve", lambda: nc.vector.tensor_scalar(out=var, in0=var, scalar1=0.0, scalar2=EPS, op0=ALU.max, op1=ALU.add), reads=["rw2"], writes=["rw2"])
                        P.op("act", lambda: nc.scalar.activation(out=var, in_=var, func=AF.Sqrt), reads=["rw2"], writes=["rw2"])
                        P.op("dve", lambda: nc.vector.reciprocal(out=var, in_=var), reads=["rw2"], writes=["rw2"])
                        P.op("dve", lambda: nc.vector.tensor_tensor(out=osb[:, :], in0=osb[:, :], in1=mu, op=ALU.subtract), reads=["osb", "rw1"], writes=["osb"])
                        P.op("dve", lambda: nc.vector.tensor_tensor(out=osb[:, :], in0=osb[:, :], in1=var, op=ALU.mult), reads=["osb", "rw2"], writes=["osb"])
                        P.op("act", (lambda h=h: nc.scalar.activation(out=osb[:, :], in_=osb[:, :], func=AF.Identity, scale=vhead(2, l, h), bias=vhead(3, l, h))),
                             reads=["osb", "vec"], writes=["osb"])
                        bg, kg = pgate(tg)
                        P.op("act", (lambda bg=bg: nc.scalar.activation(out=rw[0], in_=ps[bg][:], func=AF.Silu)), reads=[kg], writes=["rw0"])
                        mo, mk = mix_slot()
                        P.op("dve", (lambda: nc.vector.tensor_tensor(out=mo, in0=osb[:, :], in1=rw[0], op=ALU.mult)), reads=["osb", "rw0"], writes=[mk])
                        mix_flush(mo, mk, 10 + h, tg)
                        yield
            gens = [hgrn_gen(), ret_gen()]
            while gens:
                for g in list(gens):
                    try:
                        next(g)
                    except StopIteration:
                        gens.remove(g)
            P.barrier()
            R.reset(m_base)
            rt2 = [R.take([128, 16, 512], F32) for _ in range(2)]
            mixs = R.take([128, 16, 1024], BF16)
            wor = [R.take([128, 16, 256], BF16) for _ in range(2)]
            xr = [R.take([128, 512], F32) for _ in range(3)]
            so = [R.take([128, 512], F32) for _ in range(3)]
            n_w = 0
            n_s = 0
            n_x = 0
            for pp in range(2):
                P.dma("sp", (lambda: nc.sync.dma_start(out=mixs, in_=mixd[:, :, pp * 1024:(pp + 1) * 1024].rearrange("c p t -> p c t"))),
                      reads=[("mixd", cc_, pp * 2 + tl_) for cc_ in range(16) for tl_ in range(2)], writes=["mixs"])
                stbs = [stat_banks(), stat_banks()]
                pend_ln = []
                for c2 in range(8):
                    s = n_w % 2
                    n_w += 1
                    P.dma("pool", (lambda: nc.gpsimd.dma_start(out=wor[s], in_=wol[:, :, c2 * 256:(c2 + 1) * 256])), writes=[("wor", s)])
                    for cc in range(2):
                        c = c2 * 2 + cc
                        for tl in range(2):
                            tg = pp * 2 + tl
                            xs = n_x % 3
                            n_x += 1
                            P.dma("sp", (lambda: nc.sync.dma_start(out=xr[xs], in_=xres[c][:, tg * 512:(tg + 1) * 512])), reads=[("xres", c, tg)], writes=[("xr", xs)])
                            b, bk = bank()
                            for k in range(16):
                                P.op("pe", (lambda: nc.tensor.matmul(ps[b][:], lhsT=wor[s][:, k, cc * 128:(cc + 1) * 128], rhs=mixs[:, k, tl * 512:(tl + 1) * 512],
                                                                     start=(k == 0), stop=(k == 15))),
                                     reads=[("wor", s), "mixs"], writes=[bk])
                            P.op("dve", (lambda: nc.vector.scalar_tensor_tensor(out=rt2[tl][:, c, :], in0=xr[xs], scalar=ALPHA, in1=ps[b][:], op0=ALU.mult, op1=ALU.add)),
                                 reads=[bk, ("xr", xs)], writes=[("rt", tl, c)])
                            pend_ln.append((stbs[tl], c, rt2[tl][:, c, :], ("rt", tl, c)))
                            if len(pend_ln) > 1:
                                ln_add(*pend_ln.pop(0))
                while pend_ln:
                    ln_add(*pend_ln.pop(0))
                for tl in range(2):
                    tg = pp * 2 + tl

                    def emit_out(c, tg=tg, tl=tl):
                        nonlocal n_s
                        s = n_s % 3
                        n_s += 1
                        P.op("act", (lambda: nc.scalar.activation(out=A[:, c, tg * 512:(tg + 1) * 512], in_=rt2[tl][:, c, :], func=AF.Identity, scale=vln(0, l, c), bias=vln(1, l, c))),
                             reads=[("rt", tl, c), "vec"], writes=[("A", c, tg)])
                        P.op("act", (lambda: nc.scalar.activation(out=so[s], in_=rt2[tl][:, c, :], func=AF.Identity, scale=agcol(0, l, c), bias=agcol(1, l, c))),
                             reads=[("rt", tl, c), "agb"], writes=[("so", s)])
                        P.dma("sp", (lambda: nc.sync.dma_start(out=x1res[c][:, tg * 512:(tg + 1) * 512], in_=so[s])), reads=[("so", s)], writes=[("x1res", c, tg)])
                    ln_finish(stbs[tl], (lambda c, tl=tl: rt2[tl][:, c, :]), (lambda c, tl=tl: ("rt", tl, c)), emit_out)
            P.barrier()

            if stop_after == "O":
                P.emit()
                return nc
            R.reset()
            w1r = [R.take([128, 16, 256], BF16) for _ in range(3)]
            ft = [R.take([128, T], BF16) for _ in range(3)]
            rl = [R.take([128, 512], F32) for _ in range(3)]
            n_r = 0
            for pi in range(32):
                s = pi % 3
                P.dma("pool", (lambda s=s, pi=pi: nc.gpsimd.dma_start(out=w1r[s], in_=w1l[:, :, pi * 256:(pi + 1) * 256])), writes=[("w1r", s)])
                for mm in range(2):
                    m = pi * 2 + mm
                    fs = m % 3
                    for tg in range(TG):
                        b, bk = bank()
                        for k in range(16):
                            P.op("pe", (lambda b=b, s=s, k=k, mm=mm, tg=tg: nc.tensor.matmul(ps[b][:], lhsT=w1r[s][:, k, mm * 128:(mm + 1) * 128], rhs=A[:, k, tg * 512:(tg + 1) * 512],
                                                                                             start=(k == 0), stop=(k == 15))),
                                 reads=[("w1r", s)] + ([("A", kk, tg) for kk in range(16)] if k == 0 else []), writes=[bk])
                        rs = n_r % 3
                        n_r += 1
                        P.op("act", (lambda b=b, rs=rs: nc.scalar.activation(out=rl[rs], in_=ps[b][:], func=AF.Relu)), reads=[bk], writes=[("rl", rs)])
                        P.op("dve", (lambda b=b, rs=rs, fs=fs, tg=tg: nc.vector.tensor_tensor(out=ft[fs][:, tg * 512:(tg + 1) * 512], in0=rl[rs], in1=ps[b][:], op=ALU.mult)),
                             reads=[bk, ("rl", rs)], writes=[("ft", fs, tg)])
                    P.dma("sp", (lambda m=m, fs=fs: nc.sync.dma_start(out=fd[m], in_=ft[fs])), reads=[("ft", fs, tg) for tg in range(TG)], writes=[("fd", m)])
            P.barrier()

            if stop_after == "F1":
                P.emit()
                return nc
            R.reset()
            acc = R.take([128, 16, T], F32)
            Ab = A[:].rearrange("p c t -> p (c t)")
            fr = [Ab[:, i * 8192:(i + 1) * 8192].rearrange("p (k t) -> p k t", k=4) for i in range(2)]
            w2r = [Ab[:, 16384 + i * 8192:16384 + (i + 1) * 8192].rearrange("p (k t) -> p k t", k=4) for i in range(2)]
            for c in range(16):
                P.dma("sp", (lambda c=c: nc.sync.dma_start(out=acc[:, c, :], in_=x1res[c])), reads=[("x1res", c, tg) for tg in range(TG)], writes=[("acc", c, tg) for tg in range(TG)])
            def f2_group(s, c, tg):
                b, bk = bank()
                for kk in range(4):
                    P.op("pe", (lambda b=b, s=s, kk=kk, c=c, tg=tg: nc.tensor.matmul(ps[b][:], lhsT=w2r[s][:, kk, c * 128:(c + 1) * 128], rhs=fr[s][:, kk, tg * 512:(tg + 1) * 512],
                                                                                     start=(kk == 0), stop=(kk == 3))),
                         reads=[("w2r", s), ("fr", s)], writes=[bk])
                P.op("dve", (lambda b=b, c=c, tg=tg: nc.vector.tensor_tensor(out=acc[:, c, tg * 512:(tg + 1) * 512], in0=acc[:, c, tg * 512:(tg + 1) * 512], in1=ps[b][:], op=ALU.add)),
                     reads=[bk, ("acc", c, tg)], writes=[("acc", c, tg)])

            for jb in range(16):
                s = jb % 2
                P.dma("sp", (lambda s=s, jb=jb: nc.sync.dma_start(out=fr[s], in_=fd[jb * 4:(jb + 1) * 4].rearrange("k p t -> p k t"))),
                      reads=[("fd", jb * 4 + kk) for kk in range(4)], writes=[("fr", s)])
                P.dma("pool", (lambda s=s, jb=jb: nc.gpsimd.dma_start(out=w2r[s], in_=w2l[jb * 512:(jb + 1) * 512, :].rearrange("(k p) n -> p k n", p=128))), writes=[("w2r", s)])
                if jb < 15:
                    for c in range(16):
                        for tg in range(TG):
                            f2_group(s, c, tg)
                else:
                    for tg in range(TG):
                        stb = stat_banks()
                        pend_ln = []
                        for c in range(16):
                            f2_group(s, c, tg)
                            pend_ln.append((stb, c, acc[:, c, tg * 512:(tg + 1) * 512], ("acc", c, tg)))
                            if len(pend_ln) > 2:
                                ln_add(*pend_ln.pop(0))
                        while pend_ln:
                            ln_add(*pend_ln.pop(0))

                        def emit_out2(c, tg=tg):
                            accs = acc[:, c, tg * 512:(tg + 1) * 512]
                            P.op("act", (lambda c=c, tg=tg, accs=accs: nc.scalar.activation(out=accs, in_=accs, func=AF.Identity, scale=vln(2, l, c), bias=vln(3, l, c))),
                                 reads=[("acc", c, tg), "vec"], writes=[("acc", c, tg)])
                            if not (last and final_out):
                                P.dma("sp", (lambda c=c, tg=tg, accs=accs: nc.sync.dma_start(out=xres[c][:, tg * 512:(tg + 1) * 512], in_=accs)),
                                      reads=[("acc", c, tg)], writes=[("xres", c, tg)])
                        ln_finish(stb, lambda c, tg=tg: acc[:, c, tg * 512:(tg + 1) * 512], lambda c, tg=tg: ("acc", c, tg), emit_out2)
            P.barrier()
            if not (last and final_out):
                for tg in range(TG):
                    for c in range(16):
                        eng = "dve" if (c % 2 == 0) else "act"
                        if eng == "dve":
                            P.op("dve", (lambda c=c, tg=tg: nc.vector.tensor_copy(out=A[:, c, tg * 512:(tg + 1) * 512], in_=acc[:, c, tg * 512:(tg + 1) * 512])),
                                 reads=[("acc", c, tg)], writes=[("A", c, tg)])
                        else:
                            P.op("act", (lambda c=c, tg=tg: nc.scalar.activation(out=A[:, c, tg * 512:(tg + 1) * 512], in_=acc[:, c, tg * 512:(tg + 1) * 512], func=AF.Copy)),
                                 reads=[("acc", c, tg)], writes=[("A", c, tg)])
            P.barrier()
            if last and final_out:
                Af = A[:].rearrange("p c t -> p (c t)").bitcast(F32)
                ot = [Af[:, i * 2048:(i + 1) * 2048] for i in range(4)]
                n = 0
                for j in range(NT):
                    s = n % 4
                    n += 1
                    for c4 in range(4):
                        b, bk = bank()
                        for cc in range(4):
                            c = c4 * 4 + cc
                            P.op("pe", (lambda b=b, cc=cc, c=c, j=j: nc.tensor.transpose(ps[b][:, cc * 128:(cc + 1) * 128], acc[:, c, j * 128:(j + 1) * 128], ident)),
                                 reads=[("acc", c, j // 4), "cst"], writes=[bk])
                        P.op("act" if c4 % 2 else "dve",
                             (lambda b=b, s=s, c4=c4: (nc.scalar.activation(out=ot[s][:, c4 * 512:(c4 + 1) * 512], in_=ps[b][:], func=AF.Copy) if c4 % 2
                                                        else nc.vector.tensor_copy(out=ot[s][:, c4 * 512:(c4 + 1) * 512], in_=ps[b][:]))),
                             reads=[bk], writes=[("ot", s, c4)])
                    out_ops.append(P.dma("sp", (lambda j=j, s=s: nc.sync.dma_start(out=y_out[j * 128:(j + 1) * 128, :], in_=ot[s])),
                                         reads=[("ot", s, c4) for c4 in range(4)], writes=[("yout", j)]))
        if not final_out:
            pass
        P.emit(final_wait_ops=out_ops)
    return nc


def _host_inputs(inputs):
    f32 = np.float32
    x = np.asarray(inputs["x"], f32)
    pos = np.asarray(inputs["positions"]).astype(np.int32)
    Lh = L_ALL

    def cols(v, nchunk):
        v = np.asarray(v, f32).reshape(Lh, nchunk, 128)
        return np.ascontiguousarray(v.transpose(2, 0, 1).reshape(128, Lh * nchunk))
    parts = [cols(inputs["ln1_g"], 16), cols(inputs["ln1_b"], 16), cols(inputs["ln2_g"], 16), cols(inputs["ln2_b"], 16)]
    wdw = np.asarray(inputs["w_dw"], f32).reshape(Lh, CW, 4, 128)
    conv = np.concatenate([wdw.transpose(3, 0, 2, 1),
                           np.asarray(inputs["b_dw"], f32).reshape(Lh, 4, 128).transpose(2, 0, 1)[..., None],
                           np.asarray(inputs["conv_ln_g"], f32).reshape(Lh, 4, 128).transpose(2, 0, 1)[..., None],
                           np.asarray(inputs["conv_ln_b"], f32).reshape(Lh, 4, 128).transpose(2, 0, 1)[..., None]], axis=3)
    parts.append(np.ascontiguousarray(conv.reshape(128, Lh * 4 * (CW + 3))))
    parts += [cols(inputs["hgrn_lb"], 6), cols(inputs["hgrn_norm_g"], 6), cols(inputs["ret_gn_g"], 6), cols(inputs["ret_gn_b"], 6)]
    vec = np.ascontiguousarray(np.concatenate(parts, axis=1))
    p = np.arange(128)
    ident = np.eye(128, dtype=f32)
    ones = np.ones((128, 128), f32)
    tri = (p[None, :] >= p[:, None]).astype(f32)
    tms = np.maximum(p[None, :] - p[:, None], 0).astype(f32)
    iot1 = np.broadcast_to((p + 1).astype(f32)[None, :], (128, 128))
    half = 64
    invf = (np.float32(10000.0) ** (-(np.arange(half, dtype=f32)) / np.float32(half))).astype(f32)
    invf = np.broadcast_to(invf[None, :], (128, 64))
    colr = (127 - p).astype(f32)[:, None]
    per_core = []
    for c in range(8):
        b, hf = c // 2, c % 2
        sel = np.full((128, 1), float(hf), f32)
        const = np.ascontiguousarray(np.concatenate([ident, ones, tri, tms, iot1, invf, colr, sel], axis=1).astype(f32))
        xc = np.ascontiguousarray(x[b, hf * T:(hf + 1) * T, :])
        pc = np.ascontiguousarray(pos[b, hf * T:(hf + 1) * T].reshape(NT, 128).T)
        per_core.append(dict(x_in=xc, pos_in=pc, vec_in=vec, const_in=const))
    return per_core


_NC_CACHE = {}


def kernel(**inputs):
    per_core = _host_inputs(inputs)
    w = {k: np.ascontiguousarray(np.asarray(inputs[k], np.float32)) for k in ("w_in", "w_out", "w_ff1", "w_ff2")}
    if "nc" not in _NC_CACHE:
        _NC_CACHE["nc"] = build_nc(L_ALL)
    nc = _NC_CACHE["nc"]
    in_maps = []
    for c in range(8):
        m = dict(per_core[c])
        m.update(w)
        in_maps.append(m)
    res = run_bass_kernel_spmd(nc, in_maps, core_ids=list(range(8)))
    out = np.empty((4, 4096, D), np.float32)
    for c in range(8):
        out[c // 2, (c % 2) * T:(c % 2 + 1) * T, :] = res.results[c]["y_out"]
    return out
```
